# Optimizing a Trainium2 kernel written in Bass

```python
import jax, jax.numpy as jnp
from jax import lax
import numpy as np

D_MODEL = 2048
BATCH = 2
SEQ = 4096
DEPTH = 4

N_EVEN = (DEPTH + 1) // 2
N_ODD = DEPTH // 2

GLA_HEADS = 4
GLA_DK = 128
GLA_DV = 256
GLA_GATE_RANK = 16
GLA_TAU = 16.0
GLA_CHUNK = 64

DIL_PAIRS = ((128, 1), (512, 4), (2048, 16))
DIL_GROUPS = len(DIL_PAIRS)
DIL_HEADS = 4
DIL_HEAD_DIM = 128
DIL_BLOCK = 128

MLSTM_HEADS = 4
MLSTM_DQK = 128
MLSTM_DV = 256
MLSTM_CONV = 4
MLSTM_CHUNK = 64

MLA_HEADS = 8
MLA_Q_RANK = 512
MLA_KV_RANK = 512
MLA_NOPE = 128
MLA_ROPE = 64
MLA_DV = 128
ROPE_THETA = 10000.0
ATTN_BLOCK = 128

FFN_HIDDEN = -((-8 * D_MODEL) // (3 * 256)) * 256

DEEPNORM_ALPHA = (2.0 * DEPTH) ** 0.25
DEEPNORM_BETA = (8.0 * DEPTH) ** -0.25

GLA_QK_W = GLA_HEADS * GLA_DK
GLA_V_W = GLA_HEADS * GLA_DV
DIL_W = DIL_GROUPS * DIL_HEADS * DIL_HEAD_DIM
EVEN_SPLITS = (GLA_QK_W, GLA_QK_W, GLA_V_W, GLA_GATE_RANK, GLA_V_W, DIL_W, DIL_W, DIL_W)
EVEN_IN = sum(EVEN_SPLITS)
EVEN_OUT = GLA_V_W + DIL_HEADS * DIL_HEAD_DIM

MQK_W = MLSTM_HEADS * MLSTM_DQK
MV_W = MLSTM_HEADS * MLSTM_DV
ODD_SPLITS = (MQK_W, MQK_W, MV_W, MLSTM_HEADS, MLSTM_HEADS, MV_W, MLA_Q_RANK, MLA_KV_RANK, MLA_ROPE)
ODD_IN = sum(ODD_SPLITS)
ODD_OUT = MV_W + MLA_HEADS * MLA_DV

kernel_name = 'hybrid_gla_dilated_mlstm_mla_deepnorm'

F32 = jnp.float32


def _split(y, sizes):
    return jnp.split(y, np.cumsum(sizes)[:-1].tolist(), axis=-1)


def layer_norm(x, g, b, eps=1e-5):
    xf = x.astype(F32)
    mu = jnp.mean(xf, -1, keepdims=True)
    var = jnp.mean(jnp.square(xf - mu), -1, keepdims=True)
    return ((xf - mu) * lax.rsqrt(var + eps) * g.astype(F32) + b.astype(F32)).astype(x.dtype)


def head_layer_norm(x, g, eps=1e-5):
    xf = x.astype(F32)
    mu = jnp.mean(xf, -1, keepdims=True)
    var = jnp.mean(jnp.square(xf - mu), -1, keepdims=True)
    return (xf - mu) * lax.rsqrt(var + eps) * g.astype(F32)


def rms_norm(x, g, eps=1e-6):
    xf = x.astype(F32)
    return (xf * lax.rsqrt(jnp.mean(xf * xf, -1, keepdims=True) + eps) * g.astype(F32)).astype(x.dtype)


def swiglu(x, w_gu, w_d):
    gate, up = jnp.split(x @ w_gu, 2, axis=-1)
    return (jax.nn.silu(gate) * up) @ w_d


def gla(q, k, v, g1, w_g2, b_g):
    B_, S_, H, dk = q.shape
    dv = v.shape[-1]
    L = GLA_CHUNK
    N = S_ // L
    log_a = jax.nn.log_sigmoid((g1 @ w_g2 + b_g).astype(F32)) / GLA_TAU
    log_a = log_a.reshape(B_, N, L, H, dk)
    q = (q.astype(F32) * dk ** -0.5).reshape(B_, N, L, H, dk)
    k = k.astype(F32).reshape(B_, N, L, H, dk)
    v = v.astype(F32).reshape(B_, N, L, H, dv)
    b = jnp.cumsum(log_a, axis=2)
    b_last = b[:, :, -1]
    q_dec = q * jnp.exp(b)
    k_inv = k * jnp.exp(-b)
    k_end = k * jnp.exp(b_last[:, :, None] - b)
    tril = jnp.tril(jnp.ones((L, L), dtype=bool))
    scores = jnp.where(tril, jnp.einsum('bnihd,bnjhd->bnhij', q_dec, k_inv), 0.0)
    o_intra = jnp.einsum('bnhij,bnjhe->bnihe', scores, v)
    s_loc = jnp.einsum('bnjhd,bnjhe->bnhde', k_end, v)

    def step(state, inp):
        decay, s_c = inp
        return state * decay[..., None] + s_c, state

    init = jnp.zeros((B_, H, dk, dv), F32)
    _, s_prev = lax.scan(step, init, (jnp.moveaxis(jnp.exp(b_last), 1, 0), jnp.moveaxis(s_loc, 1, 0)))
    s_prev = jnp.moveaxis(s_prev, 0, 1)
    o_inter = jnp.einsum('bnihd,bnhde->bnihe', q_dec, s_prev)
    return (o_intra + o_inter).reshape(B_, S_, H, dv)


def dilated_branch(q, k, v, window, dilation):
    B_, S_, H, Dh = q.shape
    span = window // dilation
    blk = DIL_BLOCK
    unit = dilation * blk
    S_pad = -((-S_) // unit) * unit
    nb = S_pad // unit

    def regroup(t):
        t = jnp.pad(t, ((0, 0), (0, S_pad - S_), (0, 0), (0, 0)))
        t = jnp.moveaxis(t.reshape(B_, S_pad // dilation, dilation, H, Dh), 2, 1)
        return t.reshape(B_, dilation, nb, blk, H, Dh)

    def with_prev(t):
        prev = jnp.pad(t, ((0, 0), (0, 0), (1, 0), (0, 0), (0, 0), (0, 0)))[:, :, :-1]
        return jnp.concatenate([prev, t], axis=3)

    qg = regroup(q)
    kb = with_prev(regroup(k))
    vb = with_prev(regroup(v))
    s = jnp.einsum('bdnihe,bdnjhe->bdnhij', qg, kb).astype(F32) * Dh ** -0.5
    qi = jnp.arange(blk)[:, None] + blk
    kj = jnp.arange(2 * blk)[None, :]
    rel = qi - kj
    band = (rel >= 0) & (rel <= span)
    no_prev = (jnp.arange(nb) == 0)[:, None, None] & (kj < blk)[None]
    valid = band[None] & ~no_prev
    s = jnp.where(valid[:, None], s, -jnp.inf)
    m = jnp.max(s, -1, keepdims=True)
    p = jnp.exp(s - m)
    den = jnp.sum(p, -1, keepdims=True)
    o = jnp.einsum('bdnhij,bdnjhe->bdnihe', p / den, vb.astype(F32))
    lse = jnp.swapaxes((m + jnp.log(den))[..., 0], 3, 4)

    def ungroup(t):
        t = t.reshape((B_, dilation, S_pad // dilation) + t.shape[4:])
        t = jnp.moveaxis(t, 1, 2)
        return t.reshape((B_, S_pad) + t.shape[3:])[:, :S_]

    return ungroup(o), ungroup(lse)


def causal_conv(u, w, b):
    C = u.shape[-1]
    K = w.shape[0]
    out = lax.conv_general_dilated(u, w[:, None, :].astype(u.dtype), window_strides=(1,),
                                   padding=[(K - 1, 0)], dimension_numbers=('NWC', 'WIO', 'NWC'),
                                   feature_group_count=C)
    return out + b


def mlstm(q, k, v, i_pre, f_pre):
    B_, S_, H, dk = q.shape
    dv = v.shape[-1]
    L = MLSTM_CHUNK
    N = S_ // L
    q = q.astype(F32).reshape(B_, N, L, H, dk)
    k = (k.astype(F32) * dk ** -0.5).reshape(B_, N, L, H, dk)
    v = v.astype(F32).reshape(B_, N, L, H, dv)
    log_f = jax.nn.log_sigmoid(f_pre.astype(F32)).reshape(B_, N, L, H)
    log_i = i_pre.astype(F32).reshape(B_, N, L, H)
    b = jnp.cumsum(log_f, axis=2)
    b_last = b[:, :, -1]
    a = b_last[:, :, None] - b + log_i
    m_loc = jnp.max(a, axis=2)
    w = jnp.exp(a - m_loc[:, :, None])
    c_loc = jnp.einsum('bnjh,bnjhd,bnjhe->bnhde', w, k, v)
    n_loc = jnp.einsum('bnjh,bnjhd->bnhd', w, k)

    def step(carry, inp):
        c, n, m = carry
        g, cl, nl, ml = inp
        m_new = jnp.maximum(g + m, ml)
        s_old = jnp.exp(g + m - m_new)
        s_new = jnp.exp(ml - m_new)
        c_new = c * s_old[..., None, None] + cl * s_new[..., None, None]
        n_new = n * s_old[..., None] + nl * s_new[..., None]
        return (c_new, n_new, m_new), (c, n, m)

    init = (jnp.zeros((B_, H, dk, dv), F32), jnp.zeros((B_, H, dk), F32), jnp.zeros((B_, H), F32))
    xs = (jnp.moveaxis(b_last, 1, 0), jnp.moveaxis(c_loc, 1, 0), jnp.moveaxis(n_loc, 1, 0), jnp.moveaxis(m_loc, 1, 0))
    _, (c_prev, n_prev, m_prev) = lax.scan(step, init, xs)
    c_prev = jnp.moveaxis(c_prev, 0, 1)
    n_prev = jnp.moveaxis(n_prev, 0, 1)
    m_prev = jnp.moveaxis(m_prev, 0, 1)

    inter_log = b + m_prev[:, :, None, :]
    d_log = b[:, :, :, None, :] - b[:, :, None, :, :] + log_i[:, :, None, :, :]
    tril = jnp.tril(jnp.ones((L, L), dtype=bool))
    d_log = jnp.where(tril[:, :, None], d_log, -jnp.inf)
    m_t = jnp.maximum(inter_log, jnp.max(d_log, axis=3))
    dmat = jnp.exp(d_log - m_t[:, :, :, None, :])
    qk = jnp.einsum('bnihd,bnjhd->bnijh', q, k) * dmat
    inter_scale = jnp.exp(inter_log - m_t)
    num = (jnp.einsum('bnijh,bnjhe->bnihe', qk, v)
           + inter_scale[..., None] * jnp.einsum('bnihd,bnhde->bnihe', q, c_prev))
    nq = jnp.sum(qk, axis=3) + inter_scale * jnp.einsum('bnihd,bnhd->bnih', q, n_prev)
    den = jnp.maximum(jnp.abs(nq), jnp.exp(-m_t))
    return (num / den[..., None]).reshape(B_, S_, H, dv)


def rope(x, cos, sin):
    half = x.shape[-1] // 2
    x1, x2 = x[..., :half], x[..., half:]
    return jnp.concatenate([x1 * cos - x2 * sin, x1 * sin + x2 * cos], axis=-1)


def causal_latent_attention(q_nope, q_rope, k_nope, k_rope, v):
    B_, S_, H, _ = q_nope.shape
    nb = S_ // ATTN_BLOCK
    scale = (MLA_NOPE + MLA_ROPE) ** -0.5
    k_pos = jnp.arange(S_)

    def blocks(t):
        return jnp.moveaxis(t.reshape((B_, nb, ATTN_BLOCK) + t.shape[2:]), 1, 0)

    def one_block(args):
        qn, qr, i = args
        s = (jnp.einsum('bihd,bjhd->bhij', qn, k_nope)
             + jnp.einsum('bihr,bjr->bhij', qr, k_rope)).astype(F32) * scale
        q_pos = i * ATTN_BLOCK + jnp.arange(ATTN_BLOCK)
        s = jnp.where(k_pos[None, :] <= q_pos[:, None], s, -jnp.inf)
        p = jax.nn.softmax(s, axis=-1)
        return jnp.einsum('bhij,bjhe->bihe', p.astype(v.dtype), v)

    o = lax.map(one_block, (blocks(q_nope), blocks(q_rope), jnp.arange(nb)))
    return jnp.moveaxis(o, 0, 1).reshape(B_, S_, H, v.shape[-1])


def mixer_even(x, w_in, gla_wg2, gla_bg, gla_norm_g, w_o):
    B_, S_, _ = x.shape
    gq, gk, gv, gg, gr, dq, dk, dv = _split(x @ w_in, EVEN_SPLITS)
    o_a = gla(gq.reshape(B_, S_, GLA_HEADS, GLA_DK), gk.reshape(B_, S_, GLA_HEADS, GLA_DK),
              gv.reshape(B_, S_, GLA_HEADS, GLA_DV), gg, gla_wg2, gla_bg)
    o_a = rms_norm(o_a, gla_norm_g) * jax.nn.silu(gr.reshape(B_, S_, GLA_HEADS, GLA_DV).astype(F32))
    grp = (B_, S_, DIL_GROUPS, DIL_HEADS, DIL_HEAD_DIM)
    dq, dk, dv = dq.reshape(grp), dk.reshape(grp), dv.reshape(grp)
    outs = []
    lses = []
    for g, (window, dilation) in enumerate(DIL_PAIRS):
        o_g, lse_g = dilated_branch(dq[:, :, g], dk[:, :, g], dv[:, :, g], window, dilation)
        outs.append(o_g)
        lses.append(lse_g)
    wts = jax.nn.softmax(jnp.stack(lses, 0), axis=0)
    o_b = jnp.sum(wts[..., None] * jnp.stack(outs, 0), axis=0)
    o = jnp.concatenate([o_a.reshape(B_, S_, -1), o_b.reshape(B_, S_, -1)], axis=-1)
    return o.astype(x.dtype) @ w_o


def mixer_odd(x, cos, sin, w_in, conv_w, conv_b, mlstm_bi, mlstm_bf, mlstm_norm_g,
              mla_qnorm_g, mla_kvnorm_g, mla_wuq, mla_wukv, w_o):
    B_, S_, _ = x.shape
    cq, ck, cv, ci, cf, co, mq, mkv, mkr = _split(x @ w_in, ODD_SPLITS)
    qk = jax.nn.silu(causal_conv(jnp.concatenate([cq, ck], -1), conv_w, conv_b))
    cq, ck = jnp.split(qk, 2, axis=-1)
    h = mlstm(cq.reshape(B_, S_, MLSTM_HEADS, MLSTM_DQK), ck.reshape(B_, S_, MLSTM_HEADS, MLSTM_DQK),
              cv.reshape(B_, S_, MLSTM_HEADS, MLSTM_DV), ci + mlstm_bi, cf + mlstm_bf)
    o_c = jax.nn.sigmoid(co.reshape(B_, S_, MLSTM_HEADS, MLSTM_DV).astype(F32)) * head_layer_norm(h, mlstm_norm_g)
    q = (rms_norm(mq, mla_qnorm_g) @ mla_wuq).reshape(B_, S_, MLA_HEADS, MLA_NOPE + MLA_ROPE)
    q_nope, q_rope = q[..., :MLA_NOPE], q[..., MLA_NOPE:]
    q_rope = rope(q_rope, cos[:, :, None, :], sin[:, :, None, :]).astype(q.dtype)
    kv = (rms_norm(mkv, mla_kvnorm_g) @ mla_wukv).reshape(B_, S_, MLA_HEADS, MLA_NOPE + MLA_DV)
    k_nope, v = kv[..., :MLA_NOPE], kv[..., MLA_NOPE:]
    k_rope = rope(mkr, cos, sin).astype(mkr.dtype)
    o_d = causal_latent_attention(q_nope, q_rope, k_nope, k_rope, v)
    o = jnp.concatenate([o_c.reshape(B_, S_, -1).astype(x.dtype), o_d.reshape(B_, S_, -1).astype(x.dtype)], axis=-1)
    return o @ w_o


def setup_inputs(seed: int = 0) -> dict:
    key = jax.random.key(seed)
    ks = iter(jax.random.split(key, 32))

    def nrm(shape, fan_in, scale=1.0):
        return jax.random.normal(next(ks), shape, F32) * (scale * fan_in ** -0.5)

    def gain(shape):
        return 1.0 + 0.02 * jax.random.normal(next(ks), shape, F32)

    def small(shape, s=0.02):
        return s * jax.random.normal(next(ks), shape, F32)

    ne, no = N_EVEN, N_ODD
    x = jax.random.normal(next(ks), (BATCH, SEQ, D_MODEL), F32)
    positions = (jax.random.randint(next(ks), (BATCH, 1), 0, 1024, dtype=jnp.int32)
                 + jnp.arange(SEQ, dtype=jnp.int32)[None, :])
    return {
        'x': x,
        'positions': positions,
        'even_w_in': nrm((ne, D_MODEL, EVEN_IN), D_MODEL),
        'even_gla_wg2': nrm((ne, GLA_GATE_RANK, GLA_QK_W), GLA_GATE_RANK),
        'even_gla_bg': small((ne, GLA_QK_W), 0.1),
        'even_gla_norm_g': gain((ne, GLA_DV)),
        'even_w_o': nrm((ne, EVEN_OUT, D_MODEL), EVEN_OUT, DEEPNORM_BETA),
        'odd_w_in': nrm((no, D_MODEL, ODD_IN), D_MODEL),
        'odd_conv_w': nrm((no, MLSTM_CONV, 2 * MQK_W), MLSTM_CONV),
        'odd_conv_b': small((no, 2 * MQK_W)),
        'odd_mlstm_bi': small((no, MLSTM_HEADS), 0.1),
        'odd_mlstm_bf': jnp.linspace(3.0, 6.0, MLSTM_HEADS, dtype=F32)[None, :] + small((no, MLSTM_HEADS), 0.01),
        'odd_mlstm_norm_g': gain((no, MLSTM_DV)),
        'odd_mla_qnorm_g': gain((no, MLA_Q_RANK)),
        'odd_mla_kvnorm_g': gain((no, MLA_KV_RANK)),
        'odd_mla_wuq': nrm((no, MLA_Q_RANK, MLA_HEADS * (MLA_NOPE + MLA_ROPE)), MLA_Q_RANK),
        'odd_mla_wukv': nrm((no, MLA_KV_RANK, MLA_HEADS * (MLA_NOPE + MLA_DV)), MLA_KV_RANK),
        'odd_w_o': nrm((no, ODD_OUT, D_MODEL), ODD_OUT, DEEPNORM_BETA),
        'ln1_g': gain((DEPTH, D_MODEL)),
        'ln1_b': small((DEPTH, D_MODEL)),
        'ffn_wgu': nrm((DEPTH, D_MODEL, 2 * FFN_HIDDEN), D_MODEL),
        'ffn_wd': nrm((DEPTH, FFN_HIDDEN, D_MODEL), FFN_HIDDEN, DEEPNORM_BETA),
        'ln2_g': gain((DEPTH, D_MODEL)),
        'ln2_b': small((DEPTH, D_MODEL)),
    }


def reference(x, positions, even_w_in, even_gla_wg2, even_gla_bg, even_gla_norm_g, even_w_o,
              odd_w_in, odd_conv_w, odd_conv_b, odd_mlstm_bi, odd_mlstm_bf, odd_mlstm_norm_g,
              odd_mla_qnorm_g, odd_mla_kvnorm_g, odd_mla_wuq, odd_mla_wukv, odd_w_o,
              ln1_g, ln1_b, ffn_wgu, ffn_wd, ln2_g, ln2_b):
    inv_freq = ROPE_THETA ** (-jnp.arange(0, MLA_ROPE, 2, dtype=F32) / MLA_ROPE)
    angles = positions.astype(F32)[..., None] * inv_freq
    cos, sin = jnp.cos(angles), jnp.sin(angles)
    for l in range(DEPTH):
        j = l // 2
        if l % 2 == 0:
            h = mixer_even(x, even_w_in[j], even_gla_wg2[j], even_gla_bg[j], even_gla_norm_g[j], even_w_o[j])
        else:
            h = mixer_odd(x, cos, sin, odd_w_in[j], odd_conv_w[j], odd_conv_b[j], odd_mlstm_bi[j],
                          odd_mlstm_bf[j], odd_mlstm_norm_g[j], odd_mla_qnorm_g[j], odd_mla_kvnorm_g[j],
                          odd_mla_wuq[j], odd_mla_wukv[j], odd_w_o[j])
        x = layer_norm(DEEPNORM_ALPHA * x + h, ln1_g[l], ln1_b[l])
        x = layer_norm(DEEPNORM_ALPHA * x + swiglu(x, ffn_wgu[l], ffn_wd[l]), ln2_g[l], ln2_b[l])
    return x
```

```python
import math
import ml_dtypes
from concourse.bass_utils import run_bass_kernel_spmd


import numpy as np
import concourse.bass as bass
import concourse.mybir as mybir
from contextlib import ExitStack

F32 = mybir.dt.float32
BF16 = mybir.dt.bfloat16
I32 = mybir.dt.int32
AF = mybir.ActivationFunctionType
ALU = mybir.AluOpType
AX = mybir.AxisListType

ENGS = ("pe", "act", "dve", "pool", "sp")
NDS = 12


class FW:
    def __init__(self, nc):
        self.nc = nc
        self.es = ExitStack()
        self.scope = self.es
        self.ncc = 0
        self.nphase = 0
        self.q = {e: [] for e in ENGS}
        self.cnt = {e: 0 for e in ENGS}
        self.dcnt = {e: 0 for e in ENGS}
        self.known = {e: {} for e in ENGS}
        self.lastw = {}
        self.readers = {}
        self.lastx = {}
        self.sems = {}
        self.out_deps = []
        self.n_wait = 0

    def sb(self, name, shape, dt):
        return self.scope.enter_context(self.nc.sbuf_tensor(name, list(shape), dt))

    def ps(self, name, shape, dt=F32):
        return self.scope.enter_context(self.nc.psum_tensor(name, list(shape), dt))

    def _sem(self, key):
        if key not in self.sems:
            self.sems[key] = self.es.enter_context(self.nc.semaphore("s_" + "_".join(map(str, key))))
        return self.sems[key]

    def _deps(self, eng, reads, writes, excl=()):
        deps = {}

        def add(d):
            if d is None:
                return
            sk, v, e = d
            if deps.get(sk, (0,))[0] < v:
                deps[sk] = (v, e)

        for k in reads:
            add(self.lastw.get(k))
        for k in writes:
            add(self.lastw.get(k))
            for d in self.readers.get(k, ()):
                add(d)
        for k in excl:
            d = self.lastx.get(k)
            if d is not None and d[2] != eng:
                add(d)
        out = []
        for sk, (v, e) in deps.items():
            if e == eng and eng == "pe":
                continue
            if self.known[eng].get(sk, 0) >= v:
                continue
            self.known[eng][sk] = v
            out.append((sk, v))
        return out

    def _commit(self, me, reads, writes):
        for k in writes:
            self.lastw[k] = me
            self.readers[k] = []
        for k in reads:
            self.readers.setdefault(k, []).append(me)

    def op(self, eng, fn, reads=(), writes=(), excl=()):
        waits = self._deps(eng, reads, writes, excl)
        self.cnt[eng] += 1
        me = (("e", eng), self.cnt[eng], eng)
        for k in excl:
            self.lastx[k] = me
        self.q[eng].append((waits, fn, ("e", eng), 1))
        self.n_wait += len(waits)
        self._commit(me, reads, writes)
        return me

    def dma(self, queue, out, in_, reads=(), writes=(), is_output=False):
        i = self.dcnt[queue]
        self.dcnt[queue] += 1
        sk = ("d", queue, i % NDS)
        val = 16 * (i // NDS + 1)
        waits = self._deps(queue, reads, writes)
        if i // NDS > 0 and self.known[queue].get(sk, 0) < val - 16:
            waits.append((sk, val - 16))
            self.known[queue][sk] = val - 16
        fn = lambda e, out=out, in_=in_: e.dma_start(out=out, in_=in_)
        self.q[queue].append((waits, fn, sk, 16))
        me = (sk, val, "dma")
        self._commit(me, reads, writes)
        if is_output:
            self.out_deps.append(me)
        return me

    def collective(self, kind, groups, src, dst, reads=(), writes=()):
        waits = self._deps("pool", reads, ())
        self.ncc += 1
        for k in writes:
            self.lastw[k] = (("cc",), self.ncc, "cc")
            self.readers[k] = []
        fn = lambda e: e.collective_compute(kind, mybir.AluOpType.bypass, replica_groups=groups,
                                            ins=[src.ap().opt()], outs=[dst.ap().opt()])
        self.q["pool"].append((waits, fn, ("cc",), 1))

    def begin_phase(self):
        self.scope = ExitStack()

    def end_phase(self, collective=None, wait_cc=True):
        nc = self.nc
        fin = []
        for q in ("sp", "pool"):
            for k in range(min(NDS, self.dcnt[q])):
                n_k = (self.dcnt[q] - k + NDS - 1) // NDS
                fin.append((("d", q, k), 16 * n_k))
        eng_obj = {"pe": "tensor", "act": "scalar", "dve": "vector", "pool": "gpsimd", "sp": "sync"}
        for e in ENGS:
            self._sem(("e", e))
        for (waits, fn, sk, inc) in [x for e in ENGS for x in self.q[e]]:
            self._sem(sk)
            for w in waits:
                self._sem(w[0])
        for f in fin:
            self._sem(f[0])
        colls = [] if collective is None else (collective if isinstance(collective, list) else [collective])
        ccsem = self._sem(("cc",))
        self.ncc += len(colls)
        cc_total = self.ncc
        self.nphase += 1
        with nc.Block("ph%d" % self.nphase) as block:
            for e in ENGS:
                items = self.q[e]

                def body(engine, items=items, e=e):
                    for (waits, fn, sk, inc) in items:
                        for (wsk, wv) in waits:
                            engine.wait_ge(self.sems[wsk], wv)
                        ins = fn(engine)
                        ins.then_inc(self.sems[sk], inc)
                    if e in ("pool", "sp"):
                        for (wsk, wv) in fin:
                            engine.wait_ge(self.sems[wsk], wv)
                    if e == "pool":
                        for (kind, groups, src, dst) in colls:
                            engine.collective_compute(kind, mybir.AluOpType.bypass, replica_groups=groups,
                                                      ins=[src.ap().opt()], outs=[dst.ap().opt()]).then_inc(ccsem, 1)
                        if cc_total > 0 and wait_cc:
                            engine.wait_ge(ccsem, cc_total)

                getattr(block, eng_obj[e])(body)
        self.q = {e: [] for e in ENGS}
        self.lastw = {} if wait_cc else {k: v for k, v in self.lastw.items() if v[2] == "cc"}
        self.readers = {}
        self.lastx = {}
        self.out_deps = []
        if self.scope is not self.es:
            self.scope.close()
            self.scope = self.es

    def emit(self):
        self.end_phase()

    def close(self):
        self.es.close()


D = 2048
HID = 5632
NJ = HID // 128
ALPHA = (2.0 * 4) ** 0.25
TOK = 1024
NT = TOK // 128


def build_T(F, xT_out_dt=F32):
    KC = F // 128
    nc = bass.Bass("TRN2", target_bir_lowering=False)
    oT_d = nc.dram_tensor("oT", [F, TOK], BF16, kind="ExternalInput").ap()
    xres_d = nc.dram_tensor("xres", [TOK, D], F32, kind="ExternalInput").ap()
    wo_d = nc.dram_tensor("w_o", [F, D], F32, kind="ExternalInput").ap()
    wgu_d = nc.dram_tensor("wgu", [NJ, 128, 16 * 256], F32, kind="ExternalInput").ap()
    wd_d = nc.dram_tensor("wd", [HID, D], F32, kind="ExternalInput").ap()
    ln_d = nc.dram_tensor("ln", [4, D], F32, kind="ExternalInput").ap()
    id_d = nc.dram_tensor("ident", [128, 128], F32, kind="ExternalInput").ap()
    xo_d = nc.dram_tensor("xo", [TOK, D], F32, kind="ExternalOutput").ap()
    xoT_d = nc.dram_tensor("xoT", [D, TOK], xT_out_dt, kind="ExternalOutput").ap()

    fw = FW(nc)
    emit_T(fw, F, lambda fw_, oT, KC_, tmp: fw_.dma("sp", oT[:, 0:KC_, :], oT_d.rearrange("(c p) t -> p c t", p=128), writes=[("oT",)]),
           xres_d, wo_d, wgu_d, wd_d, ln_d, id_d, xo_d,
           None, xT_out_dt)
    fw.emit()
    fw.close()
    return nc


def emit_T(fw, F, oT_loader, xres_d, wo_d, wgu_d, wd_d, ln_d, id_d, xo_d, xoT_d, xT_out_dt, pfx="T", post=None, U_ext=None, wo_preloaded=False, prefetch=None):
    nc = fw.nc
    KC = F // 128
    z = fw.sb(pfx + "z", [128, NT, D], F32)
    x1T = fw.sb(pfx + "x1T", [128, 16, TOK], BF16)
    gb = [fw.sb(pfx + "g", [128, D], F32), fw.sb(pfx + "b", [128, D], F32)]
    ident = fw.sb(pfx + "id", [128, 128], F32)
    stats = fw.sb(pfx + "st", [128, NT, 4 * 6], F32)
    mv = fw.sb(pfx + "mv", [128, NT, 2], F32)
    rstd = fw.sb(pfx + "rstd", [128, NT], F32)
    nmr = fw.sb(pfx + "nmr", [128, NT], F32)
    psb = [fw.ps(pfx + "ps%d" % i, [128, 512], F32) for i in range(8)]
    U = U_ext if U_ext is not None else fw.sb(pfx + "U", [128, 32768], BF16)
    oT = U[:, 0:16384].rearrange("p (c t) -> p c t", c=16)
    wob = [U[:, 16384 + i * 8192:16384 + (i + 1) * 8192].rearrange("p (c n) -> p c n", c=16) for i in range(2)]
    WG = 3
    wgb = [U[:, i * 4096:(i + 1) * 4096].rearrange("p (c n) -> p c n", c=16) for i in range(WG)]
    WD = 8
    wdb = [U[:, 12288 + i * 2048:12288 + (i + 1) * 2048] for i in range(WD)]
    hT = [fw.sb(pfx + "hT%d" % i, [128, 4, TOK], BF16) for i in range(2)]
    P1KEYS = [("oT",), ("wo", 0), ("wo", 1)]
    sg = [fw.sb(pfx + "sg%d" % i, [128, 512], F32) for i in range(2)]
    xt_st = [fw.sb(pfx + "xts%d" % i, [128, 4, 128], xT_out_dt) for i in range(2)]

    zk = lambda t: [("z", t, cb) for cb in range(4)]

    fw.dma("sp", ident[:], id_d, writes=[("id",)])
    for t in range(NT):
        fw.dma("sp", z[:, t, :], xres_d[t * 128:(t + 1) * 128, :], writes=zk(t))
    oT_loader(fw, oT, KC, [x1T[:, i, :] for i in range(16)])

    def load_gb(i):
        for k in range(2):
            fw.dma("sp", gb[k][:], ln_d[2 * i + k, :].partition_broadcast(128), writes=[("gb", k)])

    load_gb(0)

    def load_wo(cb):
        buf = wob[cb % 2]
        fw.dma("pool", buf[:, 0:KC, :], wo_d[:, cb * 512:(cb + 1) * 512].rearrange("(c p) n -> p c n", p=128),
               writes=[("wo", cb % 2)])

    def load_wg(j):
        fw.dma("pool", wgb[j % WG].rearrange("p c n -> p (c n)"), wgu_d[j],
               writes=[("wg", j % WG)] + (P1KEYS if j < WG else []))

    def load_wd(j):
        fw.dma("pool", wdb[j % WD], wd_d[j * 128:(j + 1) * 128, :],
               writes=[("wd", j % WD)] + (P1KEYS if j < WD else []))

    if not wo_preloaded:
        load_wo(0)
        load_wo(1)

    pi = [0]

    def nextps():
        p = pi[0] % 8
        pi[0] += 1
        return p

    for cb in range(4):
        buf = wob[cb % 2]
        for t in range(NT):
            p = nextps()
            for k in range(KC):
                fw.op("pe", lambda e, p=p, k=k, t=t, buf=buf: e.matmul(
                    psb[p][:], oT[:, k, t * 128:(t + 1) * 128], buf[:, k, :], start=(k == 0), stop=(k == KC - 1)),
                    reads=[("oT",), ("oT", k), ("wo", cb % 2)], writes=[("ps", p)])
            fw.op("dve", lambda e, p=p, t=t, cb=cb: e.scalar_tensor_tensor(
                out=z[:, t, cb * 512:(cb + 1) * 512], in0=z[:, t, cb * 512:(cb + 1) * 512], scalar=ALPHA,
                in1=psb[p][:], op0=ALU.mult, op1=ALU.add),
                reads=[("ps", p), ("z", t, cb)], writes=[("z", t, cb)])
        if cb + 2 < 4:
            load_wo(cb + 2)

    def ln_stats_all():
        for t in range(NT):
            for s_ in range(4):
                fw.op("dve", lambda e, t=t, s_=s_: e.bn_stats(out=stats[:, t, s_ * 6:(s_ + 1) * 6], in_=z[:, t, s_ * 512:(s_ + 1) * 512]),
                      reads=[("z", t, s_)], writes=[("st", t, s_)])
            fw.op("dve", lambda e, t=t: e.bn_aggr(out=mv[:, t, :], in_=stats[:, t, :]),
                  reads=[("st", t, s_) for s_ in range(4)], writes=[("mv", t)])
        MVK = [("mv", t) for t in range(NT)]
        fw.op("act", lambda e: e.activation(out=rstd[:], in_=mv[:, :, 1], func=AF.Sqrt, bias=1e-5, scale=1.0),
              reads=MVK, writes=[("rstd",)])
        fw.op("dve", lambda e: e.reciprocal(out=rstd[:], in_=rstd[:]), reads=[("rstd",)], writes=[("rstd",)])
        fw.op("dve", lambda e: e.scalar_tensor_tensor(out=nmr[:], in0=mv[:, :, 0], scalar=-1.0, in1=rstd[:],
                                                       op0=ALU.mult, op1=ALU.mult),
              reads=MVK + [("rstd",)], writes=[("nmr",)])

    def layer_norm(t):
        fw.op("act", lambda e, t=t: e.activation(out=z[:, t, :], in_=z[:, t, :], func=AF.Identity, bias=nmr[:, t:t + 1], scale=rstd[:, t:t + 1]),
              reads=zk(t) + [("rstd",), ("nmr",)], writes=zk(t))
        fw.op("dve", lambda e, t=t: e.tensor_tensor(out=z[:, t, :], in0=z[:, t, :], in1=gb[0][:], op=ALU.mult),
              reads=zk(t) + [("gb", 0)], writes=zk(t))
        fw.op("dve", lambda e, t=t: e.tensor_tensor(out=z[:, t, :], in0=z[:, t, :], in1=gb[1][:], op=ALU.add),
              reads=zk(t) + [("gb", 1)], writes=zk(t))

    def transpose_tile(t, dst_fn):
        for cg in range(4):
            p = nextps()
            for cc in range(4):
                c = cg * 4 + cc
                fw.op("pe", lambda e, p=p, cc=cc, c=c, t=t: e.transpose(
                    out=psb[p][:, cc * 128:(cc + 1) * 128], in_=z[:, t, c * 128:(c + 1) * 128], identity=ident[:]),
                    reads=[("z", t, c // 4), ("id",)], writes=[("ps", p)])
            dst_fn(cg, p)

    ln_stats_all()
    for t in range(NT):
        layer_norm(t)

        def dst(cg, p, t=t):
            fw.op("act", lambda e: e.copy(out=x1T[:, cg * 4:(cg + 1) * 4, t * 128:(t + 1) * 128],
                                          in_=psb[p][:].rearrange("p (c n) -> p c n", c=4)),
                  reads=[("ps", p)], writes=[("x1T", t, cg)])
        transpose_tile(t, dst)
    load_gb(1)
    for j in range(WG):
        load_wg(j)
    for j in range(WD):
        load_wd(j)

    def up_gate(j):
        buf = wgb[j % WG]
        hb = hT[(j // 4) % 2]
        for half in range(2):
            pg = nextps()
            pu = nextps()
            for which, p in ((0, pg), (1, pu)):
                for c in range(16):
                    fw.op("pe", lambda e, p=p, c=c, which=which, half=half, buf=buf: e.matmul(
                        psb[p][:], buf[:, c, which * 128:(which + 1) * 128], x1T[:, c, half * 512:(half + 1) * 512],
                        start=(c == 0), stop=(c == 15)),
                        reads=[("wg", j % WG)] + [("x1T", t, c // 4) for t in range(half * 4, half * 4 + 4)],
                        writes=[("ps", p)])
            s = sg[half]
            fw.op("act", lambda e, s=s, pg=pg: e.activation(out=s[:], in_=psb[pg][:], func=AF.Silu),
                  reads=[("ps", pg)], writes=[("sg", half)])
            fw.op("dve", lambda e, s=s, pu=pu, hb=hb, half=half, j=j: e.tensor_tensor(
                out=hb[:, j % 4, half * 512:(half + 1) * 512], in0=s[:], in1=psb[pu][:], op=ALU.mult),
                reads=[("sg", half), ("ps", pu)], writes=[("hT", (j // 4) % 2, j % 4, half)])
        if j + WG < NJ:
            load_wg(j + WG)

    def down(G):
        hb = hT[G % 2]
        for t in range(NT):
            for cb in range(4):
                p = nextps()
                for jj in range(4):
                    j = G * 4 + jj
                    fw.op("pe", lambda e, p=p, jj=jj, j=j, t=t, cb=cb, hb=hb: e.matmul(
                        psb[p][:], hb[:, jj, t * 128:(t + 1) * 128], wdb[j % WD][:, cb * 512:(cb + 1) * 512],
                        start=(jj == 0), stop=(jj == 3)),
                        reads=[("hT", G % 2, jj, t // 4), ("wd", j % WD)], writes=[("ps", p)])
                if G == 0:
                    fw.op("dve", lambda e, p=p, t=t, cb=cb: e.scalar_tensor_tensor(
                        out=z[:, t, cb * 512:(cb + 1) * 512], in0=z[:, t, cb * 512:(cb + 1) * 512], scalar=ALPHA,
                        in1=psb[p][:], op0=ALU.mult, op1=ALU.add),
                        reads=[("ps", p), ("z", t, cb)], writes=[("z", t, cb)])
                else:
                    fw.op("dve", lambda e, p=p, t=t, cb=cb: e.tensor_tensor(
                        out=z[:, t, cb * 512:(cb + 1) * 512], in0=z[:, t, cb * 512:(cb + 1) * 512],
                        in1=psb[p][:], op=ALU.add),
                        reads=[("ps", p), ("z", t, cb)], writes=[("z", t, cb)])
        for jj in range(4):
            j = G * 4 + jj
            if j + WD < NJ:
                load_wd(j + WD)

    NG = NJ // 4
    for G in range(NG):
        for jj in range(4):
            up_gate(G * 4 + jj)
        if G >= 1:
            down(G - 1)
    down(NG - 1)
    if prefetch is not None:
        prefetch([("wg", i) for i in range(WG)] + [("wd", i) for i in range(WD)] + P1KEYS)

    ln_stats_all()
    for t in range(NT):
        layer_norm(t)
        fw.dma("sp", xo_d[t * 128:(t + 1) * 128, :], z[:, t, :], reads=zk(t), is_output=True)

        if xoT_d is None:
            continue

        def dst(cg, p, t=t):
            cbuf = x1T[:].rearrange("p c t -> p (c t)")[:, (t // 2) * 4096:(t // 2 + 1) * 4096].rearrange("p (c t) -> p c t", c=16)
            wk = [("xc", t // 2, cg, t % 2)]
            if t == 0 and cg == 0:
                wk = wk + [("x1T", tt, cgg) for tt in range(NT) for cgg in range(4)]
            fw.op("act", lambda e: e.copy(out=cbuf[:, cg * 4:(cg + 1) * 4, (t % 2) * 128:(t % 2) * 128 + 128],
                                          in_=psb[p][:].rearrange("p (c n) -> p c n", c=4)),
                  reads=[("ps", p)], writes=wk)
        transpose_tile(t, dst)
        if t % 2 == 1:
            c = t // 2
            fw.dma("sp", xoT_d(c), x1T[:].rearrange("p c t -> p (c t)")[:, c * 4096:(c + 1) * 4096],
                   reads=[("xc", c, cg, h) for cg in range(4) for h in range(2)], writes=[("xodram", c)], is_output=True)
            if post is not None:
                post(c)


D = 2048
S = 4096
NB = S // 512
NEG = -30000.0
ME_FM = [("q", 0, 128), ("k", 128, 128), ("g1", 256, 16), ("dq0", 272, 128), ("dq1", 400, 128), ("dq2", 528, 128),
         ("dk0", 656, 128), ("dk1", 784, 128), ("dk2", 912, 128)]
ME_TM0 = 1040
ME_TM1 = 1552
ME_NC = 2064
DIL_NT = (2, 5, 17)
DIL_OFF = (0, 2, 7)
DIL_TOT = 24
ME_NFG = [44, 59, 67, 75, 80, 80, 80, 80]


def emit_ME(fw, x_src, win_d, wg2_d, ng_d, cst_d, mask_d, idb_d, o_dst, pfx="E", post=None, win_ext=None, win_preloaded=False, tail_prefetch=None):
    nc = fw.nc
    win = win_ext if win_ext is not None else fw.sb(pfx + "win", [128, 16, ME_NC], BF16)
    xT = fw.sb(pfx + "xT", [128, 16, 512], BF16)
    dkT = [fw.sb(pfx + "dkT%d" % g, [128, S], BF16) for g in range(3)]
    dvh = [fw.sb(pfx + "dv%d" % g, [128, 32, 128], BF16) for g in range(3)]
    dqT2 = [[fw.sb(pfx + "dqT%d_%d" % (g, i), [128, 512], BF16) for g in range(3)] for i in range(2)]
    qT2 = [fw.sb(pfx + "qT%d" % i, [128, 512], F32) for i in range(2)]
    kT2 = [fw.sb(pfx + "kT%d" % i, [128, 512], F32) for i in range(2)]
    ktm2 = [fw.sb(pfx + "ktm%d" % i, [128, 4, 128], F32) for i in range(2)]
    vtm2 = [fw.sb(pfx + "vtm%d" % i, [128, 4, 256], BF16) for i in range(2)]
    gs2 = [fw.sb(pfx + "gs%d" % i, [128, 4, 256], F32) for i in range(2)]
    g1a2 = [fw.sb(pfx + "g1a%d" % i, [17, 512], F32) for i in range(2)]
    wg2 = fw.sb(pfx + "wg2", [17, 128], F32)
    ngb = fw.sb(pfx + "ngb", [128, 256], F32)
    cst = fw.sb(pfx + "cst", [128, 4, 128], F32)
    idb = fw.sb(pfx + "idb", [128, 128], BF16)
    mask = fw.sb(pfx + "mask", [128, DIL_TOT * 128], BF16)
    la = fw.sb(pfx + "la", [128, 128], F32)
    eb = fw.sb(pfx + "eb", [128, 128], F32)
    enb = fw.sb(pfx + "enb", [128, 128], F32)
    ee = fw.sb(pfx + "ee", [128, 128], F32)
    qdT = fw.sb(pfx + "qdT", [128, 128], BF16)
    kiT = fw.sb(pfx + "kiT", [128, 128], BF16)
    ken = fw.sb(pfx + "ken", [128, 128], BF16)
    scT = fw.sb(pfx + "scT", [128, 128], BF16)
    St = fw.sb(pfx + "S", [128, 256], F32)
    Sbf = fw.sb(pfx + "Sbf", [128, 256], BF16)
    junk = fw.sb(pfx + "junk", [128, 256], F32)
    sm = fw.sb(pfx + "sm", [128, 8], F32)
    on = fw.sb(pfx + "on", [128, 256], F32)
    ob = fw.sb(pfx + "ob", [128, 128], F32)
    sc = fw.sb(pfx + "sc", [128, DIL_TOT * 128], F32)
    P = fw.sb(pfx + "P", [128, DIL_TOT * 128], BF16)
    PT = [fw.sb(pfx + "PT%d" % i, [128, 512], BF16) for i in range(2)]
    oTa = fw.sb(pfx + "oTa", [128, 3, 512], BF16)
    psI = [fw.ps(pfx + "psI%d" % i, [128, 512], F32) for i in range(2)]
    psG = fw.ps(pfx + "psG", [128, 512], F32)
    psS = [fw.ps(pfx + "psS%d" % i, [128, 512], F32) for i in range(2)]
    psT = [fw.ps(pfx + "psT%d" % i, [128, 1024], BF16) for i in range(2)]
    psV = fw.ps(pfx + "psV", [128, 512], F32)
    identF = cst[:, 0, :]
    TriNeg = cst[:, 1, :]
    UNeg = cst[:, 2, :]
    causT = cst[:, 3, :]

    fw.dma("sp", cst[:], cst_d, writes=[("cst",)])
    fw.dma("sp", idb[:], idb_d, writes=[("idb",)])
    fw.dma("sp", mask[:], mask_d, writes=[("mask",)])
    fw.dma("sp", wg2[:], wg2_d, writes=[("wg2",)])
    fw.dma("sp", ngb[:], ng_d.partition_broadcast(128), writes=[("ngb",)])
    for i in range(2):
        fw.op("dve", lambda e, i=i: e.memset(g1a2[i][:], 1.0), writes=[("g1a", i)])

    def load_x(tb):
        for item in x_src(tb * 512, 512):
            c0, c1, tlo, thi, ap = item[:5]
            fw.dma("pool", xT[:, c0:c1, tlo:thi], ap, reads=list(item[5:]), writes=[("xT", c, tlo) for c in range(c0, c1)])

    load_x(0)
    for c4 in range(4):
        if win_preloaded:
            break
        fw.dma("pool", win[:, c4 * 4:(c4 + 1) * 4, :].rearrange("p c n -> p (c n)"),
               win_d[:, c4 * 4 * ME_NC:(c4 + 1) * 4 * ME_NC], writes=[("win", c4)])
    WINK = [("win", i) for i in range(4)]
    XK = [("xT", c, tlo) for c in range(16) for tlo in range(0, 512, 256)]

    ipi = [0]

    def evac(eng, out, in_, bank, reads, writes, func=None, scale=None):
        ex = [("PS", bank)]
        if eng == "act":
            if func is None and scale is None:
                fw.op("act", lambda e: e.copy(out=out, in_=in_), reads=reads, writes=writes, excl=ex)
            else:
                fw.op("act", lambda e: e.activation(out=out, in_=in_, func=(func or AF.Copy),
                                                    scale=(1.0 if scale is None else scale)), reads=reads, writes=writes, excl=ex)
        else:
            if scale is None:
                fw.op("dve", lambda e: e.tensor_copy(out=out, in_=in_), reads=reads, writes=writes, excl=ex)
            else:
                fw.op("dve", lambda e: e.tensor_scalar(out=out, in0=in_, scalar1=scale, scalar2=None, op0=ALU.mult),
                      reads=reads, writes=writes, excl=ex)

    def inproj(tb):
        bs = tb % 2
        qT, kT, ktm, vtm, gs, g1a, dqT = qT2[bs], kT2[bs], ktm2[bs], vtm2[bs], gs2[bs], g1a2[bs], dqT2[bs]
        for gi, (name, off, m) in enumerate(ME_FM):
            p = ipi[0] % 2
            ipi[0] += 1
            bk = "I%d" % p
            for c in range(16):
                fw.op("pe", lambda e, p=p, c=c, off=off, m=m: e.matmul(
                    psI[p][0:m, :], win[:, c, off:off + m], xT[:, c, :], start=(c == 0), stop=(c == 15)),
                    reads=WINK + XK, writes=[("psI", p)], excl=[("PS", bk)])
            yield
            src = psI[p][0:m, :]
            eng = "act" if gi % 2 == 0 else "dve"
            rd = [("psI", p)]
            if name == "q":
                evac(eng, qT[:], src, bk, rd, [("qT", bs)])
            elif name == "k":
                evac(eng, kT[:], src, bk, rd, [("kT", bs)])
            elif name == "g1":
                evac(eng, g1a[0:16, :], src, bk, rd, [("g1a", bs)])
            elif name.startswith("dq"):
                g = int(name[2])
                evac(eng, dqT[g][:], src, bk, rd, [("dqT", g, bs)], scale=128.0 ** -0.5)
            else:
                g = int(name[2])
                evac(eng, dkT[g][:, tb * 512:(tb + 1) * 512], src, bk, rd, [("dkT", g, tb)])
        for tt in range(4):
            kt = tb * 4 + tt
            for half, off in ((0, ME_TM0), (1, ME_TM1)):
                p = ipi[0] % 2
                ipi[0] += 1
                bk = "I%d" % p
                for c in range(16):
                    fw.op("pe", lambda e, p=p, c=c, off=off, tt=tt: e.matmul(
                        psI[p][:], xT[:, c, tt * 128:(tt + 1) * 128], win[:, c, off:off + 512],
                        start=(c == 0), stop=(c == 15)),
                        reads=WINK + XK, writes=[("psI", p)], excl=[("PS", bk)])
                yield
                rd = [("psI", p)]
                if half == 0:
                    evac("dve", ktm[:, tt, :], psI[p][:, 0:128], bk, rd, [("ktm", bs, tt)])
                    evac("dve", vtm[:, tt, :], psI[p][:, 128:384], bk, rd, [("vtm", bs, tt)])
                    evac("dve", dvh[0][:, kt, :], psI[p][:, 384:512], bk, rd, [("dv", 0, kt)])
                else:
                    evac("act", gs[:, tt, :], psI[p][:, 0:256], bk, rd, [("gs", bs, tt)], func=AF.Silu)
                    evac("act", dvh[1][:, kt, :], psI[p][:, 256:384], bk, rd, [("dv", 1, kt)])
                    evac("act", dvh[2][:, kt, :], psI[p][:, 384:512], bk, rd, [("dv", 2, kt)])
                    fw.op("dve", lambda e, tt=tt: e.tensor_tensor(out=gs[:, tt, :], in0=gs[:, tt, :], in1=ngb[:], op=ALU.mult),
                          reads=[("gs", bs, tt), ("ngb",)], writes=[("gs", bs, tt)])

    GX = [("PS", "G")]

    def gla_chunk(c):
        cc = c % 4
        bs = (c // 4) % 2
        qT, kT, ktm, vtm, gs, g1a = qT2[bs], kT2[bs], ktm2[bs], vtm2[bs], gs2[bs], g1a2[bs]
        ts = slice(cc * 128, (cc + 1) * 128)
        fw.op("pe", lambda e: e.matmul(psG[:, 0:128], g1a[0:17, ts], wg2[0:17, :], start=True, stop=True),
              reads=[("g1a", bs), ("wg2",)], writes=[("pG", 0)], excl=GX)
        fw.op("act", lambda e: e.activation(out=la[:], in_=psG[:, 0:128], func=AF.Exp, scale=-1.0),
              reads=[("pG", 0)], writes=[("la",)], excl=GX)
        fw.op("act", lambda e: e.activation(out=la[:], in_=la[:], func=AF.Ln, bias=1.0, scale=1.0),
              reads=[("la",)], writes=[("la",)])
        yield
        fw.op("pe", lambda e: e.matmul(psG[:, 128:256], la[:], TriNeg, start=True, stop=True),
              reads=[("la",), ("cst",)], writes=[("pG", 1)], excl=GX)
        fw.op("pe", lambda e: e.matmul(psG[:, 256:384], UNeg, la[:], start=True, stop=True),
              reads=[("la",), ("cst",)], writes=[("pG", 2)], excl=GX)
        fw.op("act", lambda e: e.activation(out=eb[:], in_=psG[:, 128:256], func=AF.Exp), reads=[("pG", 1)], writes=[("eb",)], excl=GX)
        fw.op("act", lambda e: e.activation(out=enb[:], in_=psG[:, 128:256], func=AF.Exp, scale=-1.0),
              reads=[("pG", 1)], writes=[("enb",)], excl=GX)
        fw.op("act", lambda e: e.activation(out=ee[:], in_=psG[:, 256:384], func=AF.Exp), reads=[("pG", 2)], writes=[("ee",)], excl=GX)
        fw.op("dve", lambda e: e.scalar_tensor_tensor(out=qdT[:], in0=qT[:, ts], scalar=128.0 ** -0.5, in1=eb[:],
                                                       op0=ALU.mult, op1=ALU.mult),
              reads=[("qT", bs), ("eb",)], writes=[("qdT",)])
        fw.op("dve", lambda e: e.tensor_tensor(out=kiT[:], in0=kT[:, ts], in1=enb[:], op=ALU.mult),
              reads=[("kT", bs), ("enb",)], writes=[("kiT",)])
        fw.op("dve", lambda e: e.tensor_tensor(out=ken[:], in0=ktm[:, cc, :], in1=ee[:], op=ALU.mult),
              reads=[("ktm", bs, cc), ("ee",)], writes=[("ken",)])
        yield
        fw.op("pe", lambda e: e.matmul(psG[:, 384:512], kiT[:], qdT[:], start=True, stop=True),
              reads=[("kiT",), ("qdT",)], writes=[("pG", 3)], excl=GX)
        fw.op("dve", lambda e: e.tensor_tensor(out=scT[:], in0=psG[:, 384:512], in1=causT, op=ALU.mult),
              reads=[("pG", 3), ("cst",)], writes=[("scT",)], excl=GX)
        yield
        fw.op("pe", lambda e: e.matmul(psG[:, 0:256], scT[:], vtm[:, cc, :], start=True, stop=(c == 0)),
              reads=[("scT",), ("vtm", bs, cc)], writes=[("pG", 0), ("pG", 1)], excl=GX)
        if c > 0:
            fw.op("pe", lambda e: e.matmul(psG[:, 0:256], qdT[:], Sbf[:], start=False, stop=True),
                  reads=[("qdT",), ("Sbf",)], writes=[("pG", 0), ("pG", 1)], excl=GX)
        fw.op("pe", lambda e: e.matmul(psG[:, 256:512], ken[:], vtm[:, cc, :], start=True, stop=True),
              reads=[("ken",), ("vtm", bs, cc)], writes=[("pG", 2), ("pG", 3)], excl=GX)
        if c == 0:
            fw.op("dve", lambda e: e.tensor_copy(out=St[:], in_=psG[:, 256:512]), reads=[("pG", 2), ("pG", 3)], writes=[("S",)], excl=GX)
        else:
            fw.op("dve", lambda e: e.scalar_tensor_tensor(out=St[:], in0=St[:], scalar=eb[:, 127:128], in1=psG[:, 256:512],
                                                           op0=ALU.mult, op1=ALU.add),
                  reads=[("S",), ("eb",), ("pG", 2), ("pG", 3)], writes=[("S",)], excl=GX)
        yield
        fw.op("act", lambda e: e.activation(out=junk[:], in_=psG[:, 0:256], func=AF.Square, accum_out=sm[:, 0:1]),
              reads=[("pG", 0), ("pG", 1)], writes=[("junk",), ("sm", 0)], excl=GX)
        fw.op("act", lambda e: e.copy(out=Sbf[:], in_=St[:]), reads=[("S",)], writes=[("Sbf",)])
        fw.op("act", lambda e: e.activation(out=sm[:, 1:2], in_=sm[:, 0:1], func=AF.Sqrt, bias=1e-6, scale=1.0 / 256),
              reads=[("sm", 0)], writes=[("sm", 1)])
        fw.op("dve", lambda e: e.reciprocal(out=sm[:, 2:3], in_=sm[:, 1:2]), reads=[("sm", 1)], writes=[("sm", 2)])
        fw.op("dve", lambda e: e.scalar_tensor_tensor(out=on[:], in0=psG[:, 0:256], scalar=sm[:, 2:3], in1=gs[:, cc, :],
                                                       op0=ALU.mult, op1=ALU.mult),
              reads=[("pG", 0), ("pG", 1), ("sm", 2), ("gs", bs, cc)], writes=[("on",)], excl=GX)
        for e2 in range(2):
            fw.op("pe", lambda e, e2=e2: e.transpose(out=psG[:, e2 * 128:(e2 + 1) * 128],
                                                     in_=on[:, e2 * 128:(e2 + 1) * 128], identity=identF),
                  reads=[("on",), ("cst",)], writes=[("pG", e2)], excl=GX)
        fw.op("act", lambda e: e.copy(out=oTa[:, 0:2, ts], in_=psG[:, 0:256].rearrange("p (a n) -> p a n", a=2)),
              reads=[("pG", 0), ("pG", 1)], writes=[("oTa", 0, cc)], excl=GX)
        yield

    sci = [0]
    pti = [0]
    SCK = [("sc", g, k) for g in range(3) for k in range(5)]
    VX = [("PS", "V")]

    def dil_tile(qt):
        qc = qt % 4
        bs = (qt // 4) % 2
        dqT = dqT2[bs]
        if qt < 16:
            fw.op("dve", lambda e: e.memset(sc[:], NEG), writes=SCK)
        tiles = []
        for g in range(3):
            nt = DIL_NT[g]
            lo = max(0, qt - nt + 1)
            nvt = qt - lo + 1
            dst0 = (DIL_OFF[g] + nt - nvt) * 128
            for i in range(nvt):
                tiles.append((g, lo + i, dst0 + i * 128))
            n = nvt * 128
            done = 0
            while done < n:
                m = min(512, n - done)
                p = sci[0] % 2
                sci[0] += 1
                k0 = lo * 128 + done
                d0 = dst0 + done
                tbs = sorted(set(range(k0 // 512, (k0 + m - 1) // 512 + 1)))
                fw.op("pe", lambda e, p=p, g=g, m=m, k0=k0: e.matmul(
                    psS[p][:, 0:m], dqT[g][:, qc * 128:(qc + 1) * 128], dkT[g][:, k0:k0 + m], start=True, stop=True),
                    reads=[("dqT", g, bs)] + [("dkT", g, t) for t in tbs], writes=[("psS", p)], excl=[("PS", "S%d" % p)])
                fw.op("dve", lambda e, p=p, m=m, d0=d0: e.tensor_tensor(
                    out=sc[:, d0:d0 + m], in0=psS[p][:, 0:m], in1=mask[:, d0:d0 + m], op=ALU.add),
                    reads=[("psS", p), ("mask",)], writes=[("sc", g, done // 512)], excl=[("PS", "S%d" % p)])
                done += m
                yield
        fw.op("dve", lambda e: e.reduce_max(out=sm[:, 3:4], in_=sc[:], axis=AX.X), reads=SCK, writes=[("sm", 3)])
        fw.op("dve", lambda e: e.tensor_scalar(out=sm[:, 4:5], in0=sm[:, 3:4], scalar1=-1.0, scalar2=None, op0=ALU.mult),
              reads=[("sm", 3)], writes=[("sm", 4)])
        fw.op("act", lambda e: e.activation(out=P[:], in_=sc[:], func=AF.Exp, bias=sm[:, 4:5], scale=1.0, accum_out=sm[:, 5:6]),
              reads=SCK + [("sm", 4)], writes=[("P",), ("sm", 5)])
        fw.op("dve", lambda e: e.reciprocal(out=sm[:, 6:7], in_=sm[:, 5:6]), reads=[("sm", 5)], writes=[("sm", 6)])
        yield
        ntile = len(tiles)
        for g0 in range(0, ntile, 4):
            grp = tiles[g0:g0 + 4]
            h = pti[0] % 2
            pti[0] += 1
            tx = [("PS", "T%d" % h)]
            for k, (g, kt, col) in enumerate(grp):
                fw.op("pe", lambda e, k=k, col=col, h=h: e.transpose(
                    out=psT[h][:, k * 128:(k + 1) * 128], in_=P[:, col:col + 128], identity=idb[:]),
                    reads=[("P",), ("idb",)], writes=[("psT", h)], excl=tx)
            n = len(grp) * 128
            evac("act" if h == 0 else "dve", PT[h][:, 0:n], psT[h][:, 0:n], "T%d" % h, [("psT", h)], [("PT", h)])
            for k, (g, kt, col) in enumerate(grp):
                idx = g0 + k
                fw.op("pe", lambda e, k=k, g=g, kt=kt, h=h, idx=idx: e.matmul(
                    psV[:, 0:128], PT[h][:, k * 128:(k + 1) * 128], dvh[g][:, kt, :],
                    start=(idx == 0), stop=(idx == ntile - 1)),
                    reads=[("PT", h), ("dv", g, kt)], writes=[("pV", 0)], excl=VX)
            yield
        fw.op("dve", lambda e: e.tensor_scalar(out=ob[:], in0=psV[:, 0:128], scalar1=sm[:, 6:7], scalar2=None, op0=ALU.mult),
              reads=[("pV", 0), ("sm", 6)], writes=[("ob",)], excl=VX)
        fw.op("pe", lambda e: e.transpose(out=psV[:, 128:256], in_=ob[:], identity=identF),
              reads=[("ob",), ("cst",)], writes=[("pV", 1)], excl=VX)
        fw.op("act", lambda e: e.copy(out=oTa[:, 2, qc * 128:(qc + 1) * 128], in_=psV[:, 128:256]),
              reads=[("pV", 1)], writes=[("oTa", 1, qc)], excl=VX)

    def mixers(tb):
        for cc in range(4):
            yield from gla_chunk(tb * 4 + cc)
            yield from dil_tile(tb * 4 + cc)

    for _ in inproj(0):
        pass
    load_x(1)
    for tb in range(NB):
        bg = inproj(tb + 1) if tb + 1 < NB else iter(())
        bg_live = [tb + 1 < NB]

        def bg_step(tb=tb):
            if not bg_live[0]:
                return
            try:
                next(bg)
            except StopIteration:
                bg_live[0] = False
                if tb + 2 < NB:
                    load_x(tb + 2)
                elif tail_prefetch is not None:
                    tail_prefetch(WINK)

        n = 0
        step = max(1, int(0.75 * ME_NFG[tb] / 17))
        for _ in mixers(tb):
            n += 1
            if n % step == 0:
                bg_step()
        while bg_live[0]:
            bg_step()
        fw.dma("sp", o_dst(tb * 512, 512), oTa[:],
               reads=[("oTa", a, q) for a in range(2) for q in range(4)], writes=[("odram", tb * 512)], is_output=True)
        if post is not None:
            post(tb * 512, 512)


def me_consts():
    import ml_dtypes
    j = np.arange(128)[:, None]
    i = np.arange(128)[None, :]
    cst = np.zeros((128, 4, 128), np.float32)
    cst[:, 0, :] = np.eye(128)
    cst[:, 1, :] = np.where(j <= i, -1.0 / 16, 0.0)
    cst[:, 2, :] = np.where(j > i, -1.0 / 16, 0.0)
    cst[:, 3, :] = np.where(j <= i, 1.0, 0.0)
    mask = np.full((128, DIL_TOT * 128), NEG, np.float32)
    qi = np.arange(128)[:, None]
    kj = np.arange(128)[None, :]
    for g, d in enumerate((1, 4, 16)):
        nt = DIL_NT[g]
        for Dt in range(nt):
            delta = Dt * 128 + qi - kj
            ok = (delta >= 0) & (delta <= 128 * d) & (delta % d == 0)
            pos = DIL_OFF[g] + nt - 1 - Dt
            mask[:, pos * 128:(pos + 1) * 128] = np.where(ok, 0.0, NEG)
    return cst, mask.astype(ml_dtypes.bfloat16), np.eye(128, dtype=np.float32).astype(ml_dtypes.bfloat16)


def me_win_layout(w_in, h):
    def cols(a, n):
        return w_in[:, a:a + n]
    gq = cols(0 + h * 128, 128)
    gk = cols(512 + h * 128, 128)
    gv = cols(1024 + h * 256, 256)
    gg = cols(2048, 16)
    gr = cols(2064 + h * 256, 256)
    dq = [cols(3088 + g * 512 + h * 128, 128) for g in range(3)]
    dk = [cols(4624 + g * 512 + h * 128, 128) for g in range(3)]
    dv = [cols(6160 + g * 512 + h * 128, 128) for g in range(3)]
    w = np.concatenate([gq, gk, gg] + dq + dk + [gk, gv, dv[0], gr, dv[1], dv[2]], axis=1)
    assert w.shape[1] == ME_NC
    return np.ascontiguousarray(w.reshape(16, 128, ME_NC).transpose(1, 0, 2).reshape(128, 16 * ME_NC))


def build_ME(x_dt=F32):
    nc = bass.Bass("TRN2", target_bir_lowering=False)
    xT_d = nc.dram_tensor("xT", [D, S], x_dt, kind="ExternalInput").ap()
    win_d = nc.dram_tensor("win", [128, 16 * ME_NC], F32, kind="ExternalInput").ap()
    wg2_d = nc.dram_tensor("wg2", [17, 128], F32, kind="ExternalInput").ap()
    ng_d = nc.dram_tensor("ng", [256], F32, kind="ExternalInput").ap()
    cst_d = nc.dram_tensor("cst", [128, 4, 128], F32, kind="ExternalInput").ap()
    mask_d = nc.dram_tensor("mask", [128, DIL_TOT * 128], BF16, kind="ExternalInput").ap()
    idb_d = nc.dram_tensor("idb", [128, 128], BF16, kind="ExternalInput").ap()
    oT_d = nc.dram_tensor("oT", [384, S], BF16, kind="ExternalOutput").ap()
    fw = FW(nc)
    emit_ME(fw, lambda t0, n: [(0, 16, tl, tl + 256, xT_d[:, t0 + tl:t0 + tl + 256].rearrange("(c p) t -> p c t", p=128)) for tl in range(0, n, 256)], win_d, wg2_d, ng_d, cst_d, mask_d, idb_d,
            lambda t0, n: oT_d[:, t0:t0 + n].rearrange("(a p) t -> p a t", p=128))
    fw.emit()
    fw.close()
    return nc

import math

D = 2048
S = 4096
TB = 256
NTB = TB // 128
NBO = S // TB
NEG = -30000.0
MO_FM = [("cq", 0, 128), ("ck", 128, 128), ("ci", 256, 128), ("cf", 384, 128)] + \
        [("mq%d" % i, 512 + i * 128, 128) for i in range(4)] + [("mkv%d" % i, 1024 + i * 128, 128) for i in range(4)] + \
        [("krA", 1536, 64), ("krB", 1600, 64)]
MO_TM = 1664
MO_NC = 2176
QSCALE = (128 + 64) ** -0.5
TWO_PI = 2 * math.pi
C1 = 6.28125
C2 = TWO_PI - C1


def emit_MO(fw, x_src, win_d, wuq_d, wukv_d, cols_d, ng_d, pos_d, cst_d, idb_d, o_dst, pfx="O", post=None, win_ext=None, win_preloaded=False, tail_prefetch=None):
    win = win_ext if win_ext is not None else fw.sb(pfx + "win", [128, 16, MO_NC], BF16)
    xT = fw.sb(pfx + "xT", [128, 16, TB], BF16)
    wuq = fw.sb(pfx + "wuq", [128, 4, 512], BF16)
    wukv = fw.sb(pfx + "wukv", [128, 4, 512], BF16)
    knT = [fw.sb(pfx + "knT%d" % h, [128, S], BF16) for h in range(2)]
    krT = fw.sb(pfx + "krT", [64, S], BF16)
    vh = fw.sb(pfx + "vh", [128, 32, 256], BF16)
    sc = fw.sb(pfx + "sc", [128, S], F32)
    P = fw.sb(pfx + "P", [128, S], BF16)
    PT = [fw.sb(pfx + "PT%d" % i, [128, 512], BF16) for i in range(2)]
    mqT = fw.sb(pfx + "mqT", [128, 4, TB], BF16)
    mkvT = fw.sb(pfx + "mkvT", [128, 4, TB], BF16)
    sq = fw.sb(pfx + "sq", [128, 4, TB], BF16)
    rsq = fw.sb(pfx + "rsq", [128, TB], F32)
    rskv = fw.sb(pfx + "rskv", [128, TB], F32)
    qn2 = [[fw.sb(pfx + "qn%d_%d" % (h, i), [128, TB], BF16) for h in range(2)] for i in range(2)]
    qr2 = [[fw.sb(pfx + "qr%d_%d" % (h, i), [64, TB], BF16) for h in range(2)] for i in range(2)]
    CS = fw.sb(pfx + "CS", [64, TB], F32)
    SS = fw.sb(pfx + "SS", [64, TB], F32)
    posi = fw.sb(pfx + "posi", [64, TB], I32)
    th = fw.sb(pfx + "th", [64, TB], F32)
    ru = fw.sb(pfx + "ru", [64, TB], F32)
    rk = fw.sb(pfx + "rk", [64, TB], F32)
    rki = fw.sb(pfx + "rki", [64, TB], I32)
    rt1 = fw.sb(pfx + "rt1", [64, TB], F32)
    rt2 = fw.sb(pfx + "rt2", [64, TB], F32)
    ubq = fw.sb(pfx + "ubq", [128, 3 + TB], F32)
    ubk = fw.sb(pfx + "ubk", [128, 3 + TB], F32)
    acc = fw.sb(pfx + "acc", [128, TB], F32)
    qTb2 = [fw.sb(pfx + "qTb%d" % i, [128, TB], BF16) for i in range(2)]
    kTb2 = [fw.sb(pfx + "kTb%d" % i, [128, TB], BF16) for i in range(2)]
    liB2 = [fw.sb(pfx + "liB%d" % i, [128, TB], F32) for i in range(2)]
    spB2 = [fw.sb(pfx + "spB%d" % i, [128, TB], F32) for i in range(2)]
    vtm2 = [fw.sb(pfx + "vtm%d" % i, [128, NTB, 257], BF16) for i in range(2)]
    gsig2 = [fw.sb(pfx + "gsig%d" % i, [128, NTB, 256], F32) for i in range(2)]
    ngb = fw.sb(pfx + "ngb", [128, 256], F32)
    cols = fw.sb(pfx + "cols", [128, 32], F32)
    cst = fw.sb(pfx + "cst", [128, 4, 128], F32)
    idb = fw.sb(pfx + "idb", [128, 128], BF16)
    onesb = fw.sb(pfx + "onesb", [128, 128], BF16)
    bB = fw.sb(pfx + "bB", [128, 128], F32)
    GB = fw.sb(pfx + "GB", [128, 128], F32)
    DT = fw.sb(pfx + "DT", [128, 128], F32)
    eBt = fw.sb(pfx + "eB", [128, 128], F32)
    qkDT = fw.sb(pfx + "qkDT", [128, 128], BF16)
    qsT = fw.sb(pfx + "qsT", [128, 128], BF16)
    kw = fw.sb(pfx + "kw", [128, 128], BF16)
    Ca = fw.sb(pfx + "Ca", [128, 257], F32)
    Cab = fw.sb(pfx + "Cab", [128, 257], BF16)
    hh = fw.sb(pfx + "hh", [128, 256], F32)
    junk = fw.sb(pfx + "junk", [128, 128], F32)
    od = fw.sb(pfx + "od", [128, 128], F32)
    sm = [fw.sb(pfx + "sm%d" % i, [128, 24], F32) for i in range(2)]
    am = fw.sb(pfx + "am", [128, 8], F32)
    junk2 = fw.sb(pfx + "junk2", [128, 128], F32)
    st6 = fw.sb(pfx + "st6", [128, 6], F32)
    oTa = fw.sb(pfx + "oTa", [128, 4, TB], BF16)
    psI = [fw.ps(pfx + "psI%d" % i, [128, 512], F32) for i in range(2)]
    psG = fw.ps(pfx + "psG", [128, 512], F32)
    psS = [fw.ps(pfx + "psS%d" % i, [128, 512], F32) for i in range(2)]
    psT = [fw.ps(pfx + "psT%d" % i, [128, 1024], BF16) for i in range(2)]
    psV = fw.ps(pfx + "psV", [128, 512], F32)
    identF = cst[:, 0, :]
    causT = cst[:, 1, :]
    causAdd = cst[:, 2, :]
    onesF = cst[:, 3, :]
    CW = lambda i: cols[:, i:i + 1]

    fw.dma("sp", cst[:], cst_d, writes=[("cst",)])
    fw.dma("sp", idb[:], idb_d, writes=[("idb",)])
    fw.dma("sp", cols[:], cols_d, writes=[("cols",)])
    fw.dma("sp", ngb[:], ng_d.partition_broadcast(128), writes=[("ngb",)])
    for i in range(2):
        fw.op("dve", lambda e, i=i: e.memset(vtm2[i][:], 1.0), writes=[("vtm", i, t) for t in range(NTB)])
    fw.op("dve", lambda e: e.memset(onesb[:], 1.0), writes=[("onesb",)])
    fw.op("dve", lambda e: e.memset(ubq[:, 0:3], 0.0), writes=[("ubq",)])
    fw.op("dve", lambda e: e.memset(ubk[:, 0:3], 0.0), writes=[("ubk",)])

    def load_x(tb):
        for item in x_src(tb * TB, TB):
            c0, c1, tlo, thi, ap = item[:5]
            fw.dma("pool", xT[:, c0:c1, tlo:thi], ap, reads=list(item[5:]), writes=[("xT", c, tlo) for c in range(c0, c1)])

    def load_pos(tb):
        fw.dma("sp", posi[:], pos_d[tb * TB:(tb + 1) * TB].partition_broadcast(64), writes=[("posi",)])

    load_x(0)
    load_pos(0)
    for c4 in range(4):
        if win_preloaded:
            break
        fw.dma("pool", win[:, c4 * 4:(c4 + 1) * 4, :].rearrange("p c n -> p (c n)"),
               win_d[:, c4 * 4 * MO_NC:(c4 + 1) * 4 * MO_NC], writes=[("win", c4)])
    fw.dma("pool", wuq[:].rearrange("p c n -> p (c n)"), wuq_d, writes=[("wuq",)])
    fw.dma("pool", wukv[:].rearrange("p c n -> p (c n)"), wukv_d, writes=[("wukv",)])
    WINK = [("win", i) for i in range(4)]
    XK = [("xT", c, tlo) for c in range(16) for tlo in range(0, TB, 256)]
    ipi = [0]

    def nextI():
        p = ipi[0] % 2
        ipi[0] += 1
        return p

    def rope_tables(tb):
        fw.op("dve", lambda e: e.tensor_copy(out=th[:], in_=posi[:]), reads=[("posi",)], writes=[("th",)])
        fw.op("dve", lambda e: e.tensor_scalar(out=th[:], in0=th[:], scalar1=CW(12)[0:64, :], scalar2=None, op0=ALU.mult),
              reads=[("th",), ("cols",)], writes=[("th",)])
        for (dst, ph, key) in ((CS, 13, "CS"), (SS, 14, "SS")):
            fw.op("dve", lambda e, ph=ph: e.tensor_scalar(out=ru[:], in0=th[:], scalar1=CW(ph)[0:64, :], scalar2=None, op0=ALU.add),
                  reads=[("th",), ("cols",)], writes=[("ru",)])
            fw.op("dve", lambda e: e.tensor_scalar(out=rk[:], in0=ru[:], scalar1=1.0 / TWO_PI, scalar2=None, op0=ALU.mult),
                  reads=[("ru",)], writes=[("rk",)])
            fw.op("dve", lambda e: e.tensor_copy(out=rki[:], in_=rk[:]), reads=[("rk",)], writes=[("rki",)])
            fw.op("dve", lambda e: e.tensor_copy(out=rk[:], in_=rki[:]), reads=[("rki",)], writes=[("rk",)])
            fw.op("dve", lambda e: e.scalar_tensor_tensor(out=ru[:], in0=rk[:], scalar=-C1, in1=ru[:], op0=ALU.mult, op1=ALU.add),
                  reads=[("rk",), ("ru",)], writes=[("ru",)])
            fw.op("dve", lambda e: e.scalar_tensor_tensor(out=ru[:], in0=rk[:], scalar=-C2, in1=ru[:], op0=ALU.mult, op1=ALU.add),
                  reads=[("rk",), ("ru",)], writes=[("ru",)])
            fw.op("dve", lambda e: e.tensor_scalar(out=rk[:], in0=ru[:], scalar1=math.pi, scalar2=-TWO_PI, op0=ALU.is_gt, op1=ALU.mult),
                  reads=[("ru",)], writes=[("rk",)])
            fw.op("dve", lambda e: e.tensor_tensor(out=ru[:], in0=ru[:], in1=rk[:], op=ALU.add), reads=[("ru",), ("rk",)], writes=[("ru",)])
            fw.op("dve", lambda e: e.tensor_scalar(out=rk[:], in0=ru[:], scalar1=-math.pi, scalar2=TWO_PI, op0=ALU.is_lt, op1=ALU.mult),
                  reads=[("ru",)], writes=[("rk",)])
            fw.op("dve", lambda e: e.tensor_tensor(out=ru[:], in0=ru[:], in1=rk[:], op=ALU.add), reads=[("ru",), ("rk",)], writes=[("ru",)])
            fw.op("dve", lambda e: e.tensor_scalar(out=ru[:], in0=ru[:], scalar1=-3.1415925, scalar2=3.1415925, op0=ALU.max, op1=ALU.min),
                  reads=[("ru",)], writes=[("ru",)])
            fw.op("act", lambda e, dst=dst: e.activation(out=dst[:], in_=ru[:], func=AF.Sin), reads=[("ru",)], writes=[(key,)])

    def rope_apply(pA, bkA, pB, bkB, dst, dkey, extra=None):
        fw.op("dve", lambda e: e.tensor_tensor(out=rt1[:], in0=psI[pA][0:64, 0:TB], in1=CS[:], op=ALU.mult),
              reads=[("psI", pA), ("CS",)], writes=[("rt1",)], excl=[("PS", bkA)])
        fw.op("dve", lambda e: e.tensor_tensor(out=rt2[:], in0=psI[pB][0:64, 0:TB], in1=SS[:], op=ALU.mult),
              reads=[("psI", pB), ("SS",)], writes=[("rt2",)], excl=[("PS", bkB)])
        if extra is None:
            fw.op("dve", lambda e: e.tensor_tensor(out=dst, in0=rt1[:], in1=rt2[:], op=ALU.add),
                  reads=[("rt1",), ("rt2",)], writes=[dkey])
        else:
            fw.op("dve", lambda e: e.tensor_tensor(out=rt1[:], in0=rt1[:], in1=rt2[:], op=ALU.add),
                  reads=[("rt1",), ("rt2",)], writes=[("rt1",)])
            fw.op("dve", lambda e: e.scalar_tensor_tensor(out=dst, in0=rt1[:], scalar=QSCALE, in1=extra[0:64, :], op0=ALU.mult, op1=ALU.mult),
                  reads=[("rt1",), ("rsq",)], writes=[dkey])

    def rstd_rows(src_key, dst, dkey):
        fw.op("act", lambda e: e.activation(out=dst[:], in_=psG[:, 0:TB], func=AF.Sqrt, bias=1e-6, scale=1.0 / 512),
              reads=[("pG",)], writes=[dkey], excl=[("PS", "G")])
        fw.op("dve", lambda e: e.reciprocal(out=dst[:], in_=dst[:]), reads=[dkey], writes=[dkey])

    def inproj(tb):
        t0 = tb * TB
        bs = tb % 2
        liB, spB, vtm, gsig = liB2[bs], spB2[bs], vtm2[bs], gsig2[bs]
        for gi, (name, off, m) in enumerate(MO_FM):
            if name == "krA":
                continue
            if name == "krB":
                pA, pB = nextI(), nextI()
                for (p, o2) in ((pA, 1536), (pB, 1600)):
                    for c in range(16):
                        fw.op("pe", lambda e, p=p, c=c, o2=o2: e.matmul(
                            psI[p][0:64, 0:TB], win[:, c, o2:o2 + 64], xT[:, c, :], start=(c == 0), stop=(c == 15)),
                            reads=WINK + XK, writes=[("psI", p)], excl=[("PS", "I%d" % p)])
                yield
                rope_apply(pA, "I%d" % pA, pB, "I%d" % pB, krT[:, t0:t0 + TB], ("krT", tb))
                continue
            p = nextI()
            bk = "I%d" % p
            ex = [("PS", bk)]
            for c in range(16):
                fw.op("pe", lambda e, p=p, c=c, off=off, m=m: e.matmul(
                    psI[p][0:m, 0:TB], win[:, c, off:off + m], xT[:, c, :], start=(c == 0), stop=(c == 15)),
                    reads=WINK + XK, writes=[("psI", p)], excl=ex)
            yield
            src = psI[p][:, 0:TB]
            rd = [("psI", p)]
            if name == "cq":
                fw.op("act", lambda e, src=src: e.copy(out=ubq[:, 3:3 + TB], in_=src), reads=rd, writes=[("ubq",)], excl=ex)
            elif name == "ck":
                fw.op("dve", lambda e, src=src: e.tensor_copy(out=ubk[:, 3:3 + TB], in_=src), reads=rd, writes=[("ubk",)], excl=ex)
            elif name == "ci":
                fw.op("act", lambda e, src=src: e.activation(out=liB[:], in_=src, func=AF.Identity, bias=CW(10), scale=1.0),
                      reads=rd + [("cols",)], writes=[("liB", bs)], excl=ex)
            elif name == "cf":
                fw.op("act", lambda e, src=src: e.activation(out=spB[:], in_=src, func=AF.Exp, bias=CW(11), scale=-1.0),
                      reads=rd + [("cols",)], writes=[("spB", bs)], excl=ex)
                fw.op("act", lambda e: e.activation(out=spB[:], in_=spB[:], func=AF.Ln, bias=1.0, scale=1.0),
                      reads=[("spB", bs)], writes=[("spB", bs)])
            elif name.startswith("mq") or name.startswith("mkv"):
                isq = name.startswith("mq")
                ci_ = int(name[-1])
                dstT = mqT if isq else mkvT
                gcol = (16 if isq else 20) + ci_
                lat = "mqT" if isq else "mkvT"
                fw.op("act", lambda e, src=src, ci_=ci_: e.activation(out=sq[:, ci_, :], in_=src, func=AF.Square), reads=rd, writes=[("sq", ci_)], excl=ex)
                fw.op("act", lambda e, src=src, dstT=dstT, ci_=ci_, gcol=gcol: e.activation(
                    out=dstT[:, ci_, :], in_=src, func=AF.Copy, scale=CW(gcol)),
                    reads=rd + [("cols",)], writes=[(lat, ci_)], excl=ex)
                if ci_ == 3:
                    for c4 in range(4):
                        fw.op("pe", lambda e, c4=c4: e.matmul(psG[:, 0:TB], onesb[:], sq[:, c4, :], start=(c4 == 0), stop=(c4 == 3)),
                              reads=[("onesb",), ("sq", c4)], writes=[("pG",)], excl=[("PS", "G")])
                    rstd_rows(lat, rsq if isq else rskv, ("rsq",) if isq else ("rskv",))
        for tt in range(NTB):
            p = nextI()
            ex = [("PS", "I%d" % p)]
            for c in range(16):
                fw.op("pe", lambda e, p=p, c=c, tt=tt: e.matmul(
                    psI[p][:], xT[:, c, tt * 128:(tt + 1) * 128], win[:, c, MO_TM:MO_TM + 512], start=(c == 0), stop=(c == 15)),
                    reads=WINK + XK, writes=[("psI", p)], excl=ex)
            yield
            fw.op("act", lambda e, p=p, tt=tt: e.activation(out=gsig[:, tt, :], in_=psI[p][:, 256:512], func=AF.Sigmoid),
                  reads=[("psI", p)], writes=[("gsig", bs, tt)], excl=ex)
            fw.op("act", lambda e, p=p, tt=tt: e.copy(out=vtm[:, tt, 0:256], in_=psI[p][:, 0:256]),
                  reads=[("psI", p)], writes=[("vtm", bs, tt)], excl=ex)
            fw.op("dve", lambda e, tt=tt: e.tensor_tensor(out=gsig[:, tt, :], in0=gsig[:, tt, :], in1=ngb[:], op=ALU.mult),
                  reads=[("gsig", bs, tt), ("ngb",)], writes=[("gsig", bs, tt)])

    def conv_silu(ub, ukey, woff, bcol, dst, dkey, scale):
        fw.op("dve", lambda e: e.tensor_scalar(out=acc[:], in0=ub[:, 0:TB], scalar1=CW(woff), scalar2=None, op0=ALU.mult),
              reads=[(ukey,), ("cols",)], writes=[("acc",)])
        for tau in range(1, 4):
            fw.op("dve", lambda e, tau=tau: e.scalar_tensor_tensor(out=acc[:], in0=ub[:, tau:tau + TB], scalar=CW(woff + tau), in1=acc[:],
                                                                     op0=ALU.mult, op1=ALU.add),
                  reads=[(ukey,), ("acc",), ("cols",)], writes=[("acc",)])
        fw.op("dve", lambda e: e.tensor_copy(out=ub[:, 0:3], in_=ub[:, TB:TB + 3]), reads=[(ukey,)], writes=[(ukey,)])
        fw.op("act", lambda e: e.activation(out=acc[:], in_=acc[:], func=AF.Silu, bias=CW(bcol), scale=1.0),
              reads=[("acc",), ("cols",)], writes=[("acc",)])
        fw.op("dve", lambda e: e.tensor_scalar(out=dst[:], in0=acc[:], scalar1=scale, scalar2=None, op0=ALU.mult),
              reads=[("acc",)], writes=[dkey])
        yield

    def mla_proj(tb):
        t0 = tb * TB
        bs = tb % 2
        qn, qr = qn2[bs], qr2[bs]
        for h in range(2):
            p = nextI()
            ex = [("PS", "I%d" % p)]
            for c in range(4):
                fw.op("pe", lambda e, p=p, c=c, h=h: e.matmul(psI[p][:, 0:TB], wuq[:, c, h * 256:h * 256 + 128], mqT[:, c, :],
                                                              start=(c == 0), stop=(c == 3)),
                      reads=[("wuq",), ("mqT", c)], writes=[("psI", p)], excl=ex)
            yield
            fw.op("dve", lambda e, p=p, h=h: e.scalar_tensor_tensor(out=qn[h][:], in0=psI[p][:, 0:TB], scalar=QSCALE, in1=rsq[:],
                                                                     op0=ALU.mult, op1=ALU.mult),
                  reads=[("psI", p), ("rsq",)], writes=[("qn", bs, h)], excl=ex)
            pA, pB = nextI(), nextI()
            for (p, o2) in ((pA, h * 256 + 128), (pB, h * 256 + 192)):
                for c in range(4):
                    fw.op("pe", lambda e, p=p, c=c, o2=o2: e.matmul(psI[p][0:64, 0:TB], wuq[:, c, o2:o2 + 64], mqT[:, c, :],
                                                                    start=(c == 0), stop=(c == 3)),
                          reads=[("wuq",), ("mqT", c)], writes=[("psI", p)], excl=[("PS", "I%d" % p)])
            yield
            rope_apply(pA, "I%d" % pA, pB, "I%d" % pB, qr[h][:], ("qr", bs, h), extra=rsq)
            p = nextI()
            ex = [("PS", "I%d" % p)]
            for c in range(4):
                fw.op("pe", lambda e, p=p, c=c, h=h: e.matmul(psI[p][:, 0:TB], wukv[:, c, h * 128:(h + 1) * 128], mkvT[:, c, :],
                                                              start=(c == 0), stop=(c == 3)),
                      reads=[("wukv",), ("mkvT", c)], writes=[("psI", p)], excl=ex)
            yield
            fw.op("dve", lambda e, p=p, h=h: e.tensor_tensor(out=knT[h][:, t0:t0 + TB], in0=psI[p][:, 0:TB], in1=rskv[:], op=ALU.mult),
                  reads=[("psI", p), ("rskv",)], writes=[("knT", h, tb)], excl=ex)
        for tt in range(NTB):
            kt = tb * NTB + tt
            p = nextI()
            ex = [("PS", "I%d" % p)]
            for c in range(4):
                fw.op("pe", lambda e, p=p, c=c, tt=tt: e.matmul(psI[p][:, 0:256], mkvT[:, c, tt * 128:(tt + 1) * 128], wukv[:, c, 256:512],
                                                                start=(c == 0), stop=(c == 3)),
                      reads=[("wukv",), ("mkvT", c)], writes=[("psI", p)], excl=ex)
            fw.op("dve", lambda e, tt=tt: e.scalar_tensor_tensor(out=junk2[:], in0=rskv[:, tt * 128:(tt + 1) * 128], scalar=1.0, in1=identF, op0=ALU.mult, op1=ALU.mult, accum_out=am[:, 7:8]),
                  reads=[("rskv",), ("cst",)], writes=[("junk2",), ("am", 7)])
            fw.op("dve", lambda e, p=p, kt=kt: e.tensor_scalar(out=vh[:, kt, :], in0=psI[p][:, 0:256], scalar1=am[:, 7:8], scalar2=None, op0=ALU.mult),
                  reads=[("psI", p), ("am", 7)], writes=[("vh", kt)], excl=ex)
            yield

    GX = [("PS", "G")]

    def mlstm_chunk(c):
        cc = c % NTB
        bs = (c // NTB) % 2
        qTb, kTb, liB, spB, vtm, gsig = qTb2[bs], kTb2[bs], liB2[bs], spB2[bs], vtm2[bs], gsig2[bs]
        ts = slice(cc * 128, (cc + 1) * 128)
        s = sm[c % 2]
        sp_ = sm[(c + 1) % 2]
        col = lambda i: s[:, i:i + 1]
        fw.op("dve", lambda e: e.tensor_tensor_scan(out=bB[:], data0=onesF, data1=spB[:, ts], initial=0.0, op0=ALU.mult, op1=ALU.subtract),
              reads=[("spB", bs), ("cst",)], writes=[("bB",)])
        fw.op("dve", lambda e: e.tensor_tensor(out=GB[:], in0=liB[:, ts], in1=bB[:], op=ALU.subtract), reads=[("liB", bs), ("bB",)], writes=[("GB",)])
        fw.op("dve", lambda e: e.reduce_max(out=col(0), in_=liB[:, ts], axis=AX.X), reads=[("liB", bs)], writes=[("sm", c % 2, 0)])
        if c == 0:
            fw.op("dve", lambda e: e.tensor_copy(out=col(1), in_=col(0)), reads=[("sm", c % 2, 0)], writes=[("sm", c % 2, 1)])
        else:
            fw.op("dve", lambda e: e.tensor_tensor(out=col(1), in0=col(0), in1=sp_[:, 1:2], op=ALU.max),
                  reads=[("sm", c % 2, 0), ("sm", (c + 1) % 2, 1)], writes=[("sm", c % 2, 1)])
        fw.op("dve", lambda e: e.scalar_tensor_tensor(out=junk[:], in0=bB[:], scalar=1.0, in1=identF, op0=ALU.mult, op1=ALU.mult, accum_out=col(2)), reads=[("bB",), ("cst",)], writes=[("junk",), ("sm", c % 2, 2)])
        fw.op("dve", lambda e: e.scalar_tensor_tensor(out=junk[:], in0=GB[:], scalar=1.0, in1=identF, op0=ALU.mult, op1=ALU.mult, accum_out=col(3)), reads=[("GB",), ("cst",)], writes=[("junk",), ("sm", c % 2, 3)])
        fw.op("dve", lambda e: e.tensor_tensor(out=col(4), in0=col(3), in1=col(1), op=ALU.subtract),
              reads=[("sm", c % 2, 3), ("sm", c % 2, 1)], writes=[("sm", c % 2, 4)])
        if c > 0:
            fw.op("dve", lambda e: e.tensor_tensor(out=col(5), in0=sp_[:, 1:2], in1=col(1), op=ALU.subtract),
                  reads=[("sm", (c + 1) % 2, 1), ("sm", c % 2, 1)], writes=[("sm", c % 2, 5)])
        fw.op("dve", lambda e: e.tensor_tensor(out=col(6), in0=bB[:, 127:128], in1=col(1), op=ALU.subtract),
              reads=[("bB",), ("sm", c % 2, 1)], writes=[("sm", c % 2, 6)])
        yield
        fw.op("pe", lambda e: e.matmul(psG[:, 0:128], kTb[:, ts], qTb[:, ts], start=True, stop=True),
              reads=[("kTb", bs), ("qTb", bs)], writes=[("pG",)], excl=GX)
        fw.op("act", lambda e: e.activation(out=DT[:], in_=bB[:], func=AF.Exp, bias=col(4), scale=1.0),
              reads=[("bB",), ("sm", c % 2, 4)], writes=[("DT",)])
        fw.op("dve", lambda e: e.tensor_tensor(out=DT[:], in0=DT[:], in1=causT, op=ALU.mult), reads=[("DT",), ("cst",)], writes=[("DT",)])
        fw.op("dve", lambda e: e.tensor_tensor(out=qkDT[:], in0=psG[:, 0:128], in1=DT[:], op=ALU.mult),
              reads=[("pG",), ("DT",)], writes=[("qkDT",)], excl=GX)
        yield
        if c > 0:
            fw.op("act", lambda e: e.activation(out=eBt[:], in_=bB[:], func=AF.Exp, bias=col(5), scale=1.0),
                  reads=[("bB",), ("sm", c % 2, 5)], writes=[("eB",)])
            fw.op("dve", lambda e: e.tensor_tensor(out=qsT[:], in0=qTb[:, ts], in1=eBt[:], op=ALU.mult),
                  reads=[("qTb", bs), ("eB",)], writes=[("qsT",)])
        fw.op("pe", lambda e: e.matmul(psG[:, 0:257], qkDT[:], vtm[:, cc, :], start=True, stop=(c == 0)),
              reads=[("qkDT",), ("vtm", bs, cc)], writes=[("pG",)], excl=GX)
        if c > 0:
            fw.op("pe", lambda e: e.matmul(psG[:, 0:257], qsT[:], Cab[:], start=False, stop=True),
                  reads=[("qsT",), ("Cab",)], writes=[("pG",)], excl=GX)
        fw.op("act", lambda e: e.activation(out=col(7), in_=col(1), func=AF.Exp, scale=-1.0),
              reads=[("sm", c % 2, 1)], writes=[("sm", c % 2, 7)])
        fw.op("act", lambda e: e.activation(out=col(10), in_=psG[:, 256:257], func=AF.Abs),
              reads=[("pG",)], writes=[("sm", c % 2, 10)], excl=GX)
        fw.op("dve", lambda e: e.tensor_tensor(out=col(10), in0=col(10), in1=col(7), op=ALU.max),
              reads=[("sm", c % 2, 10), ("sm", c % 2, 7)], writes=[("sm", c % 2, 10)])
        fw.op("dve", lambda e: e.reciprocal(out=col(11), in_=col(10)), reads=[("sm", c % 2, 10)], writes=[("sm", c % 2, 11)])
        fw.op("dve", lambda e: e.tensor_scalar(out=hh[:], in0=psG[:, 0:256], scalar1=col(11), scalar2=None, op0=ALU.mult),
              reads=[("pG",), ("sm", c % 2, 11)], writes=[("hh",)], excl=GX)
        yield
        fw.op("act", lambda e: e.activation(out=col(8), in_=col(3), func=AF.Exp, bias=col(6), scale=1.0),
              reads=[("sm", c % 2, 3), ("sm", c % 2, 6)], writes=[("sm", c % 2, 8)])
        h = c % 2
        tx = [("PS", "T%d" % h)]
        fw.op("pe", lambda e: e.transpose(out=psT[h][:, 0:128], in_=kTb[:, ts], identity=idb[:]),
              reads=[("kTb", bs), ("idb",)], writes=[("psT", h)], excl=tx)
        fw.op("dve", lambda e: e.tensor_scalar(out=kw[:], in0=psT[h][:, 0:128], scalar1=col(8), scalar2=None, op0=ALU.mult),
              reads=[("psT", h), ("sm", c % 2, 8)], writes=[("kw",)], excl=tx)
        fw.op("pe", lambda e: e.matmul(psG[:, 0:257], kw[:], vtm[:, cc, :], start=True, stop=True),
              reads=[("kw",), ("vtm", bs, cc)], writes=[("pG",)], excl=GX)
        if c == 0:
            fw.op("dve", lambda e: e.tensor_copy(out=Ca[:], in_=psG[:, 0:257]), reads=[("pG",)], writes=[("Ca",)], excl=GX)
        else:
            fw.op("act", lambda e: e.activation(out=col(9), in_=bB[:, 127:128], func=AF.Exp, bias=col(5), scale=1.0),
                  reads=[("bB",), ("sm", c % 2, 5)], writes=[("sm", c % 2, 9)])
            fw.op("dve", lambda e: e.scalar_tensor_tensor(out=Ca[:], in0=Ca[:], scalar=col(9), in1=psG[:, 0:257], op0=ALU.mult, op1=ALU.add),
                  reads=[("Ca",), ("sm", c % 2, 9), ("pG",)], writes=[("Ca",)], excl=GX)
        fw.op("act", lambda e: e.copy(out=Cab[:], in_=Ca[:]), reads=[("Ca",)], writes=[("Cab",)])
        yield
        fw.op("dve", lambda e: e.bn_stats(out=st6[:], in_=hh[:]), reads=[("hh",)], writes=[("st6",)])
        fw.op("dve", lambda e: e.bn_aggr(out=s[:, 12:14], in_=st6[:]), reads=[("st6",)], writes=[("sm", c % 2, 12)])
        fw.op("act", lambda e: e.activation(out=col(14), in_=col(13), func=AF.Sqrt, bias=1e-5, scale=1.0),
              reads=[("sm", c % 2, 12)], writes=[("sm", c % 2, 14)])
        fw.op("dve", lambda e: e.reciprocal(out=col(14), in_=col(14)), reads=[("sm", c % 2, 14)], writes=[("sm", c % 2, 14)])
        fw.op("dve", lambda e: e.scalar_tensor_tensor(out=col(15), in0=col(12), scalar=-1.0, in1=col(14), op0=ALU.mult, op1=ALU.mult),
              reads=[("sm", c % 2, 12), ("sm", c % 2, 14)], writes=[("sm", c % 2, 15)])
        fw.op("act", lambda e: e.activation(out=hh[:], in_=hh[:], func=AF.Identity, bias=col(15), scale=col(14)),
              reads=[("hh",), ("sm", c % 2, 14), ("sm", c % 2, 15)], writes=[("hh",)])
        fw.op("dve", lambda e: e.tensor_tensor(out=hh[:], in0=hh[:], in1=gsig[:, cc, :], op=ALU.mult),
              reads=[("hh",), ("gsig", bs, cc)], writes=[("hh",)])
        for e2 in range(2):
            fw.op("pe", lambda e, e2=e2: e.transpose(out=psG[:, e2 * 128:(e2 + 1) * 128], in_=hh[:, e2 * 128:(e2 + 1) * 128], identity=identF),
                  reads=[("hh",), ("cst",)], writes=[("pG",)], excl=GX)
        fw.op("act", lambda e: e.copy(out=oTa[:, 0:2, ts], in_=psG[:, 0:256].rearrange("p (a n) -> p a n", a=2)),
              reads=[("pG",)], writes=[("oTa", 0, cc)], excl=GX)
        yield

    sci = [0]
    pti = [0]
    VX = [("PS", "V")]

    def mla_tile(qt, h):
        qc = qt % NTB
        bs = (qt // NTB) % 2
        qn, qr = qn2[bs], qr2[bs]
        qs = slice(qc * 128, (qc + 1) * 128)
        n = (qt + 1) * 128
        done = 0
        while done < n:
            m = min(512, n - done)
            p = sci[0] % 2
            sci[0] += 1
            ex = [("PS", "S%d" % p)]
            blks = sorted(set(range(done // TB, (done + m - 1) // TB + 1)))
            fw.op("pe", lambda e, p=p, m=m, done=done: e.matmul(psS[p][:, 0:m], qn[h][:, qs], knT[h][:, done:done + m], start=True, stop=False),
                  reads=[("qn", bs, h)] + [("knT", h, b) for b in blks], writes=[("psS", p)], excl=ex)
            fw.op("pe", lambda e, p=p, m=m, done=done: e.matmul(psS[p][:, 0:m], qr[h][:, qs], krT[:, done:done + m], start=False, stop=True),
                  reads=[("qr", bs, h)] + [("krT", b) for b in blks], writes=[("psS", p)], excl=ex)
            last = (done + m == n)
            mm = m - 128 if last else m
            if mm > 0:
                fw.op("act", lambda e, p=p, mm=mm, done=done: e.copy(out=sc[:, done:done + mm], in_=psS[p][:, 0:mm]),
                      reads=[("psS", p)], writes=[("sc", done // 512)], excl=ex)
            if last:
                fw.op("dve", lambda e, p=p, mm=mm, done=done: e.tensor_tensor(out=sc[:, done + mm:done + mm + 128], in0=psS[p][:, mm:mm + 128],
                                                                               in1=causAdd, op=ALU.add),
                      reads=[("psS", p), ("cst",)], writes=[("sc", "d")], excl=ex)
            done += m
            yield
        SCK = [("sc", k) for k in range((n + 511) // 512)] + [("sc", "d")]
        fw.op("dve", lambda e: e.reduce_max(out=am[:, 0:1], in_=sc[:, 0:n], axis=AX.X), reads=SCK, writes=[("am", 0)])
        fw.op("dve", lambda e: e.tensor_scalar(out=am[:, 1:2], in0=am[:, 0:1], scalar1=-1.0, scalar2=None, op0=ALU.mult),
              reads=[("am", 0)], writes=[("am", 1)])
        fw.op("act", lambda e: e.activation(out=P[:, 0:n], in_=sc[:, 0:n], func=AF.Exp, bias=am[:, 1:2], scale=1.0, accum_out=am[:, 2:3]),
              reads=SCK + [("am", 1)], writes=[("P",), ("am", 2)])
        fw.op("dve", lambda e: e.reciprocal(out=am[:, 3:4], in_=am[:, 2:3]), reads=[("am", 2)], writes=[("am", 3)])
        yield
        ntile = qt + 1
        for g0 in range(0, ntile, 4):
            grp = list(range(g0, min(g0 + 4, ntile)))
            hb = pti[0] % 2
            pti[0] += 1
            tx = [("PS", "T%d" % hb)]
            for k, kt in enumerate(grp):
                fw.op("pe", lambda e, k=k, kt=kt, hb=hb: e.transpose(out=psT[hb][:, k * 128:(k + 1) * 128], in_=P[:, kt * 128:(kt + 1) * 128],
                                                                   identity=idb[:]),
                      reads=[("P",), ("idb",)], writes=[("psT", hb)], excl=tx)
            nn = len(grp) * 128
            if hb == 0:
                fw.op("act", lambda e, hb=hb, nn=nn: e.copy(out=PT[hb][:, 0:nn], in_=psT[hb][:, 0:nn]), reads=[("psT", hb)], writes=[("PT", hb)], excl=tx)
            else:
                fw.op("dve", lambda e, hb=hb, nn=nn: e.tensor_copy(out=PT[hb][:, 0:nn], in_=psT[hb][:, 0:nn]), reads=[("psT", hb)], writes=[("PT", hb)], excl=tx)
            for k, kt in enumerate(grp):
                fw.op("pe", lambda e, k=k, kt=kt, hb=hb: e.matmul(psV[:, 0:128], PT[hb][:, k * 128:(k + 1) * 128], vh[:, kt, h * 128:(h + 1) * 128],
                                                                start=(kt == 0), stop=(kt == ntile - 1)),
                      reads=[("PT", hb), ("vh", kt)], writes=[("pV", 0)], excl=VX)
            yield
        fw.op("dve", lambda e: e.tensor_scalar(out=od[:], in0=psV[:, 0:128], scalar1=am[:, 3:4], scalar2=None, op0=ALU.mult),
              reads=[("pV", 0), ("am", 3)], writes=[("od",)], excl=VX)
        fw.op("pe", lambda e: e.transpose(out=psV[:, 128:256], in_=od[:], identity=identF), reads=[("od",), ("cst",)], writes=[("pV", 1)], excl=VX)
        fw.op("act", lambda e: e.copy(out=oTa[:, 2 + h, qs], in_=psV[:, 128:256]), reads=[("pV", 1)], writes=[("oTa", 1 + h, qc)], excl=VX)

    def pre(tb):
        bs = tb % 2
        rope_tables(tb)
        yield
        yield from inproj(tb)
        yield from conv_silu(ubq, "ubq", 0, 8, qTb2[bs], ("qTb", bs), 1.0)
        yield from conv_silu(ubk, "ubk", 4, 9, kTb2[bs], ("kTb", bs), 128.0 ** -0.5)
        yield from mla_proj(tb)

    def mixers(tb):
        for cc in range(NTB):
            c = tb * NTB + cc
            yield from mlstm_chunk(c)
            for h in range(2):
                yield from mla_tile(c, h)

    for _ in pre(0):
        pass
    load_x(1)
    load_pos(1)
    for tb in range(NBO):
        bg = pre(tb + 1) if tb + 1 < NBO else iter(())
        bg_live = [tb + 1 < NBO]

        def bg_step(tb=tb):
            if not bg_live[0]:
                return
            try:
                next(bg)
            except StopIteration:
                bg_live[0] = False
                if tb + 2 < NBO:
                    load_x(tb + 2)
                    load_pos(tb + 2)
                elif tail_prefetch is not None:
                    tail_prefetch(WINK + [("wuq",), ("wukv",)])

        n = 0
        nfg = sum(5 + 2 * (2 * ((tb * NTB + cc) // 4 + 1) + 1) for cc in range(NTB))
        step = max(1, int(0.75 * nfg / 26))
        for _ in mixers(tb):
            n += 1
            if n % step == 0:
                bg_step()
        while bg_live[0]:
            bg_step()
        fw.dma("sp", o_dst(tb * TB, TB), oTa[:],
               reads=[("oTa", a, q) for a in range(3) for q in range(NTB)], writes=[("odram", tb * TB)], is_output=True)
        if post is not None:
            post(tb * TB, TB)


def mo_consts():
    import ml_dtypes
    j = np.arange(128)[:, None]
    i = np.arange(128)[None, :]
    cst = np.zeros((128, 4, 128), np.float32)
    cst[:, 0, :] = np.eye(128)
    cst[:, 1, :] = np.where(j <= i, 1.0, 0.0)
    cst[:, 2, :] = np.where(i <= j, 0.0, NEG)
    cst[:, 3, :] = 1.0
    return cst, np.eye(128, dtype=np.float32).astype(ml_dtypes.bfloat16)


def mo_cols(conv_w, conv_b, bi, bf, gq, gkv, h):
    cols = np.zeros((128, 32), np.float32)
    cols[:, 0:4] = conv_w[:, h * 128:(h + 1) * 128].T
    cols[:, 4:8] = conv_w[:, 512 + h * 128:512 + (h + 1) * 128].T
    cols[:, 8] = conv_b[h * 128:(h + 1) * 128]
    cols[:, 9] = conv_b[512 + h * 128:512 + (h + 1) * 128]
    cols[:, 10] = bi[h]
    cols[:, 11] = -bf[h]
    invf = (10000.0 ** (-np.arange(0, 64, 2, dtype=np.float32) / 64)).astype(np.float32)
    cols[0:64, 12] = np.concatenate([invf, invf])
    cols[0:64, 13] = math.pi / 2
    cols[0:32, 14] = math.pi
    cols[32:64, 14] = 0.0
    cols[:, 16:20] = gq.reshape(4, 128).T
    cols[:, 20:24] = gkv.reshape(4, 128).T
    return cols


def mo_win_layout(w_in, h):
    def cols(a, n):
        return w_in[:, a:a + n]
    cq = cols(h * 128, 128)
    ck = cols(512 + h * 128, 128)
    cv = cols(1024 + h * 256, 256)
    ci = np.repeat(cols(2048 + h, 1), 128, axis=1)
    cf = np.repeat(cols(2052 + h, 1), 128, axis=1)
    co = cols(2056 + h * 256, 256)
    mq = cols(3080, 512)
    mkv = cols(3592, 512)
    kr = cols(4104, 64)
    krB = np.concatenate([kr[:, 32:], kr[:, :32]], axis=1)
    w = np.concatenate([cq, ck, ci, cf, mq, mkv, kr, krB, cv, co], axis=1)
    assert w.shape[1] == MO_NC
    return np.ascontiguousarray(w.reshape(16, 128, MO_NC).transpose(1, 0, 2).reshape(128, 16 * MO_NC))


def mo_wu_layout(wuq, wukv, h):
    qs = []
    for hm in (2 * h, 2 * h + 1):
        blk = wuq[:, hm * 192:(hm + 1) * 192]
        nope, rope = blk[:, :128], blk[:, 128:]
        qs += [nope, rope, np.concatenate([rope[:, 32:], rope[:, :32]], axis=1)]
    q = np.concatenate(qs, axis=1)
    kn = [wukv[:, hm * 256:hm * 256 + 128] for hm in (2 * h, 2 * h + 1)]
    vv = [wukv[:, hm * 256 + 128:hm * 256 + 256] for hm in (2 * h, 2 * h + 1)]
    kv = np.concatenate(kn + vv, axis=1)
    lay = lambda w: np.ascontiguousarray(w.reshape(4, 128, 512).transpose(1, 0, 2).reshape(128, 4 * 512))
    return lay(q), lay(kv)


def build_MO(x_dt=F32):
    nc = bass.Bass("TRN2", target_bir_lowering=False)
    xT_d = nc.dram_tensor("xT", [D, S], x_dt, kind="ExternalInput").ap()
    win_d = nc.dram_tensor("win", [128, 16 * MO_NC], F32, kind="ExternalInput").ap()
    wuq_d = nc.dram_tensor("wuq", [128, 4 * 512], F32, kind="ExternalInput").ap()
    wukv_d = nc.dram_tensor("wukv", [128, 4 * 512], F32, kind="ExternalInput").ap()
    cols_d = nc.dram_tensor("cols", [128, 32], F32, kind="ExternalInput").ap()
    ng_d = nc.dram_tensor("ng", [256], F32, kind="ExternalInput").ap()
    pos_d = nc.dram_tensor("pos", [S], I32, kind="ExternalInput").ap()
    cst_d = nc.dram_tensor("cst", [128, 4, 128], F32, kind="ExternalInput").ap()
    idb_d = nc.dram_tensor("idb", [128, 128], BF16, kind="ExternalInput").ap()
    oT_d = nc.dram_tensor("oT", [512, S], BF16, kind="ExternalOutput").ap()
    fw = FW(nc)
    emit_MO(fw, lambda t0, n: [(0, 16, tl, tl + 256, xT_d[:, t0 + tl:t0 + tl + 256].rearrange("(c p) t -> p c t", p=128)) for tl in range(0, n, 256)], win_d, wuq_d, wukv_d, cols_d, ng_d, pos_d, cst_d, idb_d,
            lambda t0, n: oT_d[:, t0:t0 + n].rearrange("(a p) t -> p a t", p=128))
    fw.emit()
    fw.close()
    return nc


GROUPS = [[0, 1, 2, 3], [4, 5, 6, 7]]


def build_fused(nl=4, stop_after_mixer=False):
    nc = bass.Bass("TRN2", target_bir_lowering=False)

    def EI(name, shape, dt):
        return nc.dram_tensor(name, list(shape), dt, kind="ExternalInput").ap()

    x0T_d = EI("x0T", [D, S], F32)
    xres0_d = EI("xres0", [TOK, D], F32)
    pos_d = EI("pos", [S], I32)
    sel_d = EI("sel", [128, 4], F32)
    cstE_d = EI("cstE", [128, 4, 128], F32)
    maskE_d = EI("maskE", [128, DIL_TOT * 128], BF16)
    idb_d = EI("idb", [128, 128], BF16)
    cstO_d = EI("cstO", [128, 4, 128], F32)
    ident_d = EI("ident", [128, 128], F32)
    W = {}
    for j in range((nl + 1) // 2):
        W["winE", j] = EI("winE%d" % j, [128, 16 * ME_NC], F32)
        W["wg2E", j] = EI("wg2E%d" % j, [17, 128], F32)
        W["ngE", j] = EI("ngE%d" % j, [256], F32)
    for j in range(nl // 2):
        W["winO", j] = EI("winO%d" % j, [128, 16 * MO_NC], F32)
        W["wuqO", j] = EI("wuqO%d" % j, [128, 4 * 512], F32)
        W["wukvO", j] = EI("wukvO%d" % j, [128, 4 * 512], F32)
        W["colsO", j] = EI("colsO%d" % j, [128, 32], F32)
        W["ngO", j] = EI("ngO%d" % j, [256], F32)
    for l in range(nl):
        F = 1536 if l % 2 == 0 else 2048
        W["wo", l] = EI("wo%d" % l, [F, D], F32)
        W["wgu", l] = EI("wgu%d" % l, [NJ, 128, 16 * 256], F32)
        W["wd", l] = EI("wd%d" % l, [HID, D], F32)
        W["ln", l] = EI("ln%d" % l, [4, D], F32)
    xo_d = nc.dram_tensor("xo", [TOK, D], F32, kind="ExternalOutput").ap()
    FH = {0: 384, 1: 512}
    oTx = {o: [nc.dram_tensor("oTx%d_%d" % (o, c), [FH[o], TOK], BF16) for c in range(4)] for o in (0, 1)}
    oTg = {o: [nc.dram_tensor("oTg%d_%d" % (o, c), [4 * FH[o], TOK], BF16) for c in range(4)] for o in (0, 1)}
    xoT = [nc.dram_tensor("xoT_%d" % c, [128, 16 * 256], BF16) for c in range(4)]
    xTg = [[nc.dram_tensor("xTg%d_%d" % (i, c), [4 * 128, 16 * 256], BF16) for c in range(4)] for i in range(2)]
    xbuf = nc.dram_tensor("xbuf", [TOK, D], F32)

    fw = FW(nc)
    WB = fw.sb("WB", [128, 16 * MO_NC], BF16)
    for l in range(nl):
        j = l // 2
        odd = l % 2
        F = 2048 if odd else 1536
        fw.begin_phase()
        if l == 0:
            x_src = lambda t0, n: [(0, 16, tl, tl + 256, x0T_d[:, t0 + tl:t0 + tl + 256].rearrange("(c p) t -> p c t", p=128))
                                   for tl in range(0, n, 256)]
        else:
            gbs = xTg[(l - 1) % 2]
            x_src = lambda t0, n, gbs=gbs, l=l: [
                (0, 16, tl, tl + 256, gbs[((t0 + tl) % TOK) // 256].ap()[((t0 + tl) // TOK) * 128:((t0 + tl) // TOK + 1) * 128, :].rearrange(
                    "p (c t) -> p c t", c=16), ("xTg", (l - 1) % 2, ((t0 + tl) % TOK) // 256)) for tl in range(0, n, 256)]
        def post_mix(t0, n, odd=odd):
            if (t0 + n) % TOK == 0:
                c = t0 // TOK
                fw.collective("AllGather", GROUPS, oTx[odd][c], oTg[odd][c], reads=[("odram", tt) for tt in range(c * TOK, (c + 1) * TOK, n)],
                              writes=[("oTg", odd, c)])

        o_dst = lambda t0, n, odd=odd: oTx[odd][t0 // TOK].ap()[:, (t0 % TOK):(t0 % TOK) + n].rearrange("(a p) t -> p a t", p=128)
        wo_l = W["wo", l]
        KCl = F // 128

        def tail_prefetch(keys, wo_l=wo_l, KCl=KCl):
            for cb in range(2):
                dst = WB[:, 16384 + cb * 8192:16384 + (cb + 1) * 8192].rearrange("p (c n) -> p c n", c=16)
                fw.dma("pool", dst[:, 0:KCl, :], wo_l[:, cb * 512:(cb + 1) * 512].rearrange("(c p) n -> p c n", p=128), writes=list(keys))

        NCl = MO_NC if odd else ME_NC
        win_view = WB[:, 0:16 * NCl].rearrange("p (c n) -> p c n", c=16)
        pre_l = (l > 0)
        if not odd:
            emit_ME(fw, x_src, W["winE", j], W["wg2E", j], W["ngE", j], cstE_d, maskE_d, idb_d, o_dst, pfx="E%d" % l, post=post_mix,
                    win_ext=win_view, win_preloaded=pre_l, tail_prefetch=tail_prefetch)
        else:
            emit_MO(fw, x_src, W["winO", j], W["wuqO", j], W["wukvO", j], W["colsO", j], W["ngO", j], pos_d, cstO_d, idb_d,
                    o_dst, pfx="O%d" % l, post=post_mix, win_ext=win_view, win_preloaded=pre_l, tail_prefetch=tail_prefetch)
        fw.end_phase(wait_cc=False)
        fw.begin_phase()
        ogs = [t.ap() for t in oTg[odd]]

        def loader(fw_, oT, KC, tmp, ogs=ogs, l=l):
            selt = fw_.sb("selt%d" % l, [128, 4], F32)
            fw_.dma("sp", selt[:], sel_d, writes=[("selt",)])
            i = 0
            for r in range(4):
                for k in range(KC):
                    tb = tmp[i % len(tmp)]
                    key = ("tmp", i % len(tmp))
                    i += 1
                    fw_.dma("sp", tb, ogs[r][k * 128:(k + 1) * 128, :], reads=[("oTg", l % 2, r)], writes=[key])
                    if r == 0:
                        fw_.op("dve", lambda e, tb=tb, k=k: e.tensor_scalar(out=oT[:, k, :], in0=tb, scalar1=selt[:, 0:1], scalar2=None,
                                                                            op0=ALU.mult),
                               reads=[key, ("selt",)], writes=[("oT", k)])
                    else:
                        fw_.op("dve", lambda e, tb=tb, k=k, r=r: e.scalar_tensor_tensor(out=oT[:, k, :], in0=tb, scalar=selt[:, r:r + 1],
                                                                                       in1=oT[:, k, :], op0=ALU.mult, op1=ALU.add),
                               reads=[key, ("selt",), ("oT", k)], writes=[("oT", k)])

        last = (l == nl - 1)

        def prefetch_next(keys, l=l):
            nodd = (l + 1) % 2
            nj = (l + 1) // 2
            NCn = MO_NC if nodd else ME_NC
            wsrc = W["winO", nj] if nodd else W["winE", nj]
            wv = WB[:, 0:16 * NCn].rearrange("p (c n) -> p c n", c=16)
            for c4 in range(4):
                fw.dma("pool", wv[:, c4 * 4:(c4 + 1) * 4, :].rearrange("p c n -> p (c n)"),
                       wsrc[:, c4 * 4 * NCn:(c4 + 1) * 4 * NCn], writes=list(keys))

        def post_T(c, l=l):
            fw.collective("AllGather", GROUPS, xoT[c], xTg[l % 2][c], reads=[("xodram", c)], writes=[("xTg", l % 2, c)])
        emit_T(fw, F, loader, xres0_d if l == 0 else xbuf.ap(), W["wo", l], W["wgu", l], W["wd", l], W["ln", l], ident_d,
               xo_d if last else xbuf.ap(),
               None if last else (lambda c: xoT[c].ap()),
               BF16, pfx="T%d" % l, post=None if last else post_T, U_ext=WB, wo_preloaded=True,
               prefetch=None if last else prefetch_next)
        fw.end_phase(wait_cc=last)
    if nl % 2 == 1 and nl < 4:
        pass
    fw.close()
    return nc


def lay_wgu(wgu):
    g = wgu[:, :HID].reshape(16, 128, NJ, 128)
    u = wgu[:, HID:].reshape(16, 128, NJ, 128)
    gu = np.concatenate([g, u], axis=-1)
    return np.ascontiguousarray(gu.transpose(2, 1, 0, 3).reshape(NJ, 128, 16 * 256))


def wo_gathered(w_o, odd):
    rows = []
    for q in range(4):
        rows.append(w_o[q * 256:(q + 1) * 256])
        if odd:
            rows.append(w_o[1024 + q * 256:1024 + (q + 1) * 256])
        else:
            rows.append(w_o[1024 + q * 128:1024 + (q + 1) * 128])
    return np.ascontiguousarray(np.concatenate(rows, 0)).astype(np.float32)


_NC = {}
NL = 4


def kernel(**inputs):
    inp = {k: np.asarray(v) for k, v in inputs.items()}
    x = np.ascontiguousarray(inp["x"]).astype(np.float32)
    pos = inp["positions"].astype(np.int32)
    cstE, maskE, idb = me_consts()
    cstO, _ = mo_consts()
    shared = dict(cstE=cstE, maskE=maskE, idb=idb, cstO=cstO, ident=np.eye(128, dtype=np.float32))
    for l in range(NL):
        j = l // 2
        odd = l % 2
        shared["wo%d" % l] = wo_gathered(inp["odd_w_o"][j] if odd else inp["even_w_o"][j], odd)
        shared["wgu%d" % l] = lay_wgu(inp["ffn_wgu"][l])
        shared["wd%d" % l] = np.ascontiguousarray(inp["ffn_wd"][l]).astype(np.float32)
        shared["ln%d" % l] = np.stack([inp["ln1_g"][l], inp["ln1_b"][l], inp["ln2_g"][l], inp["ln2_b"][l]]).astype(np.float32)
    xT = [np.ascontiguousarray(x[b].T) for b in range(2)]
    in_maps = []
    for c in range(8):
        b, h = c // 4, c % 4
        m = dict(shared)
        m["x0T"] = xT[b]
        m["xres0"] = np.ascontiguousarray(x[b, h * TOK:(h + 1) * TOK])
        m["pos"] = np.ascontiguousarray(pos[b])
        sel = np.zeros((128, 4), np.float32)
        sel[:, h] = 1.0
        m["sel"] = sel
        for j in range((NL + 1) // 2):
            wg2 = np.concatenate([inp["even_gla_wg2"][j][:, h * 128:(h + 1) * 128],
                                  inp["even_gla_bg"][j][None, h * 128:(h + 1) * 128]], 0).astype(np.float32)
            m["winE%d" % j] = me_win_layout(inp["even_w_in"][j], h)
            m["wg2E%d" % j] = np.ascontiguousarray(wg2)
            m["ngE%d" % j] = np.ascontiguousarray(inp["even_gla_norm_g"][j]).astype(np.float32)
            if j >= NL // 2:
                continue
            wq, wkv = mo_wu_layout(inp["odd_mla_wuq"][j], inp["odd_mla_wukv"][j], h)
            m["winO%d" % j] = mo_win_layout(inp["odd_w_in"][j], h)
            m["wuqO%d" % j] = wq
            m["wukvO%d" % j] = wkv
            m["colsO%d" % j] = mo_cols(inp["odd_conv_w"][j], inp["odd_conv_b"][j], inp["odd_mlstm_bi"][j], inp["odd_mlstm_bf"][j],
                                       inp["odd_mla_qnorm_g"][j], inp["odd_mla_kvnorm_g"][j], h)
            m["ngO%d" % j] = np.ascontiguousarray(inp["odd_mlstm_norm_g"][j]).astype(np.float32)
        in_maps.append(m)
    if "nc" not in _NC:
        _NC["nc"] = build_fused(NL)
    res = run_bass_kernel_spmd(_NC["nc"], in_maps, core_ids=list(range(8)))
    out = np.empty_like(x)
    for c in range(8):
        b, h = c // 4, c % 4
        out[b, h * TOK:(h + 1) * TOK] = res.results[c]["xo"]
    return out
```

```python
import math
import ml_dtypes
from concourse.bass_utils import run_bass_kernel_spmd


import numpy as np
import concourse.bass as bass
import concourse.mybir as mybir
from contextlib import ExitStack

F32 = mybir.dt.float32
BF16 = mybir.dt.bfloat16
I32 = mybir.dt.int32
AF = mybir.ActivationFunctionType
ALU = mybir.AluOpType
AX = mybir.AxisListType

ENGS = ("pe", "act", "dve", "pool", "sp")
NDS = 16


class FW:
    def __init__(self, nc):
        self.nc = nc
        self.es = ExitStack()
        self.scope = self.es
        self.ncc = 0
        self.nphase = 0
        self.q = {e: [] for e in ENGS}
        self.cnt = {e: 0 for e in ENGS}
        self.dcnt = {e: 0 for e in ENGS}
        self.known = {e: {} for e in ENGS}
        self.lastw = {}
        self.readers = {}
        self.lastx = {}
        self.sems = {}
        self.out_deps = []
        self.n_wait = 0

    def sb(self, name, shape, dt):
        return self.scope.enter_context(self.nc.sbuf_tensor(name, list(shape), dt))

    def ps(self, name, shape, dt=F32):
        return self.scope.enter_context(self.nc.psum_tensor(name, list(shape), dt))

    def _sem(self, key):
        if key not in self.sems:
            self.sems[key] = self.es.enter_context(self.nc.semaphore("s_" + "_".join(map(str, key))))
        return self.sems[key]

    def _deps(self, eng, reads, writes, excl=()):
        deps = {}

        def add(d):
            if d is None:
                return
            sk, v, e = d
            if deps.get(sk, (0,))[0] < v:
                deps[sk] = (v, e)

        for k in reads:
            add(self.lastw.get(k))
        for k in writes:
            add(self.lastw.get(k))
            for d in self.readers.get(k, ()):
                add(d)
        for k in excl:
            d = self.lastx.get(k)
            if d is not None and d[2] != eng:
                add(d)
        out = []
        for sk, (v, e) in deps.items():
            if e == eng and eng == "pe":
                continue
            if self.known[eng].get(sk, 0) >= v:
                continue
            self.known[eng][sk] = v
            out.append((sk, v))
        return out

    def _commit(self, me, reads, writes):
        for k in writes:
            self.lastw[k] = me
            self.readers[k] = []
        for k in reads:
            self.readers.setdefault(k, []).append(me)

    def op(self, eng, fn, reads=(), writes=(), excl=()):
        waits = self._deps(eng, reads, writes, excl)
        self.cnt[eng] += 1
        me = (("e", eng), self.cnt[eng], eng)
        for k in excl:
            self.lastx[k] = me
        self.q[eng].append((waits, fn, ("e", eng), 1))
        self.n_wait += len(waits)
        self._commit(me, reads, writes)
        return me

    def dma(self, queue, out, in_, reads=(), writes=(), is_output=False):
        i = self.dcnt[queue]
        self.dcnt[queue] += 1
        sk = ("d", queue, i % NDS)
        val = 16 * (i // NDS + 1)
        waits = self._deps(queue, reads, writes)
        if i // NDS > 0 and self.known[queue].get(sk, 0) < val - 16:
            waits.append((sk, val - 16))
            self.known[queue][sk] = val - 16
        fn = lambda e, out=out, in_=in_: e.dma_start(out=out, in_=in_)
        self.q[queue].append((waits, fn, sk, 16))
        me = (sk, val, "dma")
        self._commit(me, reads, writes)
        if is_output:
            self.out_deps.append(me)
        return me

    def collective(self, kind, groups, src, dst, reads=(), writes=()):
        waits = self._deps("pool", reads, ())
        self.ncc += 1
        for k in writes:
            self.lastw[k] = (("cc",), self.ncc, "cc")
            self.readers[k] = []
        fn = lambda e: e.collective_compute(kind, mybir.AluOpType.bypass, replica_groups=groups,
                                            ins=[src.ap().opt()], outs=[dst.ap().opt()])
        self.q["pool"].append((waits, fn, ("cc",), 1))

    def begin_phase(self):
        self.scope = ExitStack()

    def end_phase(self, collective=None, wait_cc=True):
        nc = self.nc
        fin = []
        for q in ("sp", "pool"):
            for k in range(min(NDS, self.dcnt[q])):
                n_k = (self.dcnt[q] - k + NDS - 1) // NDS
                fin.append((("d", q, k), 16 * n_k))
        eng_obj = {"pe": "tensor", "act": "scalar", "dve": "vector", "pool": "gpsimd", "sp": "sync"}
        for e in ENGS:
            self._sem(("e", e))
        for (waits, fn, sk, inc) in [x for e in ENGS for x in self.q[e]]:
            self._sem(sk)
            for w in waits:
                self._sem(w[0])
        for f in fin:
            self._sem(f[0])
        colls = [] if collective is None else (collective if isinstance(collective, list) else [collective])
        ccsem = self._sem(("cc",))
        self.ncc += len(colls)
        cc_total = self.ncc
        self.nphase += 1
        with nc.Block("ph%d" % self.nphase) as block:
            for e in ENGS:
                items = self.q[e]

                def body(engine, items=items, e=e):
                    for (waits, fn, sk, inc) in items:
                        for (wsk, wv) in waits:
                            engine.wait_ge(self.sems[wsk], wv)
                        ins = fn(engine)
                        ins.then_inc(self.sems[sk], inc)
                    if e in ("pool", "sp"):
                        for (wsk, wv) in fin:
                            engine.wait_ge(self.sems[wsk], wv)
                    if e == "pool":
                        for (kind, groups, src, dst) in colls:
                            engine.collective_compute(kind, mybir.AluOpType.bypass, replica_groups=groups,
                                                      ins=[src.ap().opt()], outs=[dst.ap().opt()]).then_inc(ccsem, 1)
                        if cc_total > 0 and wait_cc:
                            engine.wait_ge(ccsem, cc_total)

                getattr(block, eng_obj[e])(body)
        self.q = {e: [] for e in ENGS}
        self.lastw = {} if wait_cc else {k: v for k, v in self.lastw.items() if v[2] == "cc"}
        self.readers = {}
        self.lastx = {}
        self.out_deps = []
        if self.scope is not self.es:
            self.scope.close()
            self.scope = self.es

    def emit(self):
        self.end_phase()

    def close(self):
        self.es.close()


D = 2048
HID = 5632
NJ = HID // 128
ALPHA = (2.0 * 4) ** 0.25
TOK = 1024
NT = TOK // 128


def build_T(F, xT_out_dt=F32):
    KC = F // 128
    nc = bass.Bass("TRN2", target_bir_lowering=False)
    oT_d = nc.dram_tensor("oT", [F, TOK], BF16, kind="ExternalInput").ap()
    xres_d = nc.dram_tensor("xres", [TOK, D], F32, kind="ExternalInput").ap()
    wo_d = nc.dram_tensor("w_o", [F, D], F32, kind="ExternalInput").ap()
    wgu_d = nc.dram_tensor("wgu", [NJ, 128, 16 * 256], F32, kind="ExternalInput").ap()
    wd_d = nc.dram_tensor("wd", [HID, D], F32, kind="ExternalInput").ap()
    ln_d = nc.dram_tensor("ln", [4, D], F32, kind="ExternalInput").ap()
    id_d = nc.dram_tensor("ident", [128, 128], F32, kind="ExternalInput").ap()
    xo_d = nc.dram_tensor("xo", [TOK, D], F32, kind="ExternalOutput").ap()
    xoT_d = nc.dram_tensor("xoT", [D, TOK], xT_out_dt, kind="ExternalOutput").ap()

    fw = FW(nc)
    emit_T(fw, F, lambda fw_, oT, KC_, tmp: fw_.dma("sp", oT[:, 0:KC_, :], oT_d.rearrange("(c p) t -> p c t", p=128), writes=[("oT",)]),
           xres_d, wo_d, wgu_d, wd_d, ln_d, id_d, xo_d,
           None, xT_out_dt)
    fw.emit()
    fw.close()
    return nc


def emit_T(fw, F, oT_loader, xres_d, wo_d, wgu_d, wd_d, ln_d, id_d, xo_d, xoT_d, xT_out_dt, pfx="T", post=None, U_ext=None, wo_preloaded=False, prefetch=None):
    nc = fw.nc
    KC = F // 128
    z = fw.sb(pfx + "z", [128, NT, D], F32)
    x1T = fw.sb(pfx + "x1T", [128, 16, TOK], BF16)
    gb = [fw.sb(pfx + "g", [128, D], F32), fw.sb(pfx + "b", [128, D], F32)]
    ident = fw.sb(pfx + "id", [128, 128], F32)
    stats = fw.sb(pfx + "st", [128, 4 * 6], F32)
    mv = fw.sb(pfx + "mv", [128, 2], F32)
    rstd = fw.sb(pfx + "rstd", [128, 1], F32)
    nmr = fw.sb(pfx + "nmr", [128, 1], F32)
    psb = [fw.ps(pfx + "ps%d" % i, [128, 512], F32) for i in range(8)]
    U = U_ext if U_ext is not None else fw.sb(pfx + "U", [128, 32768], BF16)
    oT = U[:, 0:16384].rearrange("p (c t) -> p c t", c=16)
    wob = [U[:, 16384 + i * 8192:16384 + (i + 1) * 8192].rearrange("p (c n) -> p c n", c=16) for i in range(2)]
    WG = 3
    wgb = [U[:, i * 4096:(i + 1) * 4096].rearrange("p (c n) -> p c n", c=16) for i in range(WG)]
    WD = 8
    wdb = [U[:, 12288 + i * 2048:12288 + (i + 1) * 2048] for i in range(WD)]
    hT = [fw.sb(pfx + "hT%d" % i, [128, 4, TOK], BF16) for i in range(2)]
    P1KEYS = [("oT",), ("wo", 0), ("wo", 1)]
    sg = [fw.sb(pfx + "sg%d" % i, [128, 512], F32) for i in range(2)]
    xt_st = [fw.sb(pfx + "xts%d" % i, [128, 4, 128], xT_out_dt) for i in range(2)]

    zk = lambda t: [("z", t, cb) for cb in range(4)]

    fw.dma("sp", ident[:], id_d, writes=[("id",)])
    for t in range(NT):
        fw.dma("sp", z[:, t, :], xres_d[t * 128:(t + 1) * 128, :], writes=zk(t))
    oT_loader(fw, oT, KC, [x1T[:, i, :] for i in range(16)])

    def load_gb(i):
        for k in range(2):
            fw.dma("sp", gb[k][:], ln_d[2 * i + k, :].partition_broadcast(128), writes=[("gb", k)])

    load_gb(0)

    def load_wo(cb):
        buf = wob[cb % 2]
        fw.dma("pool", buf[:, 0:KC, :], wo_d[:, cb * 512:(cb + 1) * 512].rearrange("(c p) n -> p c n", p=128),
               writes=[("wo", cb % 2)])

    def load_wg(j):
        fw.dma("pool", wgb[j % WG].rearrange("p c n -> p (c n)"), wgu_d[j],
               writes=[("wg", j % WG)] + (P1KEYS if j < WG else []))

    def load_wd(j):
        fw.dma("pool", wdb[j % WD], wd_d[j * 128:(j + 1) * 128, :],
               writes=[("wd", j % WD)] + (P1KEYS if j < WD else []))

    if not wo_preloaded:
        load_wo(0)
        load_wo(1)

    pi = [0]

    def nextps():
        p = pi[0] % 8
        pi[0] += 1
        return p

    for cb in range(4):
        buf = wob[cb % 2]
        for t in range(NT):
            p = nextps()
            for k in range(KC):
                fw.op("pe", lambda e, p=p, k=k, t=t, buf=buf: e.matmul(
                    psb[p][:], oT[:, k, t * 128:(t + 1) * 128], buf[:, k, :], start=(k == 0), stop=(k == KC - 1)),
                    reads=[("oT",), ("oT", k), ("wo", cb % 2)], writes=[("ps", p)])
            fw.op("dve", lambda e, p=p, t=t, cb=cb: e.scalar_tensor_tensor(
                out=z[:, t, cb * 512:(cb + 1) * 512], in0=z[:, t, cb * 512:(cb + 1) * 512], scalar=ALPHA,
                in1=psb[p][:], op0=ALU.mult, op1=ALU.add),
                reads=[("ps", p), ("z", t, cb)], writes=[("z", t, cb)])
        if cb + 2 < 4:
            load_wo(cb + 2)

    def layer_norm(t):
        for s in range(4):
            fw.op("dve", lambda e, t=t, s=s: e.bn_stats(out=stats[:, s * 6:(s + 1) * 6], in_=z[:, t, s * 512:(s + 1) * 512]),
                  reads=[("z", t, s)], writes=[("st", s)])
        fw.op("dve", lambda e: e.bn_aggr(out=mv[:], in_=stats[:]), reads=[("st", s) for s in range(4)], writes=[("mv",)])
        fw.op("act", lambda e: e.activation(out=rstd[:], in_=mv[:, 1:2], func=AF.Sqrt, bias=1e-5, scale=1.0),
              reads=[("mv",)], writes=[("rstd",)])
        fw.op("dve", lambda e: e.reciprocal(out=rstd[:], in_=rstd[:]), reads=[("rstd",)], writes=[("rstd",)])
        fw.op("dve", lambda e: e.scalar_tensor_tensor(out=nmr[:], in0=mv[:, 0:1], scalar=-1.0, in1=rstd[:],
                                                       op0=ALU.mult, op1=ALU.mult),
              reads=[("mv",), ("rstd",)], writes=[("nmr",)])
        fw.op("act", lambda e, t=t: e.activation(out=z[:, t, :], in_=z[:, t, :], func=AF.Identity, bias=nmr[:], scale=rstd[:]),
              reads=zk(t) + [("rstd",), ("nmr",)], writes=zk(t))
        fw.op("dve", lambda e, t=t: e.tensor_tensor(out=z[:, t, :], in0=z[:, t, :], in1=gb[0][:], op=ALU.mult),
              reads=zk(t) + [("gb", 0)], writes=zk(t))
        fw.op("dve", lambda e, t=t: e.tensor_tensor(out=z[:, t, :], in0=z[:, t, :], in1=gb[1][:], op=ALU.add),
              reads=zk(t) + [("gb", 1)], writes=zk(t))

    def transpose_tile(t, dst_fn):
        for cg in range(4):
            p = nextps()
            for cc in range(4):
                c = cg * 4 + cc
                fw.op("pe", lambda e, p=p, cc=cc, c=c, t=t: e.transpose(
                    out=psb[p][:, cc * 128:(cc + 1) * 128], in_=z[:, t, c * 128:(c + 1) * 128], identity=ident[:]),
                    reads=[("z", t, c // 4), ("id",)], writes=[("ps", p)])
            dst_fn(cg, p)

    for t in range(NT):
        layer_norm(t)

        def dst(cg, p, t=t):
            fw.op("act", lambda e: e.copy(out=x1T[:, cg * 4:(cg + 1) * 4, t * 128:(t + 1) * 128],
                                          in_=psb[p][:].rearrange("p (c n) -> p c n", c=4)),
                  reads=[("ps", p)], writes=[("x1T", t, cg)])
        transpose_tile(t, dst)
    load_gb(1)
    for j in range(WG):
        load_wg(j)
    for j in range(WD):
        load_wd(j)

    def up_gate(j):
        buf = wgb[j % WG]
        hb = hT[(j // 4) % 2]
        for half in range(2):
            pg = nextps()
            pu = nextps()
            for which, p in ((0, pg), (1, pu)):
                for c in range(16):
                    fw.op("pe", lambda e, p=p, c=c, which=which, half=half, buf=buf: e.matmul(
                        psb[p][:], buf[:, c, which * 128:(which + 1) * 128], x1T[:, c, half * 512:(half + 1) * 512],
                        start=(c == 0), stop=(c == 15)),
                        reads=[("wg", j % WG)] + [("x1T", t, c // 4) for t in range(half * 4, half * 4 + 4)],
                        writes=[("ps", p)])
            s = sg[half]
            fw.op("act", lambda e, s=s, pg=pg: e.activation(out=s[:], in_=psb[pg][:], func=AF.Silu),
                  reads=[("ps", pg)], writes=[("sg", half)])
            fw.op("dve", lambda e, s=s, pu=pu, hb=hb, half=half, j=j: e.tensor_tensor(
                out=hb[:, j % 4, half * 512:(half + 1) * 512], in0=s[:], in1=psb[pu][:], op=ALU.mult),
                reads=[("sg", half), ("ps", pu)], writes=[("hT", (j // 4) % 2, j % 4, half)])
        if j + WG < NJ:
            load_wg(j + WG)

    def down(G):
        hb = hT[G % 2]
        for t in range(NT):
            for cb in range(4):
                p = nextps()
                for jj in range(4):
                    j = G * 4 + jj
                    fw.op("pe", lambda e, p=p, jj=jj, j=j, t=t, cb=cb, hb=hb: e.matmul(
                        psb[p][:], hb[:, jj, t * 128:(t + 1) * 128], wdb[j % WD][:, cb * 512:(cb + 1) * 512],
                        start=(jj == 0), stop=(jj == 3)),
                        reads=[("hT", G % 2, jj, t // 4), ("wd", j % WD)], writes=[("ps", p)])
                if G == 0:
                    fw.op("dve", lambda e, p=p, t=t, cb=cb: e.scalar_tensor_tensor(
                        out=z[:, t, cb * 512:(cb + 1) * 512], in0=z[:, t, cb * 512:(cb + 1) * 512], scalar=ALPHA,
                        in1=psb[p][:], op0=ALU.mult, op1=ALU.add),
                        reads=[("ps", p), ("z", t, cb)], writes=[("z", t, cb)])
                else:
                    fw.op("dve", lambda e, p=p, t=t, cb=cb: e.tensor_tensor(
                        out=z[:, t, cb * 512:(cb + 1) * 512], in0=z[:, t, cb * 512:(cb + 1) * 512],
                        in1=psb[p][:], op=ALU.add),
                        reads=[("ps", p), ("z", t, cb)], writes=[("z", t, cb)])
        for jj in range(4):
            j = G * 4 + jj
            if j + WD < NJ:
                load_wd(j + WD)

    NG = NJ // 4
    for G in range(NG):
        for jj in range(4):
            up_gate(G * 4 + jj)
        if G >= 1:
            down(G - 1)
    down(NG - 1)
    if prefetch is not None:
        prefetch([("wg", i) for i in range(WG)] + [("wd", i) for i in range(WD)] + P1KEYS)

    for t in range(NT):
        layer_norm(t)
        fw.dma("sp", xo_d[t * 128:(t + 1) * 128, :], z[:, t, :], reads=zk(t), is_output=True)

        if xoT_d is None:
            continue

        def dst(cg, p, t=t):
            cbuf = x1T[:].rearrange("p c t -> p (c t)")[:, (t // 2) * 4096:(t // 2 + 1) * 4096].rearrange("p (c t) -> p c t", c=16)
            wk = [("xc", t // 2, cg, t % 2)]
            if t == 0 and cg == 0:
                wk = wk + [("x1T", tt, cgg) for tt in range(NT) for cgg in range(4)]
            fw.op("act", lambda e: e.copy(out=cbuf[:, cg * 4:(cg + 1) * 4, (t % 2) * 128:(t % 2) * 128 + 128],
                                          in_=psb[p][:].rearrange("p (c n) -> p c n", c=4)),
                  reads=[("ps", p)], writes=wk)
        transpose_tile(t, dst)
        if t % 2 == 1:
            c = t // 2
            fw.dma("sp", xoT_d(c), x1T[:].rearrange("p c t -> p (c t)")[:, c * 4096:(c + 1) * 4096],
                   reads=[("xc", c, cg, h) for cg in range(4) for h in range(2)], writes=[("xodram", c)], is_output=True)
            if post is not None:
                post(c)


D = 2048
S = 4096
NB = S // 512
NEG = -30000.0
ME_FM = [("q", 0, 128), ("k", 128, 128), ("g1", 256, 16), ("dq0", 272, 128), ("dq1", 400, 128), ("dq2", 528, 128),
         ("dk0", 656, 128), ("dk1", 784, 128), ("dk2", 912, 128)]
ME_TM0 = 1040
ME_TM1 = 1552
ME_NC = 2064
DIL_NT = (2, 5, 17)
DIL_OFF = (0, 2, 7)
DIL_TOT = 24
ME_NFG = [44, 59, 67, 75, 80, 80, 80, 80]


def emit_ME(fw, x_src, win_d, wg2_d, ng_d, cst_d, mask_d, idb_d, o_dst, pfx="E", post=None, win_ext=None, win_preloaded=False, tail_prefetch=None):
    nc = fw.nc
    win = win_ext if win_ext is not None else fw.sb(pfx + "win", [128, 16, ME_NC], BF16)
    xT = fw.sb(pfx + "xT", [128, 16, 512], BF16)
    dkT = [fw.sb(pfx + "dkT%d" % g, [128, S], BF16) for g in range(3)]
    dvh = [fw.sb(pfx + "dv%d" % g, [128, 32, 128], BF16) for g in range(3)]
    dqT2 = [[fw.sb(pfx + "dqT%d_%d" % (g, i), [128, 512], BF16) for g in range(3)] for i in range(2)]
    qT2 = [fw.sb(pfx + "qT%d" % i, [128, 512], F32) for i in range(2)]
    kT2 = [fw.sb(pfx + "kT%d" % i, [128, 512], F32) for i in range(2)]
    ktm2 = [fw.sb(pfx + "ktm%d" % i, [128, 4, 128], F32) for i in range(2)]
    vtm2 = [fw.sb(pfx + "vtm%d" % i, [128, 4, 256], BF16) for i in range(2)]
    gs2 = [fw.sb(pfx + "gs%d" % i, [128, 4, 256], F32) for i in range(2)]
    g1a2 = [fw.sb(pfx + "g1a%d" % i, [17, 512], F32) for i in range(2)]
    wg2 = fw.sb(pfx + "wg2", [17, 128], F32)
    ngb = fw.sb(pfx + "ngb", [128, 256], F32)
    cst = fw.sb(pfx + "cst", [128, 4, 128], F32)
    idb = fw.sb(pfx + "idb", [128, 128], BF16)
    mask = fw.sb(pfx + "mask", [128, DIL_TOT * 128], BF16)
    la = fw.sb(pfx + "la", [128, 128], F32)
    eb = fw.sb(pfx + "eb", [128, 128], F32)
    enb = fw.sb(pfx + "enb", [128, 128], F32)
    ee = fw.sb(pfx + "ee", [128, 128], F32)
    qdT = fw.sb(pfx + "qdT", [128, 128], BF16)
    kiT = fw.sb(pfx + "kiT", [128, 128], BF16)
    ken = fw.sb(pfx + "ken", [128, 128], BF16)
    scT = fw.sb(pfx + "scT", [128, 128], BF16)
    St = fw.sb(pfx + "S", [128, 256], F32)
    Sbf = fw.sb(pfx + "Sbf", [128, 256], BF16)
    junk = fw.sb(pfx + "junk", [128, 256], F32)
    sm = fw.sb(pfx + "sm", [128, 8], F32)
    on = fw.sb(pfx + "on", [128, 256], F32)
    ob = fw.sb(pfx + "ob", [128, 128], F32)
    sc = fw.sb(pfx + "sc", [128, DIL_TOT * 128], F32)
    P = fw.sb(pfx + "P", [128, DIL_TOT * 128], BF16)
    PT = [fw.sb(pfx + "PT%d" % i, [128, 512], BF16) for i in range(2)]
    oTa = fw.sb(pfx + "oTa", [128, 3, 512], BF16)
    psI = [fw.ps(pfx + "psI%d" % i, [128, 512], F32) for i in range(2)]
    psG = fw.ps(pfx + "psG", [128, 512], F32)
    psS = [fw.ps(pfx + "psS%d" % i, [128, 512], F32) for i in range(2)]
    psT = [fw.ps(pfx + "psT%d" % i, [128, 1024], BF16) for i in range(2)]
    psV = fw.ps(pfx + "psV", [128, 512], F32)
    identF = cst[:, 0, :]
    TriNeg = cst[:, 1, :]
    UNeg = cst[:, 2, :]
    causT = cst[:, 3, :]

    fw.dma("sp", cst[:], cst_d, writes=[("cst",)])
    fw.dma("sp", idb[:], idb_d, writes=[("idb",)])
    fw.dma("sp", mask[:], mask_d, writes=[("mask",)])
    fw.dma("sp", wg2[:], wg2_d, writes=[("wg2",)])
    fw.dma("sp", ngb[:], ng_d.partition_broadcast(128), writes=[("ngb",)])
    for i in range(2):
        fw.op("dve", lambda e, i=i: e.memset(g1a2[i][:], 1.0), writes=[("g1a", i)])

    def load_x(tb):
        for item in x_src(tb * 512, 512):
            c0, c1, tlo, thi, ap = item[:5]
            fw.dma("pool", xT[:, c0:c1, tlo:thi], ap, reads=list(item[5:]), writes=[("xT", c, tlo) for c in range(c0, c1)])

    load_x(0)
    for c4 in range(4):
        if win_preloaded:
            break
        fw.dma("pool", win[:, c4 * 4:(c4 + 1) * 4, :].rearrange("p c n -> p (c n)"),
               win_d[:, c4 * 4 * ME_NC:(c4 + 1) * 4 * ME_NC], writes=[("win", c4)])
    WINK = [("win", i) for i in range(4)]
    XK = [("xT", c, tlo) for c in range(16) for tlo in range(0, 512, 256)]

    ipi = [0]

    def evac(eng, out, in_, bank, reads, writes, func=None, scale=None):
        ex = [("PS", bank)]
        if eng == "act":
            if func is None and scale is None:
                fw.op("act", lambda e: e.copy(out=out, in_=in_), reads=reads, writes=writes, excl=ex)
            else:
                fw.op("act", lambda e: e.activation(out=out, in_=in_, func=(func or AF.Copy),
                                                    scale=(1.0 if scale is None else scale)), reads=reads, writes=writes, excl=ex)
        else:
            if scale is None:
                fw.op("dve", lambda e: e.tensor_copy(out=out, in_=in_), reads=reads, writes=writes, excl=ex)
            else:
                fw.op("dve", lambda e: e.tensor_scalar(out=out, in0=in_, scalar1=scale, scalar2=None, op0=ALU.mult),
                      reads=reads, writes=writes, excl=ex)

    def inproj(tb):
        bs = tb % 2
        qT, kT, ktm, vtm, gs, g1a, dqT = qT2[bs], kT2[bs], ktm2[bs], vtm2[bs], gs2[bs], g1a2[bs], dqT2[bs]
        for gi, (name, off, m) in enumerate(ME_FM):
            p = ipi[0] % 2
            ipi[0] += 1
            bk = "I%d" % p
            for c in range(16):
                fw.op("pe", lambda e, p=p, c=c, off=off, m=m: e.matmul(
                    psI[p][0:m, :], win[:, c, off:off + m], xT[:, c, :], start=(c == 0), stop=(c == 15)),
                    reads=WINK + XK, writes=[("psI", p)], excl=[("PS", bk)])
            yield
            src = psI[p][0:m, :]
            eng = "act" if gi % 2 == 0 else "dve"
            rd = [("psI", p)]
            if name == "q":
                evac(eng, qT[:], src, bk, rd, [("qT", bs)])
            elif name == "k":
                evac(eng, kT[:], src, bk, rd, [("kT", bs)])
            elif name == "g1":
                evac(eng, g1a[0:16, :], src, bk, rd, [("g1a", bs)])
            elif name.startswith("dq"):
                g = int(name[2])
                evac(eng, dqT[g][:], src, bk, rd, [("dqT", g, bs)], scale=128.0 ** -0.5)
            else:
                g = int(name[2])
                evac(eng, dkT[g][:, tb * 512:(tb + 1) * 512], src, bk, rd, [("dkT", g, tb)])
        for tt in range(4):
            kt = tb * 4 + tt
            for half, off in ((0, ME_TM0), (1, ME_TM1)):
                p = ipi[0] % 2
                ipi[0] += 1
                bk = "I%d" % p
                for c in range(16):
                    fw.op("pe", lambda e, p=p, c=c, off=off, tt=tt: e.matmul(
                        psI[p][:], xT[:, c, tt * 128:(tt + 1) * 128], win[:, c, off:off + 512],
                        start=(c == 0), stop=(c == 15)),
                        reads=WINK + XK, writes=[("psI", p)], excl=[("PS", bk)])
                yield
                rd = [("psI", p)]
                if half == 0:
                    evac("dve", ktm[:, tt, :], psI[p][:, 0:128], bk, rd, [("ktm", bs, tt)])
                    evac("dve", vtm[:, tt, :], psI[p][:, 128:384], bk, rd, [("vtm", bs, tt)])
                    evac("dve", dvh[0][:, kt, :], psI[p][:, 384:512], bk, rd, [("dv", 0, kt)])
                else:
                    evac("act", gs[:, tt, :], psI[p][:, 0:256], bk, rd, [("gs", bs, tt)], func=AF.Silu)
                    evac("act", dvh[1][:, kt, :], psI[p][:, 256:384], bk, rd, [("dv", 1, kt)])
                    evac("act", dvh[2][:, kt, :], psI[p][:, 384:512], bk, rd, [("dv", 2, kt)])
                    fw.op("dve", lambda e, tt=tt: e.tensor_tensor(out=gs[:, tt, :], in0=gs[:, tt, :], in1=ngb[:], op=ALU.mult),
                          reads=[("gs", bs, tt), ("ngb",)], writes=[("gs", bs, tt)])

    GX = [("PS", "G")]

    def gla_chunk(c):
        cc = c % 4
        bs = (c // 4) % 2
        qT, kT, ktm, vtm, gs, g1a = qT2[bs], kT2[bs], ktm2[bs], vtm2[bs], gs2[bs], g1a2[bs]
        ts = slice(cc * 128, (cc + 1) * 128)
        fw.op("pe", lambda e: e.matmul(psG[:, 0:128], g1a[0:17, ts], wg2[0:17, :], start=True, stop=True),
              reads=[("g1a", bs), ("wg2",)], writes=[("pG", 0)], excl=GX)
        fw.op("act", lambda e: e.activation(out=la[:], in_=psG[:, 0:128], func=AF.Exp, scale=-1.0),
              reads=[("pG", 0)], writes=[("la",)], excl=GX)
        fw.op("act", lambda e: e.activation(out=la[:], in_=la[:], func=AF.Ln, bias=1.0, scale=1.0),
              reads=[("la",)], writes=[("la",)])
        yield
        fw.op("pe", lambda e: e.matmul(psG[:, 128:256], la[:], TriNeg, start=True, stop=True),
              reads=[("la",), ("cst",)], writes=[("pG", 1)], excl=GX)
        fw.op("pe", lambda e: e.matmul(psG[:, 256:384], UNeg, la[:], start=True, stop=True),
              reads=[("la",), ("cst",)], writes=[("pG", 2)], excl=GX)
        fw.op("act", lambda e: e.activation(out=eb[:], in_=psG[:, 128:256], func=AF.Exp), reads=[("pG", 1)], writes=[("eb",)], excl=GX)
        fw.op("act", lambda e: e.activation(out=enb[:], in_=psG[:, 128:256], func=AF.Exp, scale=-1.0),
              reads=[("pG", 1)], writes=[("enb",)], excl=GX)
        fw.op("act", lambda e: e.activation(out=ee[:], in_=psG[:, 256:384], func=AF.Exp), reads=[("pG", 2)], writes=[("ee",)], excl=GX)
        fw.op("dve", lambda e: e.scalar_tensor_tensor(out=qdT[:], in0=qT[:, ts], scalar=128.0 ** -0.5, in1=eb[:],
                                                       op0=ALU.mult, op1=ALU.mult),
              reads=[("qT", bs), ("eb",)], writes=[("qdT",)])
        fw.op("dve", lambda e: e.tensor_tensor(out=kiT[:], in0=kT[:, ts], in1=enb[:], op=ALU.mult),
              reads=[("kT", bs), ("enb",)], writes=[("kiT",)])
        fw.op("dve", lambda e: e.tensor_tensor(out=ken[:], in0=ktm[:, cc, :], in1=ee[:], op=ALU.mult),
              reads=[("ktm", bs, cc), ("ee",)], writes=[("ken",)])
        yield
        fw.op("pe", lambda e: e.matmul(psG[:, 384:512], kiT[:], qdT[:], start=True, stop=True),
              reads=[("kiT",), ("qdT",)], writes=[("pG", 3)], excl=GX)
        fw.op("dve", lambda e: e.tensor_tensor(out=scT[:], in0=psG[:, 384:512], in1=causT, op=ALU.mult),
              reads=[("pG", 3), ("cst",)], writes=[("scT",)], excl=GX)
        yield
        fw.op("pe", lambda e: e.matmul(psG[:, 0:256], scT[:], vtm[:, cc, :], start=True, stop=(c == 0)),
              reads=[("scT",), ("vtm", bs, cc)], writes=[("pG", 0), ("pG", 1)], excl=GX)
        if c > 0:
            fw.op("pe", lambda e: e.matmul(psG[:, 0:256], qdT[:], Sbf[:], start=False, stop=True),
                  reads=[("qdT",), ("Sbf",)], writes=[("pG", 0), ("pG", 1)], excl=GX)
        fw.op("pe", lambda e: e.matmul(psG[:, 256:512], ken[:], vtm[:, cc, :], start=True, stop=True),
              reads=[("ken",), ("vtm", bs, cc)], writes=[("pG", 2), ("pG", 3)], excl=GX)
        if c == 0:
            fw.op("dve", lambda e: e.tensor_copy(out=St[:], in_=psG[:, 256:512]), reads=[("pG", 2), ("pG", 3)], writes=[("S",)], excl=GX)
        else:
            fw.op("dve", lambda e: e.scalar_tensor_tensor(out=St[:], in0=St[:], scalar=eb[:, 127:128], in1=psG[:, 256:512],
                                                           op0=ALU.mult, op1=ALU.add),
                  reads=[("S",), ("eb",), ("pG", 2), ("pG", 3)], writes=[("S",)], excl=GX)
        yield
        fw.op("act", lambda e: e.activation(out=junk[:], in_=psG[:, 0:256], func=AF.Square, accum_out=sm[:, 0:1]),
              reads=[("pG", 0), ("pG", 1)], writes=[("junk",), ("sm", 0)], excl=GX)
        fw.op("act", lambda e: e.copy(out=Sbf[:], in_=St[:]), reads=[("S",)], writes=[("Sbf",)])
        fw.op("act", lambda e: e.activation(out=sm[:, 1:2], in_=sm[:, 0:1], func=AF.Sqrt, bias=1e-6, scale=1.0 / 256),
              reads=[("sm", 0)], writes=[("sm", 1)])
        fw.op("dve", lambda e: e.reciprocal(out=sm[:, 2:3], in_=sm[:, 1:2]), reads=[("sm", 1)], writes=[("sm", 2)])
        fw.op("dve", lambda e: e.scalar_tensor_tensor(out=on[:], in0=psG[:, 0:256], scalar=sm[:, 2:3], in1=gs[:, cc, :],
                                                       op0=ALU.mult, op1=ALU.mult),
              reads=[("pG", 0), ("pG", 1), ("sm", 2), ("gs", bs, cc)], writes=[("on",)], excl=GX)
        for e2 in range(2):
            fw.op("pe", lambda e, e2=e2: e.transpose(out=psG[:, e2 * 128:(e2 + 1) * 128],
                                                     in_=on[:, e2 * 128:(e2 + 1) * 128], identity=identF),
                  reads=[("on",), ("cst",)], writes=[("pG", e2)], excl=GX)
        fw.op("act", lambda e: e.copy(out=oTa[:, 0:2, ts], in_=psG[:, 0:256].rearrange("p (a n) -> p a n", a=2)),
              reads=[("pG", 0), ("pG", 1)], writes=[("oTa", 0, cc)], excl=GX)
        yield

    sci = [0]
    pti = [0]
    SCK = [("sc", g, k) for g in range(3) for k in range(5)]
    VX = [("PS", "V")]

    def dil_tile(qt):
        qc = qt % 4
        bs = (qt // 4) % 2
        dqT = dqT2[bs]
        if qt < 16:
            fw.op("dve", lambda e: e.memset(sc[:], NEG), writes=SCK)
        tiles = []
        for g in range(3):
            nt = DIL_NT[g]
            lo = max(0, qt - nt + 1)
            nvt = qt - lo + 1
            dst0 = (DIL_OFF[g] + nt - nvt) * 128
            for i in range(nvt):
                tiles.append((g, lo + i, dst0 + i * 128))
            n = nvt * 128
            done = 0
            while done < n:
                m = min(512, n - done)
                p = sci[0] % 2
                sci[0] += 1
                k0 = lo * 128 + done
                d0 = dst0 + done
                tbs = sorted(set(range(k0 // 512, (k0 + m - 1) // 512 + 1)))
                fw.op("pe", lambda e, p=p, g=g, m=m, k0=k0: e.matmul(
                    psS[p][:, 0:m], dqT[g][:, qc * 128:(qc + 1) * 128], dkT[g][:, k0:k0 + m], start=True, stop=True),
                    reads=[("dqT", g, bs)] + [("dkT", g, t) for t in tbs], writes=[("psS", p)], excl=[("PS", "S%d" % p)])
                fw.op("dve", lambda e, p=p, m=m, d0=d0: e.tensor_tensor(
                    out=sc[:, d0:d0 + m], in0=psS[p][:, 0:m], in1=mask[:, d0:d0 + m], op=ALU.add),
                    reads=[("psS", p), ("mask",)], writes=[("sc", g, done // 512)], excl=[("PS", "S%d" % p)])
                done += m
                yield
        fw.op("dve", lambda e: e.reduce_max(out=sm[:, 3:4], in_=sc[:], axis=AX.X), reads=SCK, writes=[("sm", 3)])
        fw.op("dve", lambda e: e.tensor_scalar(out=sm[:, 4:5], in0=sm[:, 3:4], scalar1=-1.0, scalar2=None, op0=ALU.mult),
              reads=[("sm", 3)], writes=[("sm", 4)])
        fw.op("act", lambda e: e.activation(out=P[:], in_=sc[:], func=AF.Exp, bias=sm[:, 4:5], scale=1.0, accum_out=sm[:, 5:6]),
              reads=SCK + [("sm", 4)], writes=[("P",), ("sm", 5)])
        fw.op("dve", lambda e: e.reciprocal(out=sm[:, 6:7], in_=sm[:, 5:6]), reads=[("sm", 5)], writes=[("sm", 6)])
        yield
        ntile = len(tiles)
        for g0 in range(0, ntile, 4):
            grp = tiles[g0:g0 + 4]
            h = pti[0] % 2
            pti[0] += 1
            tx = [("PS", "T%d" % h)]
            for k, (g, kt, col) in enumerate(grp):
                fw.op("pe", lambda e, k=k, col=col, h=h: e.transpose(
                    out=psT[h][:, k * 128:(k + 1) * 128], in_=P[:, col:col + 128], identity=idb[:]),
                    reads=[("P",), ("idb",)], writes=[("psT", h)], excl=tx)
            n = len(grp) * 128
            evac("act" if h == 0 else "dve", PT[h][:, 0:n], psT[h][:, 0:n], "T%d" % h, [("psT", h)], [("PT", h)])
            for k, (g, kt, col) in enumerate(grp):
                idx = g0 + k
                fw.op("pe", lambda e, k=k, g=g, kt=kt, h=h, idx=idx: e.matmul(
                    psV[:, 0:128], PT[h][:, k * 128:(k + 1) * 128], dvh[g][:, kt, :],
                    start=(idx == 0), stop=(idx == ntile - 1)),
                    reads=[("PT", h), ("dv", g, kt)], writes=[("pV", 0)], excl=VX)
            yield
        fw.op("dve", lambda e: e.tensor_scalar(out=ob[:], in0=psV[:, 0:128], scalar1=sm[:, 6:7], scalar2=None, op0=ALU.mult),
              reads=[("pV", 0), ("sm", 6)], writes=[("ob",)], excl=VX)
        fw.op("pe", lambda e: e.transpose(out=psV[:, 128:256], in_=ob[:], identity=identF),
              reads=[("ob",), ("cst",)], writes=[("pV", 1)], excl=VX)
        fw.op("act", lambda e: e.copy(out=oTa[:, 2, qc * 128:(qc + 1) * 128], in_=psV[:, 128:256]),
              reads=[("pV", 1)], writes=[("oTa", 1, qc)], excl=VX)

    def mixers(tb):
        for cc in range(4):
            yield from gla_chunk(tb * 4 + cc)
            yield from dil_tile(tb * 4 + cc)

    for _ in inproj(0):
        pass
    load_x(1)
    for tb in range(NB):
        bg = inproj(tb + 1) if tb + 1 < NB else iter(())
        bg_live = [tb + 1 < NB]

        def bg_step(tb=tb):
            if not bg_live[0]:
                return
            try:
                next(bg)
            except StopIteration:
                bg_live[0] = False
                if tb + 2 < NB:
                    load_x(tb + 2)
                elif tail_prefetch is not None:
                    tail_prefetch(WINK)

        n = 0
        step = max(1, int(0.75 * ME_NFG[tb] / 17))
        for _ in mixers(tb):
            n += 1
            if n % step == 0:
                bg_step()
        while bg_live[0]:
            bg_step()
        fw.dma("sp", o_dst(tb * 512, 512), oTa[:],
               reads=[("oTa", a, q) for a in range(2) for q in range(4)], writes=[("odram", tb * 512)], is_output=True)
        if post is not None:
            post(tb * 512, 512)


def me_consts():
    import ml_dtypes
    j = np.arange(128)[:, None]
    i = np.arange(128)[None, :]
    cst = np.zeros((128, 4, 128), np.float32)
    cst[:, 0, :] = np.eye(128)
    cst[:, 1, :] = np.where(j <= i, -1.0 / 16, 0.0)
    cst[:, 2, :] = np.where(j > i, -1.0 / 16, 0.0)
    cst[:, 3, :] = np.where(j <= i, 1.0, 0.0)
    mask = np.full((128, DIL_TOT * 128), NEG, np.float32)
    qi = np.arange(128)[:, None]
    kj = np.arange(128)[None, :]
    for g, d in enumerate((1, 4, 16)):
        nt = DIL_NT[g]
        for Dt in range(nt):
            delta = Dt * 128 + qi - kj
            ok = (delta >= 0) & (delta <= 128 * d) & (delta % d == 0)
            pos = DIL_OFF[g] + nt - 1 - Dt
            mask[:, pos * 128:(pos + 1) * 128] = np.where(ok, 0.0, NEG)
    return cst, mask.astype(ml_dtypes.bfloat16), np.eye(128, dtype=np.float32).astype(ml_dtypes.bfloat16)


def me_win_layout(w_in, h):
    def cols(a, n):
        return w_in[:, a:a + n]
    gq = cols(0 + h * 128, 128)
    gk = cols(512 + h * 128, 128)
    gv = cols(1024 + h * 256, 256)
    gg = cols(2048, 16)
    gr = cols(2064 + h * 256, 256)
    dq = [cols(3088 + g * 512 + h * 128, 128) for g in range(3)]
    dk = [cols(4624 + g * 512 + h * 128, 128) for g in range(3)]
    dv = [cols(6160 + g * 512 + h * 128, 128) for g in range(3)]
    w = np.concatenate([gq, gk, gg] + dq + dk + [gk, gv, dv[0], gr, dv[1], dv[2]], axis=1)
    assert w.shape[1] == ME_NC
    return np.ascontiguousarray(w.reshape(16, 128, ME_NC).transpose(1, 0, 2).reshape(128, 16 * ME_NC))


def build_ME(x_dt=F32):
    nc = bass.Bass("TRN2", target_bir_lowering=False)
    xT_d = nc.dram_tensor("xT", [D, S], x_dt, kind="ExternalInput").ap()
    win_d = nc.dram_tensor("win", [128, 16 * ME_NC], F32, kind="ExternalInput").ap()
    wg2_d = nc.dram_tensor("wg2", [17, 128], F32, kind="ExternalInput").ap()
    ng_d = nc.dram_tensor("ng", [256], F32, kind="ExternalInput").ap()
    cst_d = nc.dram_tensor("cst", [128, 4, 128], F32, kind="ExternalInput").ap()
    mask_d = nc.dram_tensor("mask", [128, DIL_TOT * 128], BF16, kind="ExternalInput").ap()
    idb_d = nc.dram_tensor("idb", [128, 128], BF16, kind="ExternalInput").ap()
    oT_d = nc.dram_tensor("oT", [384, S], BF16, kind="ExternalOutput").ap()
    fw = FW(nc)
    emit_ME(fw, lambda t0, n: [(0, 16, tl, tl + 256, xT_d[:, t0 + tl:t0 + tl + 256].rearrange("(c p) t -> p c t", p=128)) for tl in range(0, n, 256)], win_d, wg2_d, ng_d, cst_d, mask_d, idb_d,
            lambda t0, n: oT_d[:, t0:t0 + n].rearrange("(a p) t -> p a t", p=128))
    fw.emit()
    fw.close()
    return nc

import math

D = 2048
S = 4096
TB = 256
NTB = TB // 128
NBO = S // TB
NEG = -30000.0
MO_FM = [("cq", 0, 128), ("ck", 128, 128), ("ci", 256, 128), ("cf", 384, 128)] + \
        [("mq%d" % i, 512 + i * 128, 128) for i in range(4)] + [("mkv%d" % i, 1024 + i * 128, 128) for i in range(4)] + \
        [("krA", 1536, 64), ("krB", 1600, 64)]
MO_TM = 1664
MO_NC = 2176
QSCALE = (128 + 64) ** -0.5
TWO_PI = 2 * math.pi
C1 = 6.28125
C2 = TWO_PI - C1


def emit_MO(fw, x_src, win_d, wuq_d, wukv_d, cols_d, ng_d, pos_d, cst_d, idb_d, o_dst, pfx="O", post=None, win_ext=None, win_preloaded=False, tail_prefetch=None):
    win = win_ext if win_ext is not None else fw.sb(pfx + "win", [128, 16, MO_NC], BF16)
    xT = fw.sb(pfx + "xT", [128, 16, TB], BF16)
    wuq = fw.sb(pfx + "wuq", [128, 4, 512], BF16)
    wukv = fw.sb(pfx + "wukv", [128, 4, 512], BF16)
    knT = [fw.sb(pfx + "knT%d" % h, [128, S], BF16) for h in range(2)]
    krT = fw.sb(pfx + "krT", [64, S], BF16)
    vh = fw.sb(pfx + "vh", [128, 32, 256], BF16)
    sc = fw.sb(pfx + "sc", [128, S], F32)
    P = fw.sb(pfx + "P", [128, S], BF16)
    PT = [fw.sb(pfx + "PT%d" % i, [128, 512], BF16) for i in range(2)]
    mqT = fw.sb(pfx + "mqT", [128, 4, TB], BF16)
    mkvT = fw.sb(pfx + "mkvT", [128, 4, TB], BF16)
    sq = fw.sb(pfx + "sq", [128, 4, TB], BF16)
    rsq = fw.sb(pfx + "rsq", [128, TB], F32)
    rskv = fw.sb(pfx + "rskv", [128, TB], F32)
    qn2 = [[fw.sb(pfx + "qn%d_%d" % (h, i), [128, TB], BF16) for h in range(2)] for i in range(2)]
    qr2 = [[fw.sb(pfx + "qr%d_%d" % (h, i), [64, TB], BF16) for h in range(2)] for i in range(2)]
    CS = fw.sb(pfx + "CS", [64, TB], F32)
    SS = fw.sb(pfx + "SS", [64, TB], F32)
    posi = fw.sb(pfx + "posi", [64, TB], I32)
    th = fw.sb(pfx + "th", [64, TB], F32)
    ru = fw.sb(pfx + "ru", [64, TB], F32)
    rk = fw.sb(pfx + "rk", [64, TB], F32)
    rki = fw.sb(pfx + "rki", [64, TB], I32)
    rt1 = fw.sb(pfx + "rt1", [64, TB], F32)
    rt2 = fw.sb(pfx + "rt2", [64, TB], F32)
    ubq = fw.sb(pfx + "ubq", [128, 3 + TB], F32)
    ubk = fw.sb(pfx + "ubk", [128, 3 + TB], F32)
    acc = fw.sb(pfx + "acc", [128, TB], F32)
    qTb2 = [fw.sb(pfx + "qTb%d" % i, [128, TB], BF16) for i in range(2)]
    kTb2 = [fw.sb(pfx + "kTb%d" % i, [128, TB], BF16) for i in range(2)]
    liB2 = [fw.sb(pfx + "liB%d" % i, [128, TB], F32) for i in range(2)]
    spB2 = [fw.sb(pfx + "spB%d" % i, [128, TB], F32) for i in range(2)]
    vtm2 = [fw.sb(pfx + "vtm%d" % i, [128, NTB, 257], BF16) for i in range(2)]
    gsig2 = [fw.sb(pfx + "gsig%d" % i, [128, NTB, 256], F32) for i in range(2)]
    ngb = fw.sb(pfx + "ngb", [128, 256], F32)
    cols = fw.sb(pfx + "cols", [128, 32], F32)
    cst = fw.sb(pfx + "cst", [128, 4, 128], F32)
    idb = fw.sb(pfx + "idb", [128, 128], BF16)
    onesb = fw.sb(pfx + "onesb", [128, 128], BF16)
    bB = fw.sb(pfx + "bB", [128, 128], F32)
    GB = fw.sb(pfx + "GB", [128, 128], F32)
    DT = fw.sb(pfx + "DT", [128, 128], F32)
    eBt = fw.sb(pfx + "eB", [128, 128], F32)
    qkDT = fw.sb(pfx + "qkDT", [128, 128], BF16)
    qsT = fw.sb(pfx + "qsT", [128, 128], BF16)
    kw = fw.sb(pfx + "kw", [128, 128], BF16)
    Ca = fw.sb(pfx + "Ca", [128, 257], F32)
    Cab = fw.sb(pfx + "Cab", [128, 257], BF16)
    hh = fw.sb(pfx + "hh", [128, 256], F32)
    junk = fw.sb(pfx + "junk", [128, 128], F32)
    od = fw.sb(pfx + "od", [128, 128], F32)
    sm = [fw.sb(pfx + "sm%d" % i, [128, 24], F32) for i in range(2)]
    am = fw.sb(pfx + "am", [128, 8], F32)
    junk2 = fw.sb(pfx + "junk2", [128, 128], F32)
    st6 = fw.sb(pfx + "st6", [128, 6], F32)
    oTa = fw.sb(pfx + "oTa", [128, 4, TB], BF16)
    psI = [fw.ps(pfx + "psI%d" % i, [128, 512], F32) for i in range(2)]
    psG = fw.ps(pfx + "psG", [128, 512], F32)
    psS = [fw.ps(pfx + "psS%d" % i, [128, 512], F32) for i in range(2)]
    psT = [fw.ps(pfx + "psT%d" % i, [128, 1024], BF16) for i in range(2)]
    psV = fw.ps(pfx + "psV", [128, 512], F32)
    identF = cst[:, 0, :]
    causT = cst[:, 1, :]
    causAdd = cst[:, 2, :]
    onesF = cst[:, 3, :]
    CW = lambda i: cols[:, i:i + 1]

    fw.dma("sp", cst[:], cst_d, writes=[("cst",)])
    fw.dma("sp", idb[:], idb_d, writes=[("idb",)])
    fw.dma("sp", cols[:], cols_d, writes=[("cols",)])
    fw.dma("sp", ngb[:], ng_d.partition_broadcast(128), writes=[("ngb",)])
    for i in range(2):
        fw.op("dve", lambda e, i=i: e.memset(vtm2[i][:], 1.0), writes=[("vtm", i, t) for t in range(NTB)])
    fw.op("dve", lambda e: e.memset(onesb[:], 1.0), writes=[("onesb",)])
    fw.op("dve", lambda e: e.memset(ubq[:, 0:3], 0.0), writes=[("ubq",)])
    fw.op("dve", lambda e: e.memset(ubk[:, 0:3], 0.0), writes=[("ubk",)])

    def load_x(tb):
        for item in x_src(tb * TB, TB):
            c0, c1, tlo, thi, ap = item[:5]
            fw.dma("pool", xT[:, c0:c1, tlo:thi], ap, reads=list(item[5:]), writes=[("xT", c, tlo) for c in range(c0, c1)])

    def load_pos(tb):
        fw.dma("sp", posi[:], pos_d[tb * TB:(tb + 1) * TB].partition_broadcast(64), writes=[("posi",)])

    load_x(0)
    load_pos(0)
    for c4 in range(4):
        if win_preloaded:
            break
        fw.dma("pool", win[:, c4 * 4:(c4 + 1) * 4, :].rearrange("p c n -> p (c n)"),
               win_d[:, c4 * 4 * MO_NC:(c4 + 1) * 4 * MO_NC], writes=[("win", c4)])
    fw.dma("pool", wuq[:].rearrange("p c n -> p (c n)"), wuq_d, writes=[("wuq",)])
    fw.dma("pool", wukv[:].rearrange("p c n -> p (c n)"), wukv_d, writes=[("wukv",)])
    WINK = [("win", i) for i in range(4)]
    XK = [("xT", c, tlo) for c in range(16) for tlo in range(0, TB, 256)]
    ipi = [0]

    def nextI():
        p = ipi[0] % 2
        ipi[0] += 1
        return p

    def rope_tables(tb):
        fw.op("dve", lambda e: e.tensor_copy(out=th[:], in_=posi[:]), reads=[("posi",)], writes=[("th",)])
        fw.op("dve", lambda e: e.tensor_scalar(out=th[:], in0=th[:], scalar1=CW(12)[0:64, :], scalar2=None, op0=ALU.mult),
              reads=[("th",), ("cols",)], writes=[("th",)])
        for (dst, ph, key) in ((CS, 13, "CS"), (SS, 14, "SS")):
            fw.op("dve", lambda e, ph=ph: e.tensor_scalar(out=ru[:], in0=th[:], scalar1=CW(ph)[0:64, :], scalar2=None, op0=ALU.add),
                  reads=[("th",), ("cols",)], writes=[("ru",)])
            fw.op("dve", lambda e: e.tensor_scalar(out=rk[:], in0=ru[:], scalar1=1.0 / TWO_PI, scalar2=None, op0=ALU.mult),
                  reads=[("ru",)], writes=[("rk",)])
            fw.op("dve", lambda e: e.tensor_copy(out=rki[:], in_=rk[:]), reads=[("rk",)], writes=[("rki",)])
            fw.op("dve", lambda e: e.tensor_copy(out=rk[:], in_=rki[:]), reads=[("rki",)], writes=[("rk",)])
            fw.op("dve", lambda e: e.scalar_tensor_tensor(out=ru[:], in0=rk[:], scalar=-C1, in1=ru[:], op0=ALU.mult, op1=ALU.add),
                  reads=[("rk",), ("ru",)], writes=[("ru",)])
            fw.op("dve", lambda e: e.scalar_tensor_tensor(out=ru[:], in0=rk[:], scalar=-C2, in1=ru[:], op0=ALU.mult, op1=ALU.add),
                  reads=[("rk",), ("ru",)], writes=[("ru",)])
            fw.op("dve", lambda e: e.tensor_scalar(out=rk[:], in0=ru[:], scalar1=math.pi, scalar2=-TWO_PI, op0=ALU.is_gt, op1=ALU.mult),
                  reads=[("ru",)], writes=[("rk",)])
            fw.op("dve", lambda e: e.tensor_tensor(out=ru[:], in0=ru[:], in1=rk[:], op=ALU.add), reads=[("ru",), ("rk",)], writes=[("ru",)])
            fw.op("dve", lambda e: e.tensor_scalar(out=rk[:], in0=ru[:], scalar1=-math.pi, scalar2=TWO_PI, op0=ALU.is_lt, op1=ALU.mult),
                  reads=[("ru",)], writes=[("rk",)])
            fw.op("dve", lambda e: e.tensor_tensor(out=ru[:], in0=ru[:], in1=rk[:], op=ALU.add), reads=[("ru",), ("rk",)], writes=[("ru",)])
            fw.op("dve", lambda e: e.tensor_scalar(out=ru[:], in0=ru[:], scalar1=-3.1415925, scalar2=3.1415925, op0=ALU.max, op1=ALU.min),
                  reads=[("ru",)], writes=[("ru",)])
            fw.op("act", lambda e, dst=dst: e.activation(out=dst[:], in_=ru[:], func=AF.Sin), reads=[("ru",)], writes=[(key,)])

    def rope_apply(pA, bkA, pB, bkB, dst, dkey, extra=None):
        fw.op("dve", lambda e: e.tensor_tensor(out=rt1[:], in0=psI[pA][0:64, 0:TB], in1=CS[:], op=ALU.mult),
              reads=[("psI", pA), ("CS",)], writes=[("rt1",)], excl=[("PS", bkA)])
        fw.op("dve", lambda e: e.tensor_tensor(out=rt2[:], in0=psI[pB][0:64, 0:TB], in1=SS[:], op=ALU.mult),
              reads=[("psI", pB), ("SS",)], writes=[("rt2",)], excl=[("PS", bkB)])
        if extra is None:
            fw.op("dve", lambda e: e.tensor_tensor(out=dst, in0=rt1[:], in1=rt2[:], op=ALU.add),
                  reads=[("rt1",), ("rt2",)], writes=[dkey])
        else:
            fw.op("dve", lambda e: e.tensor_tensor(out=rt1[:], in0=rt1[:], in1=rt2[:], op=ALU.add),
                  reads=[("rt1",), ("rt2",)], writes=[("rt1",)])
            fw.op("dve", lambda e: e.scalar_tensor_tensor(out=dst, in0=rt1[:], scalar=QSCALE, in1=extra[0:64, :], op0=ALU.mult, op1=ALU.mult),
                  reads=[("rt1",), ("rsq",)], writes=[dkey])

    def rstd_rows(src_key, dst, dkey):
        fw.op("act", lambda e: e.activation(out=dst[:], in_=psG[:, 0:TB], func=AF.Sqrt, bias=1e-6, scale=1.0 / 512),
              reads=[("pG",)], writes=[dkey], excl=[("PS", "G")])
        fw.op("dve", lambda e: e.reciprocal(out=dst[:], in_=dst[:]), reads=[dkey], writes=[dkey])

    def inproj(tb):
        t0 = tb * TB
        bs = tb % 2
        liB, spB, vtm, gsig = liB2[bs], spB2[bs], vtm2[bs], gsig2[bs]
        for gi, (name, off, m) in enumerate(MO_FM):
            if name == "krA":
                continue
            if name == "krB":
                pA, pB = nextI(), nextI()
                for (p, o2) in ((pA, 1536), (pB, 1600)):
                    for c in range(16):
                        fw.op("pe", lambda e, p=p, c=c, o2=o2: e.matmul(
                            psI[p][0:64, 0:TB], win[:, c, o2:o2 + 64], xT[:, c, :], start=(c == 0), stop=(c == 15)),
                            reads=WINK + XK, writes=[("psI", p)], excl=[("PS", "I%d" % p)])
                yield
                rope_apply(pA, "I%d" % pA, pB, "I%d" % pB, krT[:, t0:t0 + TB], ("krT", tb))
                continue
            p = nextI()
            bk = "I%d" % p
            ex = [("PS", bk)]
            for c in range(16):
                fw.op("pe", lambda e, p=p, c=c, off=off, m=m: e.matmul(
                    psI[p][0:m, 0:TB], win[:, c, off:off + m], xT[:, c, :], start=(c == 0), stop=(c == 15)),
                    reads=WINK + XK, writes=[("psI", p)], excl=ex)
            yield
            src = psI[p][:, 0:TB]
            rd = [("psI", p)]
            if name == "cq":
                fw.op("act", lambda e, src=src: e.copy(out=ubq[:, 3:3 + TB], in_=src), reads=rd, writes=[("ubq",)], excl=ex)
            elif name == "ck":
                fw.op("dve", lambda e, src=src: e.tensor_copy(out=ubk[:, 3:3 + TB], in_=src), reads=rd, writes=[("ubk",)], excl=ex)
            elif name == "ci":
                fw.op("act", lambda e, src=src: e.activation(out=liB[:], in_=src, func=AF.Identity, bias=CW(10), scale=1.0),
                      reads=rd + [("cols",)], writes=[("liB", bs)], excl=ex)
            elif name == "cf":
                fw.op("act", lambda e, src=src: e.activation(out=spB[:], in_=src, func=AF.Exp, bias=CW(11), scale=-1.0),
                      reads=rd + [("cols",)], writes=[("spB", bs)], excl=ex)
                fw.op("act", lambda e: e.activation(out=spB[:], in_=spB[:], func=AF.Ln, bias=1.0, scale=1.0),
                      reads=[("spB", bs)], writes=[("spB", bs)])
            elif name.startswith("mq") or name.startswith("mkv"):
                isq = name.startswith("mq")
                ci_ = int(name[-1])
                dstT = mqT if isq else mkvT
                gcol = (16 if isq else 20) + ci_
                lat = "mqT" if isq else "mkvT"
                fw.op("act", lambda e, src=src, ci_=ci_: e.activation(out=sq[:, ci_, :], in_=src, func=AF.Square), reads=rd, writes=[("sq", ci_)], excl=ex)
                fw.op("act", lambda e, src=src, dstT=dstT, ci_=ci_, gcol=gcol: e.activation(
                    out=dstT[:, ci_, :], in_=src, func=AF.Copy, scale=CW(gcol)),
                    reads=rd + [("cols",)], writes=[(lat, ci_)], excl=ex)
                if ci_ == 3:
                    for c4 in range(4):
                        fw.op("pe", lambda e, c4=c4: e.matmul(psG[:, 0:TB], onesb[:], sq[:, c4, :], start=(c4 == 0), stop=(c4 == 3)),
                              reads=[("onesb",), ("sq", c4)], writes=[("pG",)], excl=[("PS", "G")])
                    rstd_rows(lat, rsq if isq else rskv, ("rsq",) if isq else ("rskv",))
        for tt in range(NTB):
            p = nextI()
            ex = [("PS", "I%d" % p)]
            for c in range(16):
                fw.op("pe", lambda e, p=p, c=c, tt=tt: e.matmul(
                    psI[p][:], xT[:, c, tt * 128:(tt + 1) * 128], win[:, c, MO_TM:MO_TM + 512], start=(c == 0), stop=(c == 15)),
                    reads=WINK + XK, writes=[("psI", p)], excl=ex)
            yield
            fw.op("act", lambda e, p=p, tt=tt: e.activation(out=gsig[:, tt, :], in_=psI[p][:, 256:512], func=AF.Sigmoid),
                  reads=[("psI", p)], writes=[("gsig", bs, tt)], excl=ex)
            fw.op("act", lambda e, p=p, tt=tt: e.copy(out=vtm[:, tt, 0:256], in_=psI[p][:, 0:256]),
                  reads=[("psI", p)], writes=[("vtm", bs, tt)], excl=ex)
            fw.op("dve", lambda e, tt=tt: e.tensor_tensor(out=gsig[:, tt, :], in0=gsig[:, tt, :], in1=ngb[:], op=ALU.mult),
                  reads=[("gsig", bs, tt), ("ngb",)], writes=[("gsig", bs, tt)])

    def conv_silu(ub, ukey, woff, bcol, dst, dkey, scale):
        fw.op("dve", lambda e: e.tensor_scalar(out=acc[:], in0=ub[:, 0:TB], scalar1=CW(woff), scalar2=None, op0=ALU.mult),
              reads=[(ukey,), ("cols",)], writes=[("acc",)])
        for tau in range(1, 4):
            fw.op("dve", lambda e, tau=tau: e.scalar_tensor_tensor(out=acc[:], in0=ub[:, tau:tau + TB], scalar=CW(woff + tau), in1=acc[:],
                                                                     op0=ALU.mult, op1=ALU.add),
                  reads=[(ukey,), ("acc",), ("cols",)], writes=[("acc",)])
        fw.op("dve", lambda e: e.tensor_copy(out=ub[:, 0:3], in_=ub[:, TB:TB + 3]), reads=[(ukey,)], writes=[(ukey,)])
        fw.op("act", lambda e: e.activation(out=acc[:], in_=acc[:], func=AF.Silu, bias=CW(bcol), scale=1.0),
              reads=[("acc",), ("cols",)], writes=[("acc",)])
        fw.op("dve", lambda e: e.tensor_scalar(out=dst[:], in0=acc[:], scalar1=scale, scalar2=None, op0=ALU.mult),
              reads=[("acc",)], writes=[dkey])
        yield

    def mla_proj(tb):
        t0 = tb * TB
        bs = tb % 2
        qn, qr = qn2[bs], qr2[bs]
        for h in range(2):
            p = nextI()
            ex = [("PS", "I%d" % p)]
            for c in range(4):
                fw.op("pe", lambda e, p=p, c=c, h=h: e.matmul(psI[p][:, 0:TB], wuq[:, c, h * 256:h * 256 + 128], mqT[:, c, :],
                                                              start=(c == 0), stop=(c == 3)),
                      reads=[("wuq",), ("mqT", c)], writes=[("psI", p)], excl=ex)
            yield
            fw.op("dve", lambda e, p=p, h=h: e.scalar_tensor_tensor(out=qn[h][:], in0=psI[p][:, 0:TB], scalar=QSCALE, in1=rsq[:],
                                                                     op0=ALU.mult, op1=ALU.mult),
                  reads=[("psI", p), ("rsq",)], writes=[("qn", bs, h)], excl=ex)
            pA, pB = nextI(), nextI()
            for (p, o2) in ((pA, h * 256 + 128), (pB, h * 256 + 192)):
                for c in range(4):
                    fw.op("pe", lambda e, p=p, c=c, o2=o2: e.matmul(psI[p][0:64, 0:TB], wuq[:, c, o2:o2 + 64], mqT[:, c, :],
                                                                    start=(c == 0), stop=(c == 3)),
                          reads=[("wuq",), ("mqT", c)], writes=[("psI", p)], excl=[("PS", "I%d" % p)])
            yield
            rope_apply(pA, "I%d" % pA, pB, "I%d" % pB, qr[h][:], ("qr", bs, h), extra=rsq)
            p = nextI()
            ex = [("PS", "I%d" % p)]
            for c in range(4):
                fw.op("pe", lambda e, p=p, c=c, h=h: e.matmul(psI[p][:, 0:TB], wukv[:, c, h * 128:(h + 1) * 128], mkvT[:, c, :],
                                                              start=(c == 0), stop=(c == 3)),
                      reads=[("wukv",), ("mkvT", c)], writes=[("psI", p)], excl=ex)
            yield
            fw.op("dve", lambda e, p=p, h=h: e.tensor_tensor(out=knT[h][:, t0:t0 + TB], in0=psI[p][:, 0:TB], in1=rskv[:], op=ALU.mult),
                  reads=[("psI", p), ("rskv",)], writes=[("knT", h, tb)], excl=ex)
        for tt in range(NTB):
            kt = tb * NTB + tt
            p = nextI()
            ex = [("PS", "I%d" % p)]
            for c in range(4):
                fw.op("pe", lambda e, p=p, c=c, tt=tt: e.matmul(psI[p][:, 0:256], mkvT[:, c, tt * 128:(tt + 1) * 128], wukv[:, c, 256:512],
                                                                start=(c == 0), stop=(c == 3)),
                      reads=[("wukv",), ("mkvT", c)], writes=[("psI", p)], excl=ex)
            fw.op("dve", lambda e, tt=tt: e.scalar_tensor_tensor(out=junk2[:], in0=rskv[:, tt * 128:(tt + 1) * 128], scalar=1.0, in1=identF, op0=ALU.mult, op1=ALU.mult, accum_out=am[:, 7:8]),
                  reads=[("rskv",), ("cst",)], writes=[("junk2",), ("am", 7)])
            fw.op("dve", lambda e, p=p, kt=kt: e.tensor_scalar(out=vh[:, kt, :], in0=psI[p][:, 0:256], scalar1=am[:, 7:8], scalar2=None, op0=ALU.mult),
                  reads=[("psI", p), ("am", 7)], writes=[("vh", kt)], excl=ex)
            yield

    GX = [("PS", "G")]

    def mlstm_chunk(c):
        cc = c % NTB
        bs = (c // NTB) % 2
        qTb, kTb, liB, spB, vtm, gsig = qTb2[bs], kTb2[bs], liB2[bs], spB2[bs], vtm2[bs], gsig2[bs]
        ts = slice(cc * 128, (cc + 1) * 128)
        s = sm[c % 2]
        sp_ = sm[(c + 1) % 2]
        col = lambda i: s[:, i:i + 1]
        fw.op("dve", lambda e: e.tensor_tensor_scan(out=bB[:], data0=onesF, data1=spB[:, ts], initial=0.0, op0=ALU.mult, op1=ALU.subtract),
              reads=[("spB", bs), ("cst",)], writes=[("bB",)])
        fw.op("dve", lambda e: e.tensor_tensor(out=GB[:], in0=liB[:, ts], in1=bB[:], op=ALU.subtract), reads=[("liB", bs), ("bB",)], writes=[("GB",)])
        fw.op("dve", lambda e: e.reduce_max(out=col(0), in_=liB[:, ts], axis=AX.X), reads=[("liB", bs)], writes=[("sm", c % 2, 0)])
        if c == 0:
            fw.op("dve", lambda e: e.tensor_copy(out=col(1), in_=col(0)), reads=[("sm", c % 2, 0)], writes=[("sm", c % 2, 1)])
        else:
            fw.op("dve", lambda e: e.tensor_tensor(out=col(1), in0=col(0), in1=sp_[:, 1:2], op=ALU.max),
                  reads=[("sm", c % 2, 0), ("sm", (c + 1) % 2, 1)], writes=[("sm", c % 2, 1)])
        fw.op("dve", lambda e: e.scalar_tensor_tensor(out=junk[:], in0=bB[:], scalar=1.0, in1=identF, op0=ALU.mult, op1=ALU.mult, accum_out=col(2)), reads=[("bB",), ("cst",)], writes=[("junk",), ("sm", c % 2, 2)])
        fw.op("dve", lambda e: e.scalar_tensor_tensor(out=junk[:], in0=GB[:], scalar=1.0, in1=identF, op0=ALU.mult, op1=ALU.mult, accum_out=col(3)), reads=[("GB",), ("cst",)], writes=[("junk",), ("sm", c % 2, 3)])
        fw.op("dve", lambda e: e.tensor_tensor(out=col(4), in0=col(3), in1=col(1), op=ALU.subtract),
              reads=[("sm", c % 2, 3), ("sm", c % 2, 1)], writes=[("sm", c % 2, 4)])
        if c > 0:
            fw.op("dve", lambda e: e.tensor_tensor(out=col(5), in0=sp_[:, 1:2], in1=col(1), op=ALU.subtract),
                  reads=[("sm", (c + 1) % 2, 1), ("sm", c % 2, 1)], writes=[("sm", c % 2, 5)])
        fw.op("dve", lambda e: e.tensor_tensor(out=col(6), in0=bB[:, 127:128], in1=col(1), op=ALU.subtract),
              reads=[("bB",), ("sm", c % 2, 1)], writes=[("sm", c % 2, 6)])
        yield
        fw.op("pe", lambda e: e.matmul(psG[:, 0:128], kTb[:, ts], qTb[:, ts], start=True, stop=True),
              reads=[("kTb", bs), ("qTb", bs)], writes=[("pG",)], excl=GX)
        fw.op("act", lambda e: e.activation(out=DT[:], in_=bB[:], func=AF.Exp, bias=col(4), scale=1.0),
              reads=[("bB",), ("sm", c % 2, 4)], writes=[("DT",)])
        fw.op("dve", lambda e: e.tensor_tensor(out=DT[:], in0=DT[:], in1=causT, op=ALU.mult), reads=[("DT",), ("cst",)], writes=[("DT",)])
        fw.op("dve", lambda e: e.tensor_tensor(out=qkDT[:], in0=psG[:, 0:128], in1=DT[:], op=ALU.mult),
              reads=[("pG",), ("DT",)], writes=[("qkDT",)], excl=GX)
        yield
        if c > 0:
            fw.op("act", lambda e: e.activation(out=eBt[:], in_=bB[:], func=AF.Exp, bias=col(5), scale=1.0),
                  reads=[("bB",), ("sm", c % 2, 5)], writes=[("eB",)])
            fw.op("dve", lambda e: e.tensor_tensor(out=qsT[:], in0=qTb[:, ts], in1=eBt[:], op=ALU.mult),
                  reads=[("qTb", bs), ("eB",)], writes=[("qsT",)])
        fw.op("pe", lambda e: e.matmul(psG[:, 0:257], qkDT[:], vtm[:, cc, :], start=True, stop=(c == 0)),
              reads=[("qkDT",), ("vtm", bs, cc)], writes=[("pG",)], excl=GX)
        if c > 0:
            fw.op("pe", lambda e: e.matmul(psG[:, 0:257], qsT[:], Cab[:], start=False, stop=True),
                  reads=[("qsT",), ("Cab",)], writes=[("pG",)], excl=GX)
        fw.op("act", lambda e: e.activation(out=col(7), in_=col(1), func=AF.Exp, scale=-1.0),
              reads=[("sm", c % 2, 1)], writes=[("sm", c % 2, 7)])
        fw.op("act", lambda e: e.activation(out=col(10), in_=psG[:, 256:257], func=AF.Abs),
              reads=[("pG",)], writes=[("sm", c % 2, 10)], excl=GX)
        fw.op("dve", lambda e: e.tensor_tensor(out=col(10), in0=col(10), in1=col(7), op=ALU.max),
              reads=[("sm", c % 2, 10), ("sm", c % 2, 7)], writes=[("sm", c % 2, 10)])
        fw.op("dve", lambda e: e.reciprocal(out=col(11), in_=col(10)), reads=[("sm", c % 2, 10)], writes=[("sm", c % 2, 11)])
        fw.op("dve", lambda e: e.tensor_scalar(out=hh[:], in0=psG[:, 0:256], scalar1=col(11), scalar2=None, op0=ALU.mult),
              reads=[("pG",), ("sm", c % 2, 11)], writes=[("hh",)], excl=GX)
        yield
        fw.op("act", lambda e: e.activation(out=col(8), in_=col(3), func=AF.Exp, bias=col(6), scale=1.0),
              reads=[("sm", c % 2, 3), ("sm", c % 2, 6)], writes=[("sm", c % 2, 8)])
        h = c % 2
        tx = [("PS", "T%d" % h)]
        fw.op("pe", lambda e: e.transpose(out=psT[h][:, 0:128], in_=kTb[:, ts], identity=idb[:]),
              reads=[("kTb", bs), ("idb",)], writes=[("psT", h)], excl=tx)
        fw.op("dve", lambda e: e.tensor_scalar(out=kw[:], in0=psT[h][:, 0:128], scalar1=col(8), scalar2=None, op0=ALU.mult),
              reads=[("psT", h), ("sm", c % 2, 8)], writes=[("kw",)], excl=tx)
        fw.op("pe", lambda e: e.matmul(psG[:, 0:257], kw[:], vtm[:, cc, :], start=True, stop=True),
              reads=[("kw",), ("vtm", bs, cc)], writes=[("pG",)], excl=GX)
        if c == 0:
            fw.op("dve", lambda e: e.tensor_copy(out=Ca[:], in_=psG[:, 0:257]), reads=[("pG",)], writes=[("Ca",)], excl=GX)
        else:
            fw.op("act", lambda e: e.activation(out=col(9), in_=bB[:, 127:128], func=AF.Exp, bias=col(5), scale=1.0),
                  reads=[("bB",), ("sm", c % 2, 5)], writes=[("sm", c % 2, 9)])
            fw.op("dve", lambda e: e.scalar_tensor_tensor(out=Ca[:], in0=Ca[:], scalar=col(9), in1=psG[:, 0:257], op0=ALU.mult, op1=ALU.add),
                  reads=[("Ca",), ("sm", c % 2, 9), ("pG",)], writes=[("Ca",)], excl=GX)
        fw.op("act", lambda e: e.copy(out=Cab[:], in_=Ca[:]), reads=[("Ca",)], writes=[("Cab",)])
        yield
        fw.op("dve", lambda e: e.bn_stats(out=st6[:], in_=hh[:]), reads=[("hh",)], writes=[("st6",)])
        fw.op("dve", lambda e: e.bn_aggr(out=s[:, 12:14], in_=st6[:]), reads=[("st6",)], writes=[("sm", c % 2, 12)])
        fw.op("act", lambda e: e.activation(out=col(14), in_=col(13), func=AF.Sqrt, bias=1e-5, scale=1.0),
              reads=[("sm", c % 2, 12)], writes=[("sm", c % 2, 14)])
        fw.op("dve", lambda e: e.reciprocal(out=col(14), in_=col(14)), reads=[("sm", c % 2, 14)], writes=[("sm", c % 2, 14)])
        fw.op("dve", lambda e: e.scalar_tensor_tensor(out=col(15), in0=col(12), scalar=-1.0, in1=col(14), op0=ALU.mult, op1=ALU.mult),
              reads=[("sm", c % 2, 12), ("sm", c % 2, 14)], writes=[("sm", c % 2, 15)])
        fw.op("act", lambda e: e.activation(out=hh[:], in_=hh[:], func=AF.Identity, bias=col(15), scale=col(14)),
              reads=[("hh",), ("sm", c % 2, 14), ("sm", c % 2, 15)], writes=[("hh",)])
        fw.op("dve", lambda e: e.tensor_tensor(out=hh[:], in0=hh[:], in1=gsig[:, cc, :], op=ALU.mult),
              reads=[("hh",), ("gsig", bs, cc)], writes=[("hh",)])
        for e2 in range(2):
            fw.op("pe", lambda e, e2=e2: e.transpose(out=psG[:, e2 * 128:(e2 + 1) * 128], in_=hh[:, e2 * 128:(e2 + 1) * 128], identity=identF),
                  reads=[("hh",), ("cst",)], writes=[("pG",)], excl=GX)
        fw.op("act", lambda e: e.copy(out=oTa[:, 0:2, ts], in_=psG[:, 0:256].rearrange("p (a n) -> p a n", a=2)),
              reads=[("pG",)], writes=[("oTa", 0, cc)], excl=GX)
        yield

    sci = [0]
    pti = [0]
    VX = [("PS", "V")]

    def mla_tile(qt, h):
        qc = qt % NTB
        bs = (qt // NTB) % 2
        qn, qr = qn2[bs], qr2[bs]
        qs = slice(qc * 128, (qc + 1) * 128)
        n = (qt + 1) * 128
        done = 0
        while done < n:
            m = min(512, n - done)
            p = sci[0] % 2
            sci[0] += 1
            ex = [("PS", "S%d" % p)]
            blks = sorted(set(range(done // TB, (done + m - 1) // TB + 1)))
            fw.op("pe", lambda e, p=p, m=m, done=done: e.matmul(psS[p][:, 0:m], qn[h][:, qs], knT[h][:, done:done + m], start=True, stop=False),
                  reads=[("qn", bs, h)] + [("knT", h, b) for b in blks], writes=[("psS", p)], excl=ex)
            fw.op("pe", lambda e, p=p, m=m, done=done: e.matmul(psS[p][:, 0:m], qr[h][:, qs], krT[:, done:done + m], start=False, stop=True),
                  reads=[("qr", bs, h)] + [("krT", b) for b in blks], writes=[("psS", p)], excl=ex)
            last = (done + m == n)
            mm = m - 128 if last else m
            if mm > 0:
                fw.op("act", lambda e, p=p, mm=mm, done=done: e.copy(out=sc[:, done:done + mm], in_=psS[p][:, 0:mm]),
                      reads=[("psS", p)], writes=[("sc", done // 512)], excl=ex)
            if last:
                fw.op("dve", lambda e, p=p, mm=mm, done=done: e.tensor_tensor(out=sc[:, done + mm:done + mm + 128], in0=psS[p][:, mm:mm + 128],
                                                                               in1=causAdd, op=ALU.add),
                      reads=[("psS", p), ("cst",)], writes=[("sc", "d")], excl=ex)
            done += m
            yield
        SCK = [("sc", k) for k in range((n + 511) // 512)] + [("sc", "d")]
        fw.op("dve", lambda e: e.reduce_max(out=am[:, 0:1], in_=sc[:, 0:n], axis=AX.X), reads=SCK, writes=[("am", 0)])
        fw.op("dve", lambda e: e.tensor_scalar(out=am[:, 1:2], in0=am[:, 0:1], scalar1=-1.0, scalar2=None, op0=ALU.mult),
              reads=[("am", 0)], writes=[("am", 1)])
        fw.op("act", lambda e: e.activation(out=P[:, 0:n], in_=sc[:, 0:n], func=AF.Exp, bias=am[:, 1:2], scale=1.0, accum_out=am[:, 2:3]),
              reads=SCK + [("am", 1)], writes=[("P",), ("am", 2)])
        fw.op("dve", lambda e: e.reciprocal(out=am[:, 3:4], in_=am[:, 2:3]), reads=[("am", 2)], writes=[("am", 3)])
        yield
        ntile = qt + 1
        for g0 in range(0, ntile, 4):
            grp = list(range(g0, min(g0 + 4, ntile)))
            hb = pti[0] % 2
            pti[0] += 1
            tx = [("PS", "T%d" % hb)]
            for k, kt in enumerate(grp):
                fw.op("pe", lambda e, k=k, kt=kt, hb=hb: e.transpose(out=psT[hb][:, k * 128:(k + 1) * 128], in_=P[:, kt * 128:(kt + 1) * 128],
                                                                   identity=idb[:]),
                      reads=[("P",), ("idb",)], writes=[("psT", hb)], excl=tx)
            nn = len(grp) * 128
            if hb == 0:
                fw.op("act", lambda e, hb=hb, nn=nn: e.copy(out=PT[hb][:, 0:nn], in_=psT[hb][:, 0:nn]), reads=[("psT", hb)], writes=[("PT", hb)], excl=tx)
            else:
                fw.op("dve", lambda e, hb=hb, nn=nn: e.tensor_copy(out=PT[hb][:, 0:nn], in_=psT[hb][:, 0:nn]), reads=[("psT", hb)], writes=[("PT", hb)], excl=tx)
            for k, kt in enumerate(grp):
                fw.op("pe", lambda e, k=k, kt=kt, hb=hb: e.matmul(psV[:, 0:128], PT[hb][:, k * 128:(k + 1) * 128], vh[:, kt, h * 128:(h + 1) * 128],
                                                                start=(kt == 0), stop=(kt == ntile - 1)),
                      reads=[("PT", hb), ("vh", kt)], writes=[("pV", 0)], excl=VX)
            yield
        fw.op("dve", lambda e: e.tensor_scalar(out=od[:], in0=psV[:, 0:128], scalar1=am[:, 3:4], scalar2=None, op0=ALU.mult),
              reads=[("pV", 0), ("am", 3)], writes=[("od",)], excl=VX)
        fw.op("pe", lambda e: e.transpose(out=psV[:, 128:256], in_=od[:], identity=identF), reads=[("od",), ("cst",)], writes=[("pV", 1)], excl=VX)
        fw.op("act", lambda e: e.copy(out=oTa[:, 2 + h, qs], in_=psV[:, 128:256]), reads=[("pV", 1)], writes=[("oTa", 1 + h, qc)], excl=VX)

    def pre(tb):
        bs = tb % 2
        rope_tables(tb)
        yield
        yield from inproj(tb)
        yield from conv_silu(ubq, "ubq", 0, 8, qTb2[bs], ("qTb", bs), 1.0)
        yield from conv_silu(ubk, "ubk", 4, 9, kTb2[bs], ("kTb", bs), 128.0 ** -0.5)
        yield from mla_proj(tb)

    def mixers(tb):
        for cc in range(NTB):
            c = tb * NTB + cc
            yield from mlstm_chunk(c)
            for h in range(2):
                yield from mla_tile(c, h)

    for _ in pre(0):
        pass
    load_x(1)
    load_pos(1)
    for tb in range(NBO):
        bg = pre(tb + 1) if tb + 1 < NBO else iter(())
        bg_live = [tb + 1 < NBO]

        def bg_step(tb=tb):
            if not bg_live[0]:
                return
            try:
                next(bg)
            except StopIteration:
                bg_live[0] = False
                if tb + 2 < NBO:
                    load_x(tb + 2)
                    load_pos(tb + 2)
                elif tail_prefetch is not None:
                    tail_prefetch(WINK + [("wuq",), ("wukv",)])

        n = 0
        nfg = sum(5 + 2 * (2 * ((tb * NTB + cc) // 4 + 1) + 1) for cc in range(NTB))
        step = max(1, int(0.75 * nfg / 26))
        for _ in mixers(tb):
            n += 1
            if n % step == 0:
                bg_step()
        while bg_live[0]:
            bg_step()
        fw.dma("sp", o_dst(tb * TB, TB), oTa[:],
               reads=[("oTa", a, q) for a in range(3) for q in range(NTB)], writes=[("odram", tb * TB)], is_output=True)
        if post is not None:
            post(tb * TB, TB)


def mo_consts():
    import ml_dtypes
    j = np.arange(128)[:, None]
    i = np.arange(128)[None, :]
    cst = np.zeros((128, 4, 128), np.float32)
    cst[:, 0, :] = np.eye(128)
    cst[:, 1, :] = np.where(j <= i, 1.0, 0.0)
    cst[:, 2, :] = np.where(i <= j, 0.0, NEG)
    cst[:, 3, :] = 1.0
    return cst, np.eye(128, dtype=np.float32).astype(ml_dtypes.bfloat16)


def mo_cols(conv_w, conv_b, bi, bf, gq, gkv, h):
    cols = np.zeros((128, 32), np.float32)
    cols[:, 0:4] = conv_w[:, h * 128:(h + 1) * 128].T
    cols[:, 4:8] = conv_w[:, 512 + h * 128:512 + (h + 1) * 128].T
    cols[:, 8] = conv_b[h * 128:(h + 1) * 128]
    cols[:, 9] = conv_b[512 + h * 128:512 + (h + 1) * 128]
    cols[:, 10] = bi[h]
    cols[:, 11] = -bf[h]
    invf = (10000.0 ** (-np.arange(0, 64, 2, dtype=np.float32) / 64)).astype(np.float32)
    cols[0:64, 12] = np.concatenate([invf, invf])
    cols[0:64, 13] = math.pi / 2
    cols[0:32, 14] = math.pi
    cols[32:64, 14] = 0.0
    cols[:, 16:20] = gq.reshape(4, 128).T
    cols[:, 20:24] = gkv.reshape(4, 128).T
    return cols


def mo_win_layout(w_in, h):
    def cols(a, n):
        return w_in[:, a:a + n]
    cq = cols(h * 128, 128)
    ck = cols(512 + h * 128, 128)
    cv = cols(1024 + h * 256, 256)
    ci = np.repeat(cols(2048 + h, 1), 128, axis=1)
    cf = np.repeat(cols(2052 + h, 1), 128, axis=1)
    co = cols(2056 + h * 256, 256)
    mq = cols(3080, 512)
    mkv = cols(3592, 512)
    kr = cols(4104, 64)
    krB = np.concatenate([kr[:, 32:], kr[:, :32]], axis=1)
    w = np.concatenate([cq, ck, ci, cf, mq, mkv, kr, krB, cv, co], axis=1)
    assert w.shape[1] == MO_NC
    return np.ascontiguousarray(w.reshape(16, 128, MO_NC).transpose(1, 0, 2).reshape(128, 16 * MO_NC))


def mo_wu_layout(wuq, wukv, h):
    qs = []
    for hm in (2 * h, 2 * h + 1):
        blk = wuq[:, hm * 192:(hm + 1) * 192]
        nope, rope = blk[:, :128], blk[:, 128:]
        qs += [nope, rope, np.concatenate([rope[:, 32:], rope[:, :32]], axis=1)]
    q = np.concatenate(qs, axis=1)
    kn = [wukv[:, hm * 256:hm * 256 + 128] for hm in (2 * h, 2 * h + 1)]
    vv = [wukv[:, hm * 256 + 128:hm * 256 + 256] for hm in (2 * h, 2 * h + 1)]
    kv = np.concatenate(kn + vv, axis=1)
    lay = lambda w: np.ascontiguousarray(w.reshape(4, 128, 512).transpose(1, 0, 2).reshape(128, 4 * 512))
    return lay(q), lay(kv)


def build_MO(x_dt=F32):
    nc = bass.Bass("TRN2", target_bir_lowering=False)
    xT_d = nc.dram_tensor("xT", [D, S], x_dt, kind="ExternalInput").ap()
    win_d = nc.dram_tensor("win", [128, 16 * MO_NC], F32, kind="ExternalInput").ap()
    wuq_d = nc.dram_tensor("wuq", [128, 4 * 512], F32, kind="ExternalInput").ap()
    wukv_d = nc.dram_tensor("wukv", [128, 4 * 512], F32, kind="ExternalInput").ap()
    cols_d = nc.dram_tensor("cols", [128, 32], F32, kind="ExternalInput").ap()
    ng_d = nc.dram_tensor("ng", [256], F32, kind="ExternalInput").ap()
    pos_d = nc.dram_tensor("pos", [S], I32, kind="ExternalInput").ap()
    cst_d = nc.dram_tensor("cst", [128, 4, 128], F32, kind="ExternalInput").ap()
    idb_d = nc.dram_tensor("idb", [128, 128], BF16, kind="ExternalInput").ap()
    oT_d = nc.dram_tensor("oT", [512, S], BF16, kind="ExternalOutput").ap()
    fw = FW(nc)
    emit_MO(fw, lambda t0, n: [(0, 16, tl, tl + 256, xT_d[:, t0 + tl:t0 + tl + 256].rearrange("(c p) t -> p c t", p=128)) for tl in range(0, n, 256)], win_d, wuq_d, wukv_d, cols_d, ng_d, pos_d, cst_d, idb_d,
            lambda t0, n: oT_d[:, t0:t0 + n].rearrange("(a p) t -> p a t", p=128))
    fw.emit()
    fw.close()
    return nc


GROUPS = [[0, 1, 2, 3], [4, 5, 6, 7]]


def build_fused(nl=4, stop_after_mixer=False):
    nc = bass.Bass("TRN2", target_bir_lowering=False)

    def EI(name, shape, dt):
        return nc.dram_tensor(name, list(shape), dt, kind="ExternalInput").ap()

    x0T_d = EI("x0T", [D, S], F32)
    xres0_d = EI("xres0", [TOK, D], F32)
    pos_d = EI("pos", [S], I32)
    sel_d = EI("sel", [128, 4], F32)
    cstE_d = EI("cstE", [128, 4, 128], F32)
    maskE_d = EI("maskE", [128, DIL_TOT * 128], BF16)
    idb_d = EI("idb", [128, 128], BF16)
    cstO_d = EI("cstO", [128, 4, 128], F32)
    ident_d = EI("ident", [128, 128], F32)
    W = {}
    for j in range((nl + 1) // 2):
        W["winE", j] = EI("winE%d" % j, [128, 16 * ME_NC], F32)
        W["wg2E", j] = EI("wg2E%d" % j, [17, 128], F32)
        W["ngE", j] = EI("ngE%d" % j, [256], F32)
    for j in range(nl // 2):
        W["winO", j] = EI("winO%d" % j, [128, 16 * MO_NC], F32)
        W["wuqO", j] = EI("wuqO%d" % j, [128, 4 * 512], F32)
        W["wukvO", j] = EI("wukvO%d" % j, [128, 4 * 512], F32)
        W["colsO", j] = EI("colsO%d" % j, [128, 32], F32)
        W["ngO", j] = EI("ngO%d" % j, [256], F32)
    for l in range(nl):
        F = 1536 if l % 2 == 0 else 2048
        W["wo", l] = EI("wo%d" % l, [F, D], F32)
        W["wgu", l] = EI("wgu%d" % l, [NJ, 128, 16 * 256], F32)
        W["wd", l] = EI("wd%d" % l, [HID, D], F32)
        W["ln", l] = EI("ln%d" % l, [4, D], F32)
    xo_d = nc.dram_tensor("xo", [TOK, D], F32, kind="ExternalOutput").ap()
    FH = {0: 384, 1: 512}
    oTx = {o: [nc.dram_tensor("oTx%d_%d" % (o, c), [FH[o], TOK], BF16) for c in range(4)] for o in (0, 1)}
    oTg = {o: [nc.dram_tensor("oTg%d_%d" % (o, c), [4 * FH[o], TOK], BF16) for c in range(4)] for o in (0, 1)}
    xoT = [nc.dram_tensor("xoT_%d" % c, [128, 16 * 256], BF16) for c in range(4)]
    xTg = [[nc.dram_tensor("xTg%d_%d" % (i, c), [4 * 128, 16 * 256], BF16) for c in range(4)] for i in range(2)]
    xbuf = nc.dram_tensor("xbuf", [TOK, D], F32)

    fw = FW(nc)
    WB = fw.sb("WB", [128, 16 * MO_NC], BF16)
    for l in range(nl):
        j = l // 2
        odd = l % 2
        F = 2048 if odd else 1536
        fw.begin_phase()
        if l == 0:
            x_src = lambda t0, n: [(0, 16, tl, tl + 256, x0T_d[:, t0 + tl:t0 + tl + 256].rearrange("(c p) t -> p c t", p=128))
                                   for tl in range(0, n, 256)]
        else:
            gbs = xTg[(l - 1) % 2]
            x_src = lambda t0, n, gbs=gbs, l=l: [
                (0, 16, tl, tl + 256, gbs[((t0 + tl) % TOK) // 256].ap()[((t0 + tl) // TOK) * 128:((t0 + tl) // TOK + 1) * 128, :].rearrange(
                    "p (c t) -> p c t", c=16), ("xTg", (l - 1) % 2, ((t0 + tl) % TOK) // 256)) for tl in range(0, n, 256)]
        def post_mix(t0, n, odd=odd):
            if (t0 + n) % TOK == 0:
                c = t0 // TOK
                fw.collective("AllGather", GROUPS, oTx[odd][c], oTg[odd][c], reads=[("odram", tt) for tt in range(c * TOK, (c + 1) * TOK, n)],
                              writes=[("oTg", odd, c)])

        o_dst = lambda t0, n, odd=odd: oTx[odd][t0 // TOK].ap()[:, (t0 % TOK):(t0 % TOK) + n].rearrange("(a p) t -> p a t", p=128)
        wo_l = W["wo", l]
        KCl = F // 128

        def tail_prefetch(keys, wo_l=wo_l, KCl=KCl):
            for cb in range(2):
                dst = WB[:, 16384 + cb * 8192:16384 + (cb + 1) * 8192].rearrange("p (c n) -> p c n", c=16)
                fw.dma("pool", dst[:, 0:KCl, :], wo_l[:, cb * 512:(cb + 1) * 512].rearrange("(c p) n -> p c n", p=128), writes=list(keys))

        NCl = MO_NC if odd else ME_NC
        win_view = WB[:, 0:16 * NCl].rearrange("p (c n) -> p c n", c=16)
        pre_l = (l > 0)
        if not odd:
            emit_ME(fw, x_src, W["winE", j], W["wg2E", j], W["ngE", j], cstE_d, maskE_d, idb_d, o_dst, pfx="E%d" % l, post=post_mix,
                    win_ext=win_view, win_preloaded=pre_l, tail_prefetch=tail_prefetch)
        else:
            emit_MO(fw, x_src, W["winO", j], W["wuqO", j], W["wukvO", j], W["colsO", j], W["ngO", j], pos_d, cstO_d, idb_d,
                    o_dst, pfx="O%d" % l, post=post_mix, win_ext=win_view, win_preloaded=pre_l, tail_prefetch=tail_prefetch)
        fw.end_phase(wait_cc=False)
        fw.begin_phase()
        ogs = [t.ap() for t in oTg[odd]]

        def loader(fw_, oT, KC, tmp, ogs=ogs, l=l):
            selt = fw_.sb("selt%d" % l, [128, 4], F32)
            fw_.dma("sp", selt[:], sel_d, writes=[("selt",)])
            i = 0
            for r in range(4):
                for k in range(KC):
                    tb = tmp[i % len(tmp)]
                    key = ("tmp", i % len(tmp))
                    i += 1
                    fw_.dma("sp", tb, ogs[r][k * 128:(k + 1) * 128, :], reads=[("oTg", l % 2, r)], writes=[key])
                    if r == 0:
                        fw_.op("dve", lambda e, tb=tb, k=k: e.tensor_scalar(out=oT[:, k, :], in0=tb, scalar1=selt[:, 0:1], scalar2=None,
                                                                            op0=ALU.mult),
                               reads=[key, ("selt",)], writes=[("oT", k)])
                    else:
                        fw_.op("dve", lambda e, tb=tb, k=k, r=r: e.scalar_tensor_tensor(out=oT[:, k, :], in0=tb, scalar=selt[:, r:r + 1],
                                                                                       in1=oT[:, k, :], op0=ALU.mult, op1=ALU.add),
                               reads=[key, ("selt",), ("oT", k)], writes=[("oT", k)])

        last = (l == nl - 1)

        def prefetch_next(keys, l=l):
            nodd = (l + 1) % 2
            nj = (l + 1) // 2
            NCn = MO_NC if nodd else ME_NC
            wsrc = W["winO", nj] if nodd else W["winE", nj]
            wv = WB[:, 0:16 * NCn].rearrange("p (c n) -> p c n", c=16)
            for c4 in range(4):
                fw.dma("pool", wv[:, c4 * 4:(c4 + 1) * 4, :].rearrange("p c n -> p (c n)"),
                       wsrc[:, c4 * 4 * NCn:(c4 + 1) * 4 * NCn], writes=list(keys))

        def post_T(c, l=l):
            fw.collective("AllGather", GROUPS, xoT[c], xTg[l % 2][c], reads=[("xodram", c)], writes=[("xTg", l % 2, c)])
        emit_T(fw, F, loader, xres0_d if l == 0 else xbuf.ap(), W["wo", l], W["wgu", l], W["wd", l], W["ln", l], ident_d,
               xo_d if last else xbuf.ap(),
               None if last else (lambda c: xoT[c].ap()),
               BF16, pfx="T%d" % l, post=None if last else post_T, U_ext=WB, wo_preloaded=True,
               prefetch=None if last else prefetch_next)
        fw.end_phase(wait_cc=last)
    if nl % 2 == 1 and nl < 4:
        pass
    fw.close()
    return nc


def lay_wgu(wgu):
    g = wgu[:, :HID].reshape(16, 128, NJ, 128)
    u = wgu[:, HID:].reshape(16, 128, NJ, 128)
    gu = np.concatenate([g, u], axis=-1)
    return np.ascontiguousarray(gu.transpose(2, 1, 0, 3).reshape(NJ, 128, 16 * 256))


def wo_gathered(w_o, odd):
    rows = []
    for q in range(4):
        rows.append(w_o[q * 256:(q + 1) * 256])
        if odd:
            rows.append(w_o[1024 + q * 256:1024 + (q + 1) * 256])
        else:
            rows.append(w_o[1024 + q * 128:1024 + (q + 1) * 128])
    return np.ascontiguousarray(np.concatenate(rows, 0)).astype(np.float32)


_NC = {}
NL = 4


def kernel(**inputs):
    inp = {k: np.asarray(v) for k, v in inputs.items()}
    x = np.ascontiguousarray(inp["x"]).astype(np.float32)
    pos = inp["positions"].astype(np.int32)
    cstE, maskE, idb = me_consts()
    cstO, _ = mo_consts()
    shared = dict(cstE=cstE, maskE=maskE, idb=idb, cstO=cstO, ident=np.eye(128, dtype=np.float32))
    for l in range(NL):
        j = l // 2
        odd = l % 2
        shared["wo%d" % l] = wo_gathered(inp["odd_w_o"][j] if odd else inp["even_w_o"][j], odd)
        shared["wgu%d" % l] = lay_wgu(inp["ffn_wgu"][l])
        shared["wd%d" % l] = np.ascontiguousarray(inp["ffn_wd"][l]).astype(np.float32)
        shared["ln%d" % l] = np.stack([inp["ln1_g"][l], inp["ln1_b"][l], inp["ln2_g"][l], inp["ln2_b"][l]]).astype(np.float32)
    xT = [np.ascontiguousarray(x[b].T) for b in range(2)]
    in_maps = []
    for c in range(8):
        b, h = c // 4, c % 4
        m = dict(shared)
        m["x0T"] = xT[b]
        m["xres0"] = np.ascontiguousarray(x[b, h * TOK:(h + 1) * TOK])
        m["pos"] = np.ascontiguousarray(pos[b])
        sel = np.zeros((128, 4), np.float32)
        sel[:, h] = 1.0
        m["sel"] = sel
        for j in range((NL + 1) // 2):
            wg2 = np.concatenate([inp["even_gla_wg2"][j][:, h * 128:(h + 1) * 128],
                                  inp["even_gla_bg"][j][None, h * 128:(h + 1) * 128]], 0).astype(np.float32)
            m["winE%d" % j] = me_win_layout(inp["even_w_in"][j], h)
            m["wg2E%d" % j] = np.ascontiguousarray(wg2)
            m["ngE%d" % j] = np.ascontiguousarray(inp["even_gla_norm_g"][j]).astype(np.float32)
            if j >= NL // 2:
                continue
            wq, wkv = mo_wu_layout(inp["odd_mla_wuq"][j], inp["odd_mla_wukv"][j], h)
            m["winO%d" % j] = mo_win_layout(inp["odd_w_in"][j], h)
            m["wuqO%d" % j] = wq
            m["wukvO%d" % j] = wkv
            m["colsO%d" % j] = mo_cols(inp["odd_conv_w"][j], inp["odd_conv_b"][j], inp["odd_mlstm_bi"][j], inp["odd_mlstm_bf"][j],
                                       inp["odd_mla_qnorm_g"][j], inp["odd_mla_kvnorm_g"][j], h)
            m["ngO%d" % j] = np.ascontiguousarray(inp["odd_mlstm_norm_g"][j]).astype(np.float32)
        in_maps.append(m)
    if "nc" not in _NC:
        _NC["nc"] = build_fused(NL)
    res = run_bass_kernel_spmd(_NC["nc"], in_maps, core_ids=list(range(8)))
    out = np.empty_like(x)
    for c in range(8):
        b, h = c // 4, c % 4
        out[b, h * TOK:(h + 1) * TOK] = res.results[c]["xo"]
    return out
```

```python
import math
import ml_dtypes
from concourse.bass_utils import run_bass_kernel_spmd


import numpy as np
import concourse.bass as bass
import concourse.mybir as mybir
from contextlib import ExitStack

F32 = mybir.dt.float32
BF16 = mybir.dt.bfloat16
I32 = mybir.dt.int32
AF = mybir.ActivationFunctionType
ALU = mybir.AluOpType
AX = mybir.AxisListType

ENGS = ("pe", "act", "dve", "pool", "sp")
NDS = 12


class FW:
    def __init__(self, nc):
        self.nc = nc
        self.es = ExitStack()
        self.scope = self.es
        self.ncc = 0
        self.nphase = 0
        self.q = {e: [] for e in ENGS}
        self.cnt = {e: 0 for e in ENGS}
        self.dcnt = {e: 0 for e in ENGS}
        self.known = {e: {} for e in ENGS}
        self.lastw = {}
        self.readers = {}
        self.lastx = {}
        self.sems = {}
        self.out_deps = []
        self.n_wait = 0

    def sb(self, name, shape, dt):
        return self.scope.enter_context(self.nc.sbuf_tensor(name, list(shape), dt))

    def ps(self, name, shape, dt=F32):
        return self.scope.enter_context(self.nc.psum_tensor(name, list(shape), dt))

    def _sem(self, key):
        if key not in self.sems:
            self.sems[key] = self.es.enter_context(self.nc.semaphore("s_" + "_".join(map(str, key))))
        return self.sems[key]

    def _deps(self, eng, reads, writes, excl=()):
        deps = {}

        def add(d):
            if d is None:
                return
            sk, v, e = d
            if deps.get(sk, (0,))[0] < v:
                deps[sk] = (v, e)

        for k in reads:
            add(self.lastw.get(k))
        for k in writes:
            add(self.lastw.get(k))
            for d in self.readers.get(k, ()):
                add(d)
        for k in excl:
            d = self.lastx.get(k)
            if d is not None and d[2] != eng:
                add(d)
        out = []
        for sk, (v, e) in deps.items():
            if e == eng and eng == "pe":
                continue
            if self.known[eng].get(sk, 0) >= v:
                continue
            self.known[eng][sk] = v
            out.append((sk, v))
        return out

    def _commit(self, me, reads, writes):
        for k in writes:
            self.lastw[k] = me
            self.readers[k] = []
        for k in reads:
            self.readers.setdefault(k, []).append(me)

    def op(self, eng, fn, reads=(), writes=(), excl=()):
        waits = self._deps(eng, reads, writes, excl)
        self.cnt[eng] += 1
        me = (("e", eng), self.cnt[eng], eng)
        for k in excl:
            self.lastx[k] = me
        self.q[eng].append((waits, fn, ("e", eng), 1))
        self.n_wait += len(waits)
        self._commit(me, reads, writes)
        return me

    def dma(self, queue, out, in_, reads=(), writes=(), is_output=False):
        i = self.dcnt[queue]
        self.dcnt[queue] += 1
        sk = ("d", queue, i % NDS)
        val = 16 * (i // NDS + 1)
        waits = self._deps(queue, reads, writes)
        if i // NDS > 0 and self.known[queue].get(sk, 0) < val - 16:
            waits.append((sk, val - 16))
            self.known[queue][sk] = val - 16
        fn = lambda e, out=out, in_=in_: e.dma_start(out=out, in_=in_)
        self.q[queue].append((waits, fn, sk, 16))
        me = (sk, val, "dma")
        self._commit(me, reads, writes)
        if is_output:
            self.out_deps.append(me)
        return me

    def collective(self, kind, groups, src, dst, reads=(), writes=()):
        waits = self._deps("pool", reads, ())
        self.ncc += 1
        for k in writes:
            self.lastw[k] = (("cc",), self.ncc, "cc")
            self.readers[k] = []
        fn = lambda e: e.collective_compute(kind, mybir.AluOpType.bypass, replica_groups=groups,
                                            ins=[src.ap().opt()], outs=[dst.ap().opt()])
        self.q["pool"].append((waits, fn, ("cc",), 1))

    def begin_phase(self):
        self.scope = ExitStack()

    def end_phase(self, collective=None, wait_cc=True):
        nc = self.nc
        fin = []
        for q in ("sp", "pool"):
            for k in range(min(NDS, self.dcnt[q])):
                n_k = (self.dcnt[q] - k + NDS - 1) // NDS
                fin.append((("d", q, k), 16 * n_k))
        eng_obj = {"pe": "tensor", "act": "scalar", "dve": "vector", "pool": "gpsimd", "sp": "sync"}
        for e in ENGS:
            self._sem(("e", e))
        for (waits, fn, sk, inc) in [x for e in ENGS for x in self.q[e]]:
            self._sem(sk)
            for w in waits:
                self._sem(w[0])
        for f in fin:
            self._sem(f[0])
        colls = [] if collective is None else (collective if isinstance(collective, list) else [collective])
        ccsem = self._sem(("cc",))
        self.ncc += len(colls)
        cc_total = self.ncc
        self.nphase += 1
        with nc.Block("ph%d" % self.nphase) as block:
            for e in ENGS:
                items = self.q[e]

                def body(engine, items=items, e=e):
                    for (waits, fn, sk, inc) in items:
                        for (wsk, wv) in waits:
                            engine.wait_ge(self.sems[wsk], wv)
                        ins = fn(engine)
                        ins.then_inc(self.sems[sk], inc)
                    if e in ("pool", "sp"):
                        for (wsk, wv) in fin:
                            engine.wait_ge(self.sems[wsk], wv)
                    if e == "pool":
                        for (kind, groups, src, dst) in colls:
                            engine.collective_compute(kind, mybir.AluOpType.bypass, replica_groups=groups,
                                                      ins=[src.ap().opt()], outs=[dst.ap().opt()]).then_inc(ccsem, 1)
                        if cc_total > 0 and wait_cc:
                            engine.wait_ge(ccsem, cc_total)

                getattr(block, eng_obj[e])(body)
        self.q = {e: [] for e in ENGS}
        self.lastw = {} if wait_cc else {k: v for k, v in self.lastw.items() if v[2] == "cc"}
        self.readers = {}
        self.lastx = {}
        self.out_deps = []
        if self.scope is not self.es:
            self.scope.close()
            self.scope = self.es

    def emit(self):
        self.end_phase()

    def close(self):
        self.es.close()


D = 2048
HID = 5632
NJ = HID // 128
ALPHA = (2.0 * 4) ** 0.25
TOK = 1024
NT = TOK // 128


def build_T(F, xT_out_dt=F32):
    KC = F // 128
    nc = bass.Bass("TRN2", target_bir_lowering=False)
    oT_d = nc.dram_tensor("oT", [F, TOK], BF16, kind="ExternalInput").ap()
    xres_d = nc.dram_tensor("xres", [TOK, D], F32, kind="ExternalInput").ap()
    wo_d = nc.dram_tensor("w_o", [F, D], F32, kind="ExternalInput").ap()
    wgu_d = nc.dram_tensor("wgu", [NJ, 128, 16 * 256], F32, kind="ExternalInput").ap()
    wd_d = nc.dram_tensor("wd", [HID, D], F32, kind="ExternalInput").ap()
    ln_d = nc.dram_tensor("ln", [4, D], F32, kind="ExternalInput").ap()
    id_d = nc.dram_tensor("ident", [128, 128], F32, kind="ExternalInput").ap()
    xo_d = nc.dram_tensor("xo", [TOK, D], F32, kind="ExternalOutput").ap()
    xoT_d = nc.dram_tensor("xoT", [D, TOK], xT_out_dt, kind="ExternalOutput").ap()

    fw = FW(nc)
    emit_T(fw, F, lambda fw_, oT, KC_, tmp: fw_.dma("sp", oT[:, 0:KC_, :], oT_d.rearrange("(c p) t -> p c t", p=128), writes=[("oT",)]),
           xres_d, wo_d, wgu_d, wd_d, ln_d, id_d, xo_d,
           None, xT_out_dt)
    fw.emit()
    fw.close()
    return nc


def emit_T(fw, F, oT_loader, xres_d, wo_d, wgu_d, wd_d, ln_d, id_d, xo_d, xoT_d, xT_out_dt, pfx="T", post=None, U_ext=None, wo_preloaded=False, prefetch=None):
    nc = fw.nc
    KC = F // 128
    z = fw.sb(pfx + "z", [128, NT, D], F32)
    x1T = fw.sb(pfx + "x1T", [128, 16, TOK], BF16)
    gb = [fw.sb(pfx + "g", [128, D], F32), fw.sb(pfx + "b", [128, D], F32)]
    ident = fw.sb(pfx + "id", [128, 128], F32)
    stats = fw.sb(pfx + "st", [128, 4 * 6], F32)
    mv = fw.sb(pfx + "mv", [128, 2], F32)
    rstd = fw.sb(pfx + "rstd", [128, 1], F32)
    nmr = fw.sb(pfx + "nmr", [128, 1], F32)
    psb = [fw.ps(pfx + "ps%d" % i, [128, 512], F32) for i in range(8)]
    U = U_ext if U_ext is not None else fw.sb(pfx + "U", [128, 32768], BF16)
    oT = U[:, 0:16384].rearrange("p (c t) -> p c t", c=16)
    wob = [U[:, 16384 + i * 8192:16384 + (i + 1) * 8192].rearrange("p (c n) -> p c n", c=16) for i in range(2)]
    WG = 3
    wgb = [U[:, i * 4096:(i + 1) * 4096].rearrange("p (c n) -> p c n", c=16) for i in range(WG)]
    WD = 8
    wdb = [U[:, 12288 + i * 2048:12288 + (i + 1) * 2048] for i in range(WD)]
    hT = [fw.sb(pfx + "hT%d" % i, [128, 4, TOK], BF16) for i in range(2)]
    P1KEYS = [("oT",), ("wo", 0), ("wo", 1)]
    sg = [fw.sb(pfx + "sg%d" % i, [128, 512], F32) for i in range(2)]
    xt_st = [fw.sb(pfx + "xts%d" % i, [128, 4, 128], xT_out_dt) for i in range(2)]

    zk = lambda t: [("z", t, cb) for cb in range(4)]

    fw.dma("sp", ident[:], id_d, writes=[("id",)])
    for t in range(NT):
        fw.dma("sp", z[:, t, :], xres_d[t * 128:(t + 1) * 128, :], writes=zk(t))
    oT_loader(fw, oT, KC, [x1T[:, i, :] for i in range(16)])

    def load_gb(i):
        for k in range(2):
            fw.dma("sp", gb[k][:], ln_d[2 * i + k, :].partition_broadcast(128), writes=[("gb", k)])

    load_gb(0)

    def load_wo(cb):
        buf = wob[cb % 2]
        fw.dma("pool", buf[:, 0:KC, :], wo_d[:, cb * 512:(cb + 1) * 512].rearrange("(c p) n -> p c n", p=128),
               writes=[("wo", cb % 2)])

    def load_wg(j):
        fw.dma("pool", wgb[j % WG].rearrange("p c n -> p (c n)"), wgu_d[j],
               writes=[("wg", j % WG)] + (P1KEYS if j < WG else []))

    def load_wd(j):
        fw.dma("pool", wdb[j % WD], wd_d[j * 128:(j + 1) * 128, :],
               writes=[("wd", j % WD)] + (P1KEYS if j < WD else []))

    if not wo_preloaded:
        load_wo(0)
        load_wo(1)

    pi = [0]

    def nextps():
        p = pi[0] % 8
        pi[0] += 1
        return p

    for cb in range(4):
        buf = wob[cb % 2]
        for t in range(NT):
            p = nextps()
            for k in range(KC):
                fw.op("pe", lambda e, p=p, k=k, t=t, buf=buf: e.matmul(
                    psb[p][:], oT[:, k, t * 128:(t + 1) * 128], buf[:, k, :], start=(k == 0), stop=(k == KC - 1)),
                    reads=[("oT",), ("oT", k), ("wo", cb % 2)], writes=[("ps", p)])
            fw.op("dve", lambda e, p=p, t=t, cb=cb: e.scalar_tensor_tensor(
                out=z[:, t, cb * 512:(cb + 1) * 512], in0=z[:, t, cb * 512:(cb + 1) * 512], scalar=ALPHA,
                in1=psb[p][:], op0=ALU.mult, op1=ALU.add),
                reads=[("ps", p), ("z", t, cb)], writes=[("z", t, cb)])
        if cb + 2 < 4:
            load_wo(cb + 2)

    def layer_norm(t):
        for s in range(4):
            fw.op("dve", lambda e, t=t, s=s: e.bn_stats(out=stats[:, s * 6:(s + 1) * 6], in_=z[:, t, s * 512:(s + 1) * 512]),
                  reads=[("z", t, s)], writes=[("st", s)])
        fw.op("dve", lambda e: e.bn_aggr(out=mv[:], in_=stats[:]), reads=[("st", s) for s in range(4)], writes=[("mv",)])
        fw.op("act", lambda e: e.activation(out=rstd[:], in_=mv[:, 1:2], func=AF.Sqrt, bias=1e-5, scale=1.0),
              reads=[("mv",)], writes=[("rstd",)])
        fw.op("dve", lambda e: e.reciprocal(out=rstd[:], in_=rstd[:]), reads=[("rstd",)], writes=[("rstd",)])
        fw.op("dve", lambda e: e.scalar_tensor_tensor(out=nmr[:], in0=mv[:, 0:1], scalar=-1.0, in1=rstd[:],
                                                       op0=ALU.mult, op1=ALU.mult),
              reads=[("mv",), ("rstd",)], writes=[("nmr",)])
        fw.op("act", lambda e, t=t: e.activation(out=z[:, t, :], in_=z[:, t, :], func=AF.Identity, bias=nmr[:], scale=rstd[:]),
              reads=zk(t) + [("rstd",), ("nmr",)], writes=zk(t))
        fw.op("dve", lambda e, t=t: e.tensor_tensor(out=z[:, t, :], in0=z[:, t, :], in1=gb[0][:], op=ALU.mult),
              reads=zk(t) + [("gb", 0)], writes=zk(t))
        fw.op("dve", lambda e, t=t: e.tensor_tensor(out=z[:, t, :], in0=z[:, t, :], in1=gb[1][:], op=ALU.add),
              reads=zk(t) + [("gb", 1)], writes=zk(t))

    def transpose_tile(t, dst_fn):
        for cg in range(4):
            p = nextps()
            for cc in range(4):
                c = cg * 4 + cc
                fw.op("pe", lambda e, p=p, cc=cc, c=c, t=t: e.transpose(
                    out=psb[p][:, cc * 128:(cc + 1) * 128], in_=z[:, t, c * 128:(c + 1) * 128], identity=ident[:]),
                    reads=[("z", t, c // 4), ("id",)], writes=[("ps", p)])
            dst_fn(cg, p)

    for t in range(NT):
        layer_norm(t)

        def dst(cg, p, t=t):
            fw.op("act", lambda e: e.copy(out=x1T[:, cg * 4:(cg + 1) * 4, t * 128:(t + 1) * 128],
                                          in_=psb[p][:].rearrange("p (c n) -> p c n", c=4)),
                  reads=[("ps", p)], writes=[("x1T", t, cg)])
        transpose_tile(t, dst)
    load_gb(1)
    for j in range(WG):
        load_wg(j)
    for j in range(WD):
        load_wd(j)

    def up_gate(j):
        buf = wgb[j % WG]
        hb = hT[(j // 4) % 2]
        for half in range(2):
            pg = nextps()
            pu = nextps()
            for which, p in ((0, pg), (1, pu)):
                for c in range(16):
                    fw.op("pe", lambda e, p=p, c=c, which=which, half=half, buf=buf: e.matmul(
                        psb[p][:], buf[:, c, which * 128:(which + 1) * 128], x1T[:, c, half * 512:(half + 1) * 512],
                        start=(c == 0), stop=(c == 15)),
                        reads=[("wg", j % WG)] + [("x1T", t, c // 4) for t in range(half * 4, half * 4 + 4)],
                        writes=[("ps", p)])
            s = sg[half]
            fw.op("act", lambda e, s=s, pg=pg: e.activation(out=s[:], in_=psb[pg][:], func=AF.Silu),
                  reads=[("ps", pg)], writes=[("sg", half)])
            fw.op("dve", lambda e, s=s, pu=pu, hb=hb, half=half, j=j: e.tensor_tensor(
                out=hb[:, j % 4, half * 512:(half + 1) * 512], in0=s[:], in1=psb[pu][:], op=ALU.mult),
                reads=[("sg", half), ("ps", pu)], writes=[("hT", (j // 4) % 2, j % 4, half)])
        if j + WG < NJ:
            load_wg(j + WG)

    def down(G):
        hb = hT[G % 2]
        for t in range(NT):
            for cb in range(4):
                p = nextps()
                for jj in range(4):
                    j = G * 4 + jj
                    fw.op("pe", lambda e, p=p, jj=jj, j=j, t=t, cb=cb, hb=hb: e.matmul(
                        psb[p][:], hb[:, jj, t * 128:(t + 1) * 128], wdb[j % WD][:, cb * 512:(cb + 1) * 512],
                        start=(jj == 0), stop=(jj == 3)),
                        reads=[("hT", G % 2, jj, t // 4), ("wd", j % WD)], writes=[("ps", p)])
                if G == 0:
                    fw.op("dve", lambda e, p=p, t=t, cb=cb: e.scalar_tensor_tensor(
                        out=z[:, t, cb * 512:(cb + 1) * 512], in0=z[:, t, cb * 512:(cb + 1) * 512], scalar=ALPHA,
                        in1=psb[p][:], op0=ALU.mult, op1=ALU.add),
                        reads=[("ps", p), ("z", t, cb)], writes=[("z", t, cb)])
                else:
                    fw.op("dve", lambda e, p=p, t=t, cb=cb: e.tensor_tensor(
                        out=z[:, t, cb * 512:(cb + 1) * 512], in0=z[:, t, cb * 512:(cb + 1) * 512],
                        in1=psb[p][:], op=ALU.add),
                        reads=[("ps", p), ("z", t, cb)], writes=[("z", t, cb)])
        for jj in range(4):
            j = G * 4 + jj
            if j + WD < NJ:
                load_wd(j + WD)

    NG = NJ // 4
    for G in range(NG):
        for jj in range(4):
            up_gate(G * 4 + jj)
        if G >= 1:
            down(G - 1)
    down(NG - 1)
    if prefetch is not None:
        prefetch([("wg", i) for i in range(WG)] + [("wd", i) for i in range(WD)] + P1KEYS)

    for t in range(NT):
        layer_norm(t)
        fw.dma("sp", xo_d[t * 128:(t + 1) * 128, :], z[:, t, :], reads=zk(t), is_output=True)

        if xoT_d is None:
            continue

        def dst(cg, p, t=t):
            cbuf = x1T[:].rearrange("p c t -> p (c t)")[:, (t // 2) * 4096:(t // 2 + 1) * 4096].rearrange("p (c t) -> p c t", c=16)
            wk = [("xc", t // 2, cg, t % 2)]
            if t == 0 and cg == 0:
                wk = wk + [("x1T", tt, cgg) for tt in range(NT) for cgg in range(4)]
            fw.op("act", lambda e: e.copy(out=cbuf[:, cg * 4:(cg + 1) * 4, (t % 2) * 128:(t % 2) * 128 + 128],
                                          in_=psb[p][:].rearrange("p (c n) -> p c n", c=4)),
                  reads=[("ps", p)], writes=wk)
        transpose_tile(t, dst)
        if t % 2 == 1:
            c = t // 2
            fw.dma("sp", xoT_d(c), x1T[:].rearrange("p c t -> p (c t)")[:, c * 4096:(c + 1) * 4096],
                   reads=[("xc", c, cg, h) for cg in range(4) for h in range(2)], writes=[("xodram", c)], is_output=True)
            if post is not None:
                post(c)


D = 2048
S = 4096
NB = S // 512
NEG = -30000.0
ME_FM = [("q", 0, 128), ("k", 128, 128), ("g1", 256, 16), ("dq0", 272, 128), ("dq1", 400, 128), ("dq2", 528, 128),
         ("dk0", 656, 128), ("dk1", 784, 128), ("dk2", 912, 128)]
ME_TM0 = 1040
ME_TM1 = 1552
ME_NC = 2064
DIL_NT = (2, 5, 17)
DIL_OFF = (0, 2, 7)
DIL_TOT = 24
ME_NFG = [44, 59, 67, 75, 80, 80, 80, 80]


def emit_ME(fw, x_src, win_d, wg2_d, ng_d, cst_d, mask_d, idb_d, o_dst, pfx="E", post=None, win_ext=None, win_preloaded=False, tail_prefetch=None):
    nc = fw.nc
    win = win_ext if win_ext is not None else fw.sb(pfx + "win", [128, 16, ME_NC], BF16)
    xT = fw.sb(pfx + "xT", [128, 16, 512], BF16)
    dkT = [fw.sb(pfx + "dkT%d" % g, [128, S], BF16) for g in range(3)]
    dvh = [fw.sb(pfx + "dv%d" % g, [128, 32, 128], BF16) for g in range(3)]
    dqT2 = [[fw.sb(pfx + "dqT%d_%d" % (g, i), [128, 512], BF16) for g in range(3)] for i in range(2)]
    qT2 = [fw.sb(pfx + "qT%d" % i, [128, 512], F32) for i in range(2)]
    kT2 = [fw.sb(pfx + "kT%d" % i, [128, 512], F32) for i in range(2)]
    ktm2 = [fw.sb(pfx + "ktm%d" % i, [128, 4, 128], F32) for i in range(2)]
    vtm2 = [fw.sb(pfx + "vtm%d" % i, [128, 4, 256], BF16) for i in range(2)]
    gs2 = [fw.sb(pfx + "gs%d" % i, [128, 4, 256], F32) for i in range(2)]
    g1a2 = [fw.sb(pfx + "g1a%d" % i, [17, 512], F32) for i in range(2)]
    wg2 = fw.sb(pfx + "wg2", [17, 128], F32)
    ngb = fw.sb(pfx + "ngb", [128, 256], F32)
    cst = fw.sb(pfx + "cst", [128, 4, 128], F32)
    idb = fw.sb(pfx + "idb", [128, 128], BF16)
    mask = fw.sb(pfx + "mask", [128, DIL_TOT * 128], BF16)
    la = fw.sb(pfx + "la", [128, 128], F32)
    eb = fw.sb(pfx + "eb", [128, 128], F32)
    enb = fw.sb(pfx + "enb", [128, 128], F32)
    ee = fw.sb(pfx + "ee", [128, 128], F32)
    qdT = fw.sb(pfx + "qdT", [128, 128], BF16)
    kiT = fw.sb(pfx + "kiT", [128, 128], BF16)
    ken = fw.sb(pfx + "ken", [128, 128], BF16)
    scT = fw.sb(pfx + "scT", [128, 128], BF16)
    St = fw.sb(pfx + "S", [128, 256], F32)
    Sbf = fw.sb(pfx + "Sbf", [128, 256], BF16)
    junk = fw.sb(pfx + "junk", [128, 256], F32)
    sm = fw.sb(pfx + "sm", [128, 8], F32)
    on = fw.sb(pfx + "on", [128, 256], F32)
    ob = fw.sb(pfx + "ob", [128, 128], F32)
    sc = fw.sb(pfx + "sc", [128, DIL_TOT * 128], F32)
    P = fw.sb(pfx + "P", [128, DIL_TOT * 128], BF16)
    PT = [fw.sb(pfx + "PT%d" % i, [128, 512], BF16) for i in range(2)]
    oTa = fw.sb(pfx + "oTa", [128, 3, 512], BF16)
    psI = [fw.ps(pfx + "psI%d" % i, [128, 512], F32) for i in range(2)]
    psG = fw.ps(pfx + "psG", [128, 512], F32)
    psS = [fw.ps(pfx + "psS%d" % i, [128, 512], F32) for i in range(2)]
    psT = [fw.ps(pfx + "psT%d" % i, [128, 1024], BF16) for i in range(2)]
    psV = fw.ps(pfx + "psV", [128, 512], F32)
    identF = cst[:, 0, :]
    TriNeg = cst[:, 1, :]
    UNeg = cst[:, 2, :]
    causT = cst[:, 3, :]

    fw.dma("sp", cst[:], cst_d, writes=[("cst",)])
    fw.dma("sp", idb[:], idb_d, writes=[("idb",)])
    fw.dma("sp", mask[:], mask_d, writes=[("mask",)])
    fw.dma("sp", wg2[:], wg2_d, writes=[("wg2",)])
    fw.dma("sp", ngb[:], ng_d.partition_broadcast(128), writes=[("ngb",)])
    for i in range(2):
        fw.op("dve", lambda e, i=i: e.memset(g1a2[i][:], 1.0), writes=[("g1a", i)])

    def load_x(tb):
        for item in x_src(tb * 512, 512):
            c0, c1, tlo, thi, ap = item[:5]
            fw.dma("pool", xT[:, c0:c1, tlo:thi], ap, reads=list(item[5:]), writes=[("xT", c, tlo) for c in range(c0, c1)])

    load_x(0)
    for c4 in range(4):
        if win_preloaded:
            break
        fw.dma("pool", win[:, c4 * 4:(c4 + 1) * 4, :].rearrange("p c n -> p (c n)"),
               win_d[:, c4 * 4 * ME_NC:(c4 + 1) * 4 * ME_NC], writes=[("win", c4)])
    WINK = [("win", i) for i in range(4)]
    XK = [("xT", c, tlo) for c in range(16) for tlo in range(0, 512, 256)]

    ipi = [0]

    def evac(eng, out, in_, bank, reads, writes, func=None, scale=None):
        ex = [("PS", bank)]
        if eng == "act":
            if func is None and scale is None:
                fw.op("act", lambda e: e.copy(out=out, in_=in_), reads=reads, writes=writes, excl=ex)
            else:
                fw.op("act", lambda e: e.activation(out=out, in_=in_, func=(func or AF.Copy),
                                                    scale=(1.0 if scale is None else scale)), reads=reads, writes=writes, excl=ex)
        else:
            if scale is None:
                fw.op("dve", lambda e: e.tensor_copy(out=out, in_=in_), reads=reads, writes=writes, excl=ex)
            else:
                fw.op("dve", lambda e: e.tensor_scalar(out=out, in0=in_, scalar1=scale, scalar2=None, op0=ALU.mult),
                      reads=reads, writes=writes, excl=ex)

    def inproj(tb):
        bs = tb % 2
        qT, kT, ktm, vtm, gs, g1a, dqT = qT2[bs], kT2[bs], ktm2[bs], vtm2[bs], gs2[bs], g1a2[bs], dqT2[bs]
        for gi, (name, off, m) in enumerate(ME_FM):
            p = ipi[0] % 2
            ipi[0] += 1
            bk = "I%d" % p
            for c in range(16):
                fw.op("pe", lambda e, p=p, c=c, off=off, m=m: e.matmul(
                    psI[p][0:m, :], win[:, c, off:off + m], xT[:, c, :], start=(c == 0), stop=(c == 15)),
                    reads=WINK + XK, writes=[("psI", p)], excl=[("PS", bk)])
            yield
            src = psI[p][0:m, :]
            eng = "act" if gi % 2 == 0 else "dve"
            rd = [("psI", p)]
            if name == "q":
                evac(eng, qT[:], src, bk, rd, [("qT", bs)])
            elif name == "k":
                evac(eng, kT[:], src, bk, rd, [("kT", bs)])
            elif name == "g1":
                evac(eng, g1a[0:16, :], src, bk, rd, [("g1a", bs)])
            elif name.startswith("dq"):
                g = int(name[2])
                evac(eng, dqT[g][:], src, bk, rd, [("dqT", g, bs)], scale=128.0 ** -0.5)
            else:
                g = int(name[2])
                evac(eng, dkT[g][:, tb * 512:(tb + 1) * 512], src, bk, rd, [("dkT", g, tb)])
        for tt in range(4):
            kt = tb * 4 + tt
            for half, off in ((0, ME_TM0), (1, ME_TM1)):
                p = ipi[0] % 2
                ipi[0] += 1
                bk = "I%d" % p
                for c in range(16):
                    fw.op("pe", lambda e, p=p, c=c, off=off, tt=tt: e.matmul(
                        psI[p][:], xT[:, c, tt * 128:(tt + 1) * 128], win[:, c, off:off + 512],
                        start=(c == 0), stop=(c == 15)),
                        reads=WINK + XK, writes=[("psI", p)], excl=[("PS", bk)])
                yield
                rd = [("psI", p)]
                if half == 0:
                    evac("dve", ktm[:, tt, :], psI[p][:, 0:128], bk, rd, [("ktm", bs, tt)])
                    evac("dve", vtm[:, tt, :], psI[p][:, 128:384], bk, rd, [("vtm", bs, tt)])
                    evac("dve", dvh[0][:, kt, :], psI[p][:, 384:512], bk, rd, [("dv", 0, kt)])
                else:
                    evac("act", gs[:, tt, :], psI[p][:, 0:256], bk, rd, [("gs", bs, tt)], func=AF.Silu)
                    evac("act", dvh[1][:, kt, :], psI[p][:, 256:384], bk, rd, [("dv", 1, kt)])
                    evac("act", dvh[2][:, kt, :], psI[p][:, 384:512], bk, rd, [("dv", 2, kt)])
                    fw.op("dve", lambda e, tt=tt: e.tensor_tensor(out=gs[:, tt, :], in0=gs[:, tt, :], in1=ngb[:], op=ALU.mult),
                          reads=[("gs", bs, tt), ("ngb",)], writes=[("gs", bs, tt)])

    GX = [("PS", "G")]

    def gla_chunk(c):
        cc = c % 4
        bs = (c // 4) % 2
        qT, kT, ktm, vtm, gs, g1a = qT2[bs], kT2[bs], ktm2[bs], vtm2[bs], gs2[bs], g1a2[bs]
        ts = slice(cc * 128, (cc + 1) * 128)
        fw.op("pe", lambda e: e.matmul(psG[:, 0:128], g1a[0:17, ts], wg2[0:17, :], start=True, stop=True),
              reads=[("g1a", bs), ("wg2",)], writes=[("pG", 0)], excl=GX)
        fw.op("act", lambda e: e.activation(out=la[:], in_=psG[:, 0:128], func=AF.Exp, scale=-1.0),
              reads=[("pG", 0)], writes=[("la",)], excl=GX)
        fw.op("act", lambda e: e.activation(out=la[:], in_=la[:], func=AF.Ln, bias=1.0, scale=1.0),
              reads=[("la",)], writes=[("la",)])
        yield
        fw.op("pe", lambda e: e.matmul(psG[:, 128:256], la[:], TriNeg, start=True, stop=True),
              reads=[("la",), ("cst",)], writes=[("pG", 1)], excl=GX)
        fw.op("pe", lambda e: e.matmul(psG[:, 256:384], UNeg, la[:], start=True, stop=True),
              reads=[("la",), ("cst",)], writes=[("pG", 2)], excl=GX)
        fw.op("act", lambda e: e.activation(out=eb[:], in_=psG[:, 128:256], func=AF.Exp), reads=[("pG", 1)], writes=[("eb",)], excl=GX)
        fw.op("act", lambda e: e.activation(out=enb[:], in_=psG[:, 128:256], func=AF.Exp, scale=-1.0),
              reads=[("pG", 1)], writes=[("enb",)], excl=GX)
        fw.op("act", lambda e: e.activation(out=ee[:], in_=psG[:, 256:384], func=AF.Exp), reads=[("pG", 2)], writes=[("ee",)], excl=GX)
        fw.op("dve", lambda e: e.scalar_tensor_tensor(out=qdT[:], in0=qT[:, ts], scalar=128.0 ** -0.5, in1=eb[:],
                                                       op0=ALU.mult, op1=ALU.mult),
              reads=[("qT", bs), ("eb",)], writes=[("qdT",)])
        fw.op("dve", lambda e: e.tensor_tensor(out=kiT[:], in0=kT[:, ts], in1=enb[:], op=ALU.mult),
              reads=[("kT", bs), ("enb",)], writes=[("kiT",)])
        fw.op("dve", lambda e: e.tensor_tensor(out=ken[:], in0=ktm[:, cc, :], in1=ee[:], op=ALU.mult),
              reads=[("ktm", bs, cc), ("ee",)], writes=[("ken",)])
        yield
        fw.op("pe", lambda e: e.matmul(psG[:, 384:512], kiT[:], qdT[:], start=True, stop=True),
              reads=[("kiT",), ("qdT",)], writes=[("pG", 3)], excl=GX)
        fw.op("dve", lambda e: e.tensor_tensor(out=scT[:], in0=psG[:, 384:512], in1=causT, op=ALU.mult),
              reads=[("pG", 3), ("cst",)], writes=[("scT",)], excl=GX)
        yield
        fw.op("pe", lambda e: e.matmul(psG[:, 0:256], scT[:], vtm[:, cc, :], start=True, stop=(c == 0)),
              reads=[("scT",), ("vtm", bs, cc)], writes=[("pG", 0), ("pG", 1)], excl=GX)
        if c > 0:
            fw.op("pe", lambda e: e.matmul(psG[:, 0:256], qdT[:], Sbf[:], start=False, stop=True),
                  reads=[("qdT",), ("Sbf",)], writes=[("pG", 0), ("pG", 1)], excl=GX)
        fw.op("pe", lambda e: e.matmul(psG[:, 256:512], ken[:], vtm[:, cc, :], start=True, stop=True),
              reads=[("ken",), ("vtm", bs, cc)], writes=[("pG", 2), ("pG", 3)], excl=GX)
        if c == 0:
            fw.op("dve", lambda e: e.tensor_copy(out=St[:], in_=psG[:, 256:512]), reads=[("pG", 2), ("pG", 3)], writes=[("S",)], excl=GX)
        else:
            fw.op("dve", lambda e: e.scalar_tensor_tensor(out=St[:], in0=St[:], scalar=eb[:, 127:128], in1=psG[:, 256:512],
                                                           op0=ALU.mult, op1=ALU.add),
                  reads=[("S",), ("eb",), ("pG", 2), ("pG", 3)], writes=[("S",)], excl=GX)
        yield
        fw.op("act", lambda e: e.activation(out=junk[:], in_=psG[:, 0:256], func=AF.Square, accum_out=sm[:, 0:1]),
              reads=[("pG", 0), ("pG", 1)], writes=[("junk",), ("sm", 0)], excl=GX)
        fw.op("act", lambda e: e.copy(out=Sbf[:], in_=St[:]), reads=[("S",)], writes=[("Sbf",)])
        fw.op("act", lambda e: e.activation(out=sm[:, 1:2], in_=sm[:, 0:1], func=AF.Sqrt, bias=1e-6, scale=1.0 / 256),
              reads=[("sm", 0)], writes=[("sm", 1)])
        fw.op("dve", lambda e: e.reciprocal(out=sm[:, 2:3], in_=sm[:, 1:2]), reads=[("sm", 1)], writes=[("sm", 2)])
        fw.op("dve", lambda e: e.scalar_tensor_tensor(out=on[:], in0=psG[:, 0:256], scalar=sm[:, 2:3], in1=gs[:, cc, :],
                                                       op0=ALU.mult, op1=ALU.mult),
              reads=[("pG", 0), ("pG", 1), ("sm", 2), ("gs", bs, cc)], writes=[("on",)], excl=GX)
        for e2 in range(2):
            fw.op("pe", lambda e, e2=e2: e.transpose(out=psG[:, e2 * 128:(e2 + 1) * 128],
                                                     in_=on[:, e2 * 128:(e2 + 1) * 128], identity=identF),
                  reads=[("on",), ("cst",)], writes=[("pG", e2)], excl=GX)
        fw.op("act", lambda e: e.copy(out=oTa[:, 0:2, ts], in_=psG[:, 0:256].rearrange("p (a n) -> p a n", a=2)),
              reads=[("pG", 0), ("pG", 1)], writes=[("oTa", 0, cc)], excl=GX)
        yield

    sci = [0]
    pti = [0]
    SCK = [("sc", g, k) for g in range(3) for k in range(5)]
    VX = [("PS", "V")]

    def dil_tile(qt):
        qc = qt % 4
        bs = (qt // 4) % 2
        dqT = dqT2[bs]
        if qt < 16:
            fw.op("dve", lambda e: e.memset(sc[:], NEG), writes=SCK)
        tiles = []
        for g in range(3):
            nt = DIL_NT[g]
            lo = max(0, qt - nt + 1)
            nvt = qt - lo + 1
            dst0 = (DIL_OFF[g] + nt - nvt) * 128
            for i in range(nvt):
                tiles.append((g, lo + i, dst0 + i * 128))
            n = nvt * 128
            done = 0
            while done < n:
                m = min(512, n - done)
                p = sci[0] % 2
                sci[0] += 1
                k0 = lo * 128 + done
                d0 = dst0 + done
                tbs = sorted(set(range(k0 // 512, (k0 + m - 1) // 512 + 1)))
                fw.op("pe", lambda e, p=p, g=g, m=m, k0=k0: e.matmul(
                    psS[p][:, 0:m], dqT[g][:, qc * 128:(qc + 1) * 128], dkT[g][:, k0:k0 + m], start=True, stop=True),
                    reads=[("dqT", g, bs)] + [("dkT", g, t) for t in tbs], writes=[("psS", p)], excl=[("PS", "S%d" % p)])
                fw.op("dve", lambda e, p=p, m=m, d0=d0: e.tensor_tensor(
                    out=sc[:, d0:d0 + m], in0=psS[p][:, 0:m], in1=mask[:, d0:d0 + m], op=ALU.add),
                    reads=[("psS", p), ("mask",)], writes=[("sc", g, done // 512)], excl=[("PS", "S%d" % p)])
                done += m
                yield
        fw.op("dve", lambda e: e.reduce_max(out=sm[:, 3:4], in_=sc[:], axis=AX.X), reads=SCK, writes=[("sm", 3)])
        fw.op("dve", lambda e: e.tensor_scalar(out=sm[:, 4:5], in0=sm[:, 3:4], scalar1=-1.0, scalar2=None, op0=ALU.mult),
              reads=[("sm", 3)], writes=[("sm", 4)])
        fw.op("act", lambda e: e.activation(out=P[:], in_=sc[:], func=AF.Exp, bias=sm[:, 4:5], scale=1.0, accum_out=sm[:, 5:6]),
              reads=SCK + [("sm", 4)], writes=[("P",), ("sm", 5)])
        fw.op("dve", lambda e: e.reciprocal(out=sm[:, 6:7], in_=sm[:, 5:6]), reads=[("sm", 5)], writes=[("sm", 6)])
        yield
        ntile = len(tiles)
        for g0 in range(0, ntile, 4):
            grp = tiles[g0:g0 + 4]
            h = pti[0] % 2
            pti[0] += 1
            tx = [("PS", "T%d" % h)]
            for k, (g, kt, col) in enumerate(grp):
                fw.op("pe", lambda e, k=k, col=col, h=h: e.transpose(
                    out=psT[h][:, k * 128:(k + 1) * 128], in_=P[:, col:col + 128], identity=idb[:]),
                    reads=[("P",), ("idb",)], writes=[("psT", h)], excl=tx)
            n = len(grp) * 128
            evac("act" if h == 0 else "dve", PT[h][:, 0:n], psT[h][:, 0:n], "T%d" % h, [("psT", h)], [("PT", h)])
            for k, (g, kt, col) in enumerate(grp):
                idx = g0 + k
                fw.op("pe", lambda e, k=k, g=g, kt=kt, h=h, idx=idx: e.matmul(
                    psV[:, 0:128], PT[h][:, k * 128:(k + 1) * 128], dvh[g][:, kt, :],
                    start=(idx == 0), stop=(idx == ntile - 1)),
                    reads=[("PT", h), ("dv", g, kt)], writes=[("pV", 0)], excl=VX)
            yield
        fw.op("dve", lambda e: e.tensor_scalar(out=ob[:], in0=psV[:, 0:128], scalar1=sm[:, 6:7], scalar2=None, op0=ALU.mult),
              reads=[("pV", 0), ("sm", 6)], writes=[("ob",)], excl=VX)
        fw.op("pe", lambda e: e.transpose(out=psV[:, 128:256], in_=ob[:], identity=identF),
              reads=[("ob",), ("cst",)], writes=[("pV", 1)], excl=VX)
        fw.op("act", lambda e: e.copy(out=oTa[:, 2, qc * 128:(qc + 1) * 128], in_=psV[:, 128:256]),
              reads=[("pV", 1)], writes=[("oTa", 1, qc)], excl=VX)

    def mixers(tb):
        for cc in range(4):
            yield from gla_chunk(tb * 4 + cc)
            yield from dil_tile(tb * 4 + cc)

    for _ in inproj(0):
        pass
    load_x(1)
    for tb in range(NB):
        bg = inproj(tb + 1) if tb + 1 < NB else iter(())
        bg_live = [tb + 1 < NB]

        def bg_step(tb=tb):
            if not bg_live[0]:
                return
            try:
                next(bg)
            except StopIteration:
                bg_live[0] = False
                if tb + 2 < NB:
                    load_x(tb + 2)
                elif tail_prefetch is not None:
                    tail_prefetch(WINK)

        n = 0
        step = max(1, int(0.5 * ME_NFG[tb] / 17))
        for _ in mixers(tb):
            n += 1
            if n % step == 0:
                bg_step()
        while bg_live[0]:
            bg_step()
        fw.dma("sp", o_dst(tb * 512, 512), oTa[:],
               reads=[("oTa", a, q) for a in range(2) for q in range(4)], writes=[("odram", tb * 512)], is_output=True)
        if post is not None:
            post(tb * 512, 512)


def me_consts():
    import ml_dtypes
    j = np.arange(128)[:, None]
    i = np.arange(128)[None, :]
    cst = np.zeros((128, 4, 128), np.float32)
    cst[:, 0, :] = np.eye(128)
    cst[:, 1, :] = np.where(j <= i, -1.0 / 16, 0.0)
    cst[:, 2, :] = np.where(j > i, -1.0 / 16, 0.0)
    cst[:, 3, :] = np.where(j <= i, 1.0, 0.0)
    mask = np.full((128, DIL_TOT * 128), NEG, np.float32)
    qi = np.arange(128)[:, None]
    kj = np.arange(128)[None, :]
    for g, d in enumerate((1, 4, 16)):
        nt = DIL_NT[g]
        for Dt in range(nt):
            delta = Dt * 128 + qi - kj
            ok = (delta >= 0) & (delta <= 128 * d) & (delta % d == 0)
            pos = DIL_OFF[g] + nt - 1 - Dt
            mask[:, pos * 128:(pos + 1) * 128] = np.where(ok, 0.0, NEG)
    return cst, mask.astype(ml_dtypes.bfloat16), np.eye(128, dtype=np.float32).astype(ml_dtypes.bfloat16)


def me_win_layout(w_in, h):
    def cols(a, n):
        return w_in[:, a:a + n]
    gq = cols(0 + h * 128, 128)
    gk = cols(512 + h * 128, 128)
    gv = cols(1024 + h * 256, 256)
    gg = cols(2048, 16)
    gr = cols(2064 + h * 256, 256)
    dq = [cols(3088 + g * 512 + h * 128, 128) for g in range(3)]
    dk = [cols(4624 + g * 512 + h * 128, 128) for g in range(3)]
    dv = [cols(6160 + g * 512 + h * 128, 128) for g in range(3)]
    w = np.concatenate([gq, gk, gg] + dq + dk + [gk, gv, dv[0], gr, dv[1], dv[2]], axis=1)
    assert w.shape[1] == ME_NC
    return np.ascontiguousarray(w.reshape(16, 128, ME_NC).transpose(1, 0, 2).reshape(128, 16 * ME_NC))


def build_ME(x_dt=F32):
    nc = bass.Bass("TRN2", target_bir_lowering=False)
    xT_d = nc.dram_tensor("xT", [D, S], x_dt, kind="ExternalInput").ap()
    win_d = nc.dram_tensor("win", [128, 16 * ME_NC], F32, kind="ExternalInput").ap()
    wg2_d = nc.dram_tensor("wg2", [17, 128], F32, kind="ExternalInput").ap()
    ng_d = nc.dram_tensor("ng", [256], F32, kind="ExternalInput").ap()
    cst_d = nc.dram_tensor("cst", [128, 4, 128], F32, kind="ExternalInput").ap()
    mask_d = nc.dram_tensor("mask", [128, DIL_TOT * 128], BF16, kind="ExternalInput").ap()
    idb_d = nc.dram_tensor("idb", [128, 128], BF16, kind="ExternalInput").ap()
    oT_d = nc.dram_tensor("oT", [384, S], BF16, kind="ExternalOutput").ap()
    fw = FW(nc)
    emit_ME(fw, lambda t0, n: [(0, 16, tl, tl + 256, xT_d[:, t0 + tl:t0 + tl + 256].rearrange("(c p) t -> p c t", p=128)) for tl in range(0, n, 256)], win_d, wg2_d, ng_d, cst_d, mask_d, idb_d,
            lambda t0, n: oT_d[:, t0:t0 + n].rearrange("(a p) t -> p a t", p=128))
    fw.emit()
    fw.close()
    return nc

import math

D = 2048
S = 4096
TB = 256
NTB = TB // 128
NBO = S // TB
NEG = -30000.0
MO_FM = [("cq", 0, 128), ("ck", 128, 128), ("ci", 256, 128), ("cf", 384, 128)] + \
        [("mq%d" % i, 512 + i * 128, 128) for i in range(4)] + [("mkv%d" % i, 1024 + i * 128, 128) for i in range(4)] + \
        [("krA", 1536, 64), ("krB", 1600, 64)]
MO_TM = 1664
MO_NC = 2176
QSCALE = (128 + 64) ** -0.5
TWO_PI = 2 * math.pi
C1 = 6.28125
C2 = TWO_PI - C1


def emit_MO(fw, x_src, win_d, wuq_d, wukv_d, cols_d, ng_d, pos_d, cst_d, idb_d, o_dst, pfx="O", post=None, win_ext=None, win_preloaded=False, tail_prefetch=None):
    win = win_ext if win_ext is not None else fw.sb(pfx + "win", [128, 16, MO_NC], BF16)
    xT = fw.sb(pfx + "xT", [128, 16, TB], BF16)
    wuq = fw.sb(pfx + "wuq", [128, 4, 512], BF16)
    wukv = fw.sb(pfx + "wukv", [128, 4, 512], BF16)
    knT = [fw.sb(pfx + "knT%d" % h, [128, S], BF16) for h in range(2)]
    krT = fw.sb(pfx + "krT", [64, S], BF16)
    vh = fw.sb(pfx + "vh", [128, 32, 256], BF16)
    sc = fw.sb(pfx + "sc", [128, S], F32)
    P = fw.sb(pfx + "P", [128, S], BF16)
    PT = [fw.sb(pfx + "PT%d" % i, [128, 512], BF16) for i in range(2)]
    mqT = fw.sb(pfx + "mqT", [128, 4, TB], BF16)
    mkvT = fw.sb(pfx + "mkvT", [128, 4, TB], BF16)
    sq = fw.sb(pfx + "sq", [128, 4, TB], BF16)
    rsq = fw.sb(pfx + "rsq", [128, TB], F32)
    rskv = fw.sb(pfx + "rskv", [128, TB], F32)
    qn2 = [[fw.sb(pfx + "qn%d_%d" % (h, i), [128, TB], BF16) for h in range(2)] for i in range(2)]
    qr2 = [[fw.sb(pfx + "qr%d_%d" % (h, i), [64, TB], BF16) for h in range(2)] for i in range(2)]
    CS = fw.sb(pfx + "CS", [64, TB], F32)
    SS = fw.sb(pfx + "SS", [64, TB], F32)
    posi = fw.sb(pfx + "posi", [64, TB], I32)
    th = fw.sb(pfx + "th", [64, TB], F32)
    ru = fw.sb(pfx + "ru", [64, TB], F32)
    rk = fw.sb(pfx + "rk", [64, TB], F32)
    rki = fw.sb(pfx + "rki", [64, TB], I32)
    rt1 = fw.sb(pfx + "rt1", [64, TB], F32)
    rt2 = fw.sb(pfx + "rt2", [64, TB], F32)
    ubq = fw.sb(pfx + "ubq", [128, 3 + TB], F32)
    ubk = fw.sb(pfx + "ubk", [128, 3 + TB], F32)
    acc = fw.sb(pfx + "acc", [128, TB], F32)
    qTb2 = [fw.sb(pfx + "qTb%d" % i, [128, TB], BF16) for i in range(2)]
    kTb2 = [fw.sb(pfx + "kTb%d" % i, [128, TB], BF16) for i in range(2)]
    liB2 = [fw.sb(pfx + "liB%d" % i, [128, TB], F32) for i in range(2)]
    spB2 = [fw.sb(pfx + "spB%d" % i, [128, TB], F32) for i in range(2)]
    vtm2 = [fw.sb(pfx + "vtm%d" % i, [128, NTB, 257], BF16) for i in range(2)]
    gsig2 = [fw.sb(pfx + "gsig%d" % i, [128, NTB, 256], F32) for i in range(2)]
    ngb = fw.sb(pfx + "ngb", [128, 256], F32)
    cols = fw.sb(pfx + "cols", [128, 32], F32)
    cst = fw.sb(pfx + "cst", [128, 4, 128], F32)
    idb = fw.sb(pfx + "idb", [128, 128], BF16)
    onesb = fw.sb(pfx + "onesb", [128, 128], BF16)
    bB = fw.sb(pfx + "bB", [128, 128], F32)
    GB = fw.sb(pfx + "GB", [128, 128], F32)
    DT = fw.sb(pfx + "DT", [128, 128], F32)
    eBt = fw.sb(pfx + "eB", [128, 128], F32)
    qkDT = fw.sb(pfx + "qkDT", [128, 128], BF16)
    qsT = fw.sb(pfx + "qsT", [128, 128], BF16)
    kw = fw.sb(pfx + "kw", [128, 128], BF16)
    Ca = fw.sb(pfx + "Ca", [128, 257], F32)
    Cab = fw.sb(pfx + "Cab", [128, 257], BF16)
    hh = fw.sb(pfx + "hh", [128, 256], F32)
    junk = fw.sb(pfx + "junk", [128, 128], F32)
    od = fw.sb(pfx + "od", [128, 128], F32)
    sm = [fw.sb(pfx + "sm%d" % i, [128, 24], F32) for i in range(2)]
    am = fw.sb(pfx + "am", [128, 8], F32)
    junk2 = fw.sb(pfx + "junk2", [128, 128], F32)
    st6 = fw.sb(pfx + "st6", [128, 6], F32)
    oTa = fw.sb(pfx + "oTa", [128, 4, TB], BF16)
    psI = [fw.ps(pfx + "psI%d" % i, [128, 512], F32) for i in range(2)]
    psG = fw.ps(pfx + "psG", [128, 512], F32)
    psS = [fw.ps(pfx + "psS%d" % i, [128, 512], F32) for i in range(2)]
    psT = [fw.ps(pfx + "psT%d" % i, [128, 1024], BF16) for i in range(2)]
    psV = fw.ps(pfx + "psV", [128, 512], F32)
    identF = cst[:, 0, :]
    causT = cst[:, 1, :]
    causAdd = cst[:, 2, :]
    onesF = cst[:, 3, :]
    CW = lambda i: cols[:, i:i + 1]

    fw.dma("sp", cst[:], cst_d, writes=[("cst",)])
    fw.dma("sp", idb[:], idb_d, writes=[("idb",)])
    fw.dma("sp", cols[:], cols_d, writes=[("cols",)])
    fw.dma("sp", ngb[:], ng_d.partition_broadcast(128), writes=[("ngb",)])
    for i in range(2):
        fw.op("dve", lambda e, i=i: e.memset(vtm2[i][:], 1.0), writes=[("vtm", i, t) for t in range(NTB)])
    fw.op("dve", lambda e: e.memset(onesb[:], 1.0), writes=[("onesb",)])
    fw.op("dve", lambda e: e.memset(ubq[:, 0:3], 0.0), writes=[("ubq",)])
    fw.op("dve", lambda e: e.memset(ubk[:, 0:3], 0.0), writes=[("ubk",)])

    def load_x(tb):
        for item in x_src(tb * TB, TB):
            c0, c1, tlo, thi, ap = item[:5]
            fw.dma("pool", xT[:, c0:c1, tlo:thi], ap, reads=list(item[5:]), writes=[("xT", c, tlo) for c in range(c0, c1)])

    def load_pos(tb):
        fw.dma("sp", posi[:], pos_d[tb * TB:(tb + 1) * TB].partition_broadcast(64), writes=[("posi",)])

    load_x(0)
    load_pos(0)
    for c4 in range(4):
        if win_preloaded:
            break
        fw.dma("pool", win[:, c4 * 4:(c4 + 1) * 4, :].rearrange("p c n -> p (c n)"),
               win_d[:, c4 * 4 * MO_NC:(c4 + 1) * 4 * MO_NC], writes=[("win", c4)])
    fw.dma("pool", wuq[:].rearrange("p c n -> p (c n)"), wuq_d, writes=[("wuq",)])
    fw.dma("pool", wukv[:].rearrange("p c n -> p (c n)"), wukv_d, writes=[("wukv",)])
    WINK = [("win", i) for i in range(4)]
    XK = [("xT", c, tlo) for c in range(16) for tlo in range(0, TB, 256)]
    ipi = [0]

    def nextI():
        p = ipi[0] % 2
        ipi[0] += 1
        return p

    def rope_tables(tb):
        fw.op("dve", lambda e: e.tensor_copy(out=th[:], in_=posi[:]), reads=[("posi",)], writes=[("th",)])
        fw.op("dve", lambda e: e.tensor_scalar(out=th[:], in0=th[:], scalar1=CW(12)[0:64, :], scalar2=None, op0=ALU.mult),
              reads=[("th",), ("cols",)], writes=[("th",)])
        for (dst, ph, key) in ((CS, 13, "CS"), (SS, 14, "SS")):
            fw.op("dve", lambda e, ph=ph: e.tensor_scalar(out=ru[:], in0=th[:], scalar1=CW(ph)[0:64, :], scalar2=None, op0=ALU.add),
                  reads=[("th",), ("cols",)], writes=[("ru",)])
            fw.op("dve", lambda e: e.tensor_scalar(out=rk[:], in0=ru[:], scalar1=1.0 / TWO_PI, scalar2=None, op0=ALU.mult),
                  reads=[("ru",)], writes=[("rk",)])
            fw.op("dve", lambda e: e.tensor_copy(out=rki[:], in_=rk[:]), reads=[("rk",)], writes=[("rki",)])
            fw.op("dve", lambda e: e.tensor_copy(out=rk[:], in_=rki[:]), reads=[("rki",)], writes=[("rk",)])
            fw.op("dve", lambda e: e.scalar_tensor_tensor(out=ru[:], in0=rk[:], scalar=-C1, in1=ru[:], op0=ALU.mult, op1=ALU.add),
                  reads=[("rk",), ("ru",)], writes=[("ru",)])
            fw.op("dve", lambda e: e.scalar_tensor_tensor(out=ru[:], in0=rk[:], scalar=-C2, in1=ru[:], op0=ALU.mult, op1=ALU.add),
                  reads=[("rk",), ("ru",)], writes=[("ru",)])
            fw.op("dve", lambda e: e.tensor_scalar(out=rk[:], in0=ru[:], scalar1=math.pi, scalar2=-TWO_PI, op0=ALU.is_gt, op1=ALU.mult),
                  reads=[("ru",)], writes=[("rk",)])
            fw.op("dve", lambda e: e.tensor_tensor(out=ru[:], in0=ru[:], in1=rk[:], op=ALU.add), reads=[("ru",), ("rk",)], writes=[("ru",)])
            fw.op("dve", lambda e: e.tensor_scalar(out=rk[:], in0=ru[:], scalar1=-math.pi, scalar2=TWO_PI, op0=ALU.is_lt, op1=ALU.mult),
                  reads=[("ru",)], writes=[("rk",)])
            fw.op("dve", lambda e: e.tensor_tensor(out=ru[:], in0=ru[:], in1=rk[:], op=ALU.add), reads=[("ru",), ("rk",)], writes=[("ru",)])
            fw.op("dve", lambda e: e.tensor_scalar(out=ru[:], in0=ru[:], scalar1=-3.1415925, scalar2=3.1415925, op0=ALU.max, op1=ALU.min),
                  reads=[("ru",)], writes=[("ru",)])
            fw.op("act", lambda e, dst=dst: e.activation(out=dst[:], in_=ru[:], func=AF.Sin), reads=[("ru",)], writes=[(key,)])

    def rope_apply(pA, bkA, pB, bkB, dst, dkey, extra=None):
        fw.op("dve", lambda e: e.tensor_tensor(out=rt1[:], in0=psI[pA][0:64, 0:TB], in1=CS[:], op=ALU.mult),
              reads=[("psI", pA), ("CS",)], writes=[("rt1",)], excl=[("PS", bkA)])
        fw.op("dve", lambda e: e.tensor_tensor(out=rt2[:], in0=psI[pB][0:64, 0:TB], in1=SS[:], op=ALU.mult),
              reads=[("psI", pB), ("SS",)], writes=[("rt2",)], excl=[("PS", bkB)])
        if extra is None:
            fw.op("dve", lambda e: e.tensor_tensor(out=dst, in0=rt1[:], in1=rt2[:], op=ALU.add),
                  reads=[("rt1",), ("rt2",)], writes=[dkey])
        else:
            fw.op("dve", lambda e: e.tensor_tensor(out=rt1[:], in0=rt1[:], in1=rt2[:], op=ALU.add),
                  reads=[("rt1",), ("rt2",)], writes=[("rt1",)])
            fw.op("dve", lambda e: e.scalar_tensor_tensor(out=dst, in0=rt1[:], scalar=QSCALE, in1=extra[0:64, :], op0=ALU.mult, op1=ALU.mult),
                  reads=[("rt1",), ("rsq",)], writes=[dkey])

    def rstd_rows(src_key, dst, dkey):
        fw.op("act", lambda e: e.activation(out=dst[:], in_=psG[:, 0:TB], func=AF.Sqrt, bias=1e-6, scale=1.0 / 512),
              reads=[("pG",)], writes=[dkey], excl=[("PS", "G")])
        fw.op("dve", lambda e: e.reciprocal(out=dst[:], in_=dst[:]), reads=[dkey], writes=[dkey])

    def inproj(tb):
        t0 = tb * TB
        bs = tb % 2
        liB, spB, vtm, gsig = liB2[bs], spB2[bs], vtm2[bs], gsig2[bs]
        for gi, (name, off, m) in enumerate(MO_FM):
            if name == "krA":
                continue
            if name == "krB":
                pA, pB = nextI(), nextI()
                for (p, o2) in ((pA, 1536), (pB, 1600)):
                    for c in range(16):
                        fw.op("pe", lambda e, p=p, c=c, o2=o2: e.matmul(
                            psI[p][0:64, 0:TB], win[:, c, o2:o2 + 64], xT[:, c, :], start=(c == 0), stop=(c == 15)),
                            reads=WINK + XK, writes=[("psI", p)], excl=[("PS", "I%d" % p)])
                yield
                rope_apply(pA, "I%d" % pA, pB, "I%d" % pB, krT[:, t0:t0 + TB], ("krT", tb))
                continue
            p = nextI()
            bk = "I%d" % p
            ex = [("PS", bk)]
            for c in range(16):
                fw.op("pe", lambda e, p=p, c=c, off=off, m=m: e.matmul(
                    psI[p][0:m, 0:TB], win[:, c, off:off + m], xT[:, c, :], start=(c == 0), stop=(c == 15)),
                    reads=WINK + XK, writes=[("psI", p)], excl=ex)
            yield
            src = psI[p][:, 0:TB]
            rd = [("psI", p)]
            if name == "cq":
                fw.op("act", lambda e, src=src: e.copy(out=ubq[:, 3:3 + TB], in_=src), reads=rd, writes=[("ubq",)], excl=ex)
            elif name == "ck":
                fw.op("dve", lambda e, src=src: e.tensor_copy(out=ubk[:, 3:3 + TB], in_=src), reads=rd, writes=[("ubk",)], excl=ex)
            elif name == "ci":
                fw.op("act", lambda e, src=src: e.activation(out=liB[:], in_=src, func=AF.Identity, bias=CW(10), scale=1.0),
                      reads=rd + [("cols",)], writes=[("liB", bs)], excl=ex)
            elif name == "cf":
                fw.op("act", lambda e, src=src: e.activation(out=spB[:], in_=src, func=AF.Exp, bias=CW(11), scale=-1.0),
                      reads=rd + [("cols",)], writes=[("spB", bs)], excl=ex)
                fw.op("act", lambda e: e.activation(out=spB[:], in_=spB[:], func=AF.Ln, bias=1.0, scale=1.0),
                      reads=[("spB", bs)], writes=[("spB", bs)])
            elif name.startswith("mq") or name.startswith("mkv"):
                isq = name.startswith("mq")
                ci_ = int(name[-1])
                dstT = mqT if isq else mkvT
                gcol = (16 if isq else 20) + ci_
                lat = "mqT" if isq else "mkvT"
                fw.op("act", lambda e, src=src, ci_=ci_: e.activation(out=sq[:, ci_, :], in_=src, func=AF.Square), reads=rd, writes=[("sq", ci_)], excl=ex)
                fw.op("act", lambda e, src=src, dstT=dstT, ci_=ci_, gcol=gcol: e.activation(
                    out=dstT[:, ci_, :], in_=src, func=AF.Copy, scale=CW(gcol)),
                    reads=rd + [("cols",)], writes=[(lat, ci_)], excl=ex)
                if ci_ == 3:
                    for c4 in range(4):
                        fw.op("pe", lambda e, c4=c4: e.matmul(psG[:, 0:TB], onesb[:], sq[:, c4, :], start=(c4 == 0), stop=(c4 == 3)),
                              reads=[("onesb",), ("sq", c4)], writes=[("pG",)], excl=[("PS", "G")])
                    rstd_rows(lat, rsq if isq else rskv, ("rsq",) if isq else ("rskv",))
        for tt in range(NTB):
            p = nextI()
            ex = [("PS", "I%d" % p)]
            for c in range(16):
                fw.op("pe", lambda e, p=p, c=c, tt=tt: e.matmul(
                    psI[p][:], xT[:, c, tt * 128:(tt + 1) * 128], win[:, c, MO_TM:MO_TM + 512], start=(c == 0), stop=(c == 15)),
                    reads=WINK + XK, writes=[("psI", p)], excl=ex)
            yield
            fw.op("act", lambda e, p=p, tt=tt: e.activation(out=gsig[:, tt, :], in_=psI[p][:, 256:512], func=AF.Sigmoid),
                  reads=[("psI", p)], writes=[("gsig", bs, tt)], excl=ex)
            fw.op("act", lambda e, p=p, tt=tt: e.copy(out=vtm[:, tt, 0:256], in_=psI[p][:, 0:256]),
                  reads=[("psI", p)], writes=[("vtm", bs, tt)], excl=ex)
            fw.op("dve", lambda e, tt=tt: e.tensor_tensor(out=gsig[:, tt, :], in0=gsig[:, tt, :], in1=ngb[:], op=ALU.mult),
                  reads=[("gsig", bs, tt), ("ngb",)], writes=[("gsig", bs, tt)])

    def conv_silu(ub, ukey, woff, bcol, dst, dkey, scale):
        fw.op("dve", lambda e: e.tensor_scalar(out=acc[:], in0=ub[:, 0:TB], scalar1=CW(woff), scalar2=None, op0=ALU.mult),
              reads=[(ukey,), ("cols",)], writes=[("acc",)])
        for tau in range(1, 4):
            fw.op("dve", lambda e, tau=tau: e.scalar_tensor_tensor(out=acc[:], in0=ub[:, tau:tau + TB], scalar=CW(woff + tau), in1=acc[:],
                                                                     op0=ALU.mult, op1=ALU.add),
                  reads=[(ukey,), ("acc",), ("cols",)], writes=[("acc",)])
        fw.op("dve", lambda e: e.tensor_copy(out=ub[:, 0:3], in_=ub[:, TB:TB + 3]), reads=[(ukey,)], writes=[(ukey,)])
        fw.op("act", lambda e: e.activation(out=acc[:], in_=acc[:], func=AF.Silu, bias=CW(bcol), scale=1.0),
              reads=[("acc",), ("cols",)], writes=[("acc",)])
        fw.op("dve", lambda e: e.tensor_scalar(out=dst[:], in0=acc[:], scalar1=scale, scalar2=None, op0=ALU.mult),
              reads=[("acc",)], writes=[dkey])
        yield

    def mla_proj(tb):
        t0 = tb * TB
        bs = tb % 2
        qn, qr = qn2[bs], qr2[bs]
        for h in range(2):
            p = nextI()
            ex = [("PS", "I%d" % p)]
            for c in range(4):
                fw.op("pe", lambda e, p=p, c=c, h=h: e.matmul(psI[p][:, 0:TB], wuq[:, c, h * 256:h * 256 + 128], mqT[:, c, :],
                                                              start=(c == 0), stop=(c == 3)),
                      reads=[("wuq",), ("mqT", c)], writes=[("psI", p)], excl=ex)
            yield
            fw.op("dve", lambda e, p=p, h=h: e.scalar_tensor_tensor(out=qn[h][:], in0=psI[p][:, 0:TB], scalar=QSCALE, in1=rsq[:],
                                                                     op0=ALU.mult, op1=ALU.mult),
                  reads=[("psI", p), ("rsq",)], writes=[("qn", bs, h)], excl=ex)
            pA, pB = nextI(), nextI()
            for (p, o2) in ((pA, h * 256 + 128), (pB, h * 256 + 192)):
                for c in range(4):
                    fw.op("pe", lambda e, p=p, c=c, o2=o2: e.matmul(psI[p][0:64, 0:TB], wuq[:, c, o2:o2 + 64], mqT[:, c, :],
                                                                    start=(c == 0), stop=(c == 3)),
                          reads=[("wuq",), ("mqT", c)], writes=[("psI", p)], excl=[("PS", "I%d" % p)])
            yield
            rope_apply(pA, "I%d" % pA, pB, "I%d" % pB, qr[h][:], ("qr", bs, h), extra=rsq)
            p = nextI()
            ex = [("PS", "I%d" % p)]
            for c in range(4):
                fw.op("pe", lambda e, p=p, c=c, h=h: e.matmul(psI[p][:, 0:TB], wukv[:, c, h * 128:(h + 1) * 128], mkvT[:, c, :],
                                                              start=(c == 0), stop=(c == 3)),
                      reads=[("wukv",), ("mkvT", c)], writes=[("psI", p)], excl=ex)
            yield
            fw.op("dve", lambda e, p=p, h=h: e.tensor_tensor(out=knT[h][:, t0:t0 + TB], in0=psI[p][:, 0:TB], in1=rskv[:], op=ALU.mult),
                  reads=[("psI", p), ("rskv",)], writes=[("knT", h, tb)], excl=ex)
        for tt in range(NTB):
            kt = tb * NTB + tt
            p = nextI()
            ex = [("PS", "I%d" % p)]
            for c in range(4):
                fw.op("pe", lambda e, p=p, c=c, tt=tt: e.matmul(psI[p][:, 0:256], mkvT[:, c, tt * 128:(tt + 1) * 128], wukv[:, c, 256:512],
                                                                start=(c == 0), stop=(c == 3)),
                      reads=[("wukv",), ("mkvT", c)], writes=[("psI", p)], excl=ex)
            fw.op("dve", lambda e, tt=tt: e.scalar_tensor_tensor(out=junk2[:], in0=rskv[:, tt * 128:(tt + 1) * 128], scalar=1.0, in1=identF, op0=ALU.mult, op1=ALU.mult, accum_out=am[:, 7:8]),
                  reads=[("rskv",), ("cst",)], writes=[("junk2",), ("am", 7)])
            fw.op("dve", lambda e, p=p, kt=kt: e.tensor_scalar(out=vh[:, kt, :], in0=psI[p][:, 0:256], scalar1=am[:, 7:8], scalar2=None, op0=ALU.mult),
                  reads=[("psI", p), ("am", 7)], writes=[("vh", kt)], excl=ex)
            yield

    GX = [("PS", "G")]

    def mlstm_chunk(c):
        cc = c % NTB
        bs = (c // NTB) % 2
        qTb, kTb, liB, spB, vtm, gsig = qTb2[bs], kTb2[bs], liB2[bs], spB2[bs], vtm2[bs], gsig2[bs]
        ts = slice(cc * 128, (cc + 1) * 128)
        s = sm[c % 2]
        sp_ = sm[(c + 1) % 2]
        col = lambda i: s[:, i:i + 1]
        fw.op("dve", lambda e: e.tensor_tensor_scan(out=bB[:], data0=onesF, data1=spB[:, ts], initial=0.0, op0=ALU.mult, op1=ALU.subtract),
              reads=[("spB", bs), ("cst",)], writes=[("bB",)])
        fw.op("dve", lambda e: e.tensor_tensor(out=GB[:], in0=liB[:, ts], in1=bB[:], op=ALU.subtract), reads=[("liB", bs), ("bB",)], writes=[("GB",)])
        fw.op("dve", lambda e: e.reduce_max(out=col(0), in_=liB[:, ts], axis=AX.X), reads=[("liB", bs)], writes=[("sm", c % 2, 0)])
        if c == 0:
            fw.op("dve", lambda e: e.tensor_copy(out=col(1), in_=col(0)), reads=[("sm", c % 2, 0)], writes=[("sm", c % 2, 1)])
        else:
            fw.op("dve", lambda e: e.tensor_tensor(out=col(1), in0=col(0), in1=sp_[:, 1:2], op=ALU.max),
                  reads=[("sm", c % 2, 0), ("sm", (c + 1) % 2, 1)], writes=[("sm", c % 2, 1)])
        fw.op("dve", lambda e: e.scalar_tensor_tensor(out=junk[:], in0=bB[:], scalar=1.0, in1=identF, op0=ALU.mult, op1=ALU.mult, accum_out=col(2)), reads=[("bB",), ("cst",)], writes=[("junk",), ("sm", c % 2, 2)])
        fw.op("dve", lambda e: e.scalar_tensor_tensor(out=junk[:], in0=GB[:], scalar=1.0, in1=identF, op0=ALU.mult, op1=ALU.mult, accum_out=col(3)), reads=[("GB",), ("cst",)], writes=[("junk",), ("sm", c % 2, 3)])
        fw.op("dve", lambda e: e.tensor_tensor(out=col(4), in0=col(3), in1=col(1), op=ALU.subtract),
              reads=[("sm", c % 2, 3), ("sm", c % 2, 1)], writes=[("sm", c % 2, 4)])
        if c > 0:
            fw.op("dve", lambda e: e.tensor_tensor(out=col(5), in0=sp_[:, 1:2], in1=col(1), op=ALU.subtract),
                  reads=[("sm", (c + 1) % 2, 1), ("sm", c % 2, 1)], writes=[("sm", c % 2, 5)])
        fw.op("dve", lambda e: e.tensor_tensor(out=col(6), in0=bB[:, 127:128], in1=col(1), op=ALU.subtract),
              reads=[("bB",), ("sm", c % 2, 1)], writes=[("sm", c % 2, 6)])
        yield
        fw.op("pe", lambda e: e.matmul(psG[:, 0:128], kTb[:, ts], qTb[:, ts], start=True, stop=True),
              reads=[("kTb", bs), ("qTb", bs)], writes=[("pG",)], excl=GX)
        fw.op("act", lambda e: e.activation(out=DT[:], in_=bB[:], func=AF.Exp, bias=col(4), scale=1.0),
              reads=[("bB",), ("sm", c % 2, 4)], writes=[("DT",)])
        fw.op("dve", lambda e: e.tensor_tensor(out=DT[:], in0=DT[:], in1=causT, op=ALU.mult), reads=[("DT",), ("cst",)], writes=[("DT",)])
        fw.op("dve", lambda e: e.tensor_tensor(out=qkDT[:], in0=psG[:, 0:128], in1=DT[:], op=ALU.mult),
              reads=[("pG",), ("DT",)], writes=[("qkDT",)], excl=GX)
        yield
        if c > 0:
            fw.op("act", lambda e: e.activation(out=eBt[:], in_=bB[:], func=AF.Exp, bias=col(5), scale=1.0),
                  reads=[("bB",), ("sm", c % 2, 5)], writes=[("eB",)])
            fw.op("dve", lambda e: e.tensor_tensor(out=qsT[:], in0=qTb[:, ts], in1=eBt[:], op=ALU.mult),
                  reads=[("qTb", bs), ("eB",)], writes=[("qsT",)])
        fw.op("pe", lambda e: e.matmul(psG[:, 0:257], qkDT[:], vtm[:, cc, :], start=True, stop=(c == 0)),
              reads=[("qkDT",), ("vtm", bs, cc)], writes=[("pG",)], excl=GX)
        if c > 0:
            fw.op("pe", lambda e: e.matmul(psG[:, 0:257], qsT[:], Cab[:], start=False, stop=True),
                  reads=[("qsT",), ("Cab",)], writes=[("pG",)], excl=GX)
        fw.op("act", lambda e: e.activation(out=col(7), in_=col(1), func=AF.Exp, scale=-1.0),
              reads=[("sm", c % 2, 1)], writes=[("sm", c % 2, 7)])
        fw.op("act", lambda e: e.activation(out=col(10), in_=psG[:, 256:257], func=AF.Abs),
              reads=[("pG",)], writes=[("sm", c % 2, 10)], excl=GX)
        fw.op("dve", lambda e: e.tensor_tensor(out=col(10), in0=col(10), in1=col(7), op=ALU.max),
              reads=[("sm", c % 2, 10), ("sm", c % 2, 7)], writes=[("sm", c % 2, 10)])
        fw.op("dve", lambda e: e.reciprocal(out=col(11), in_=col(10)), reads=[("sm", c % 2, 10)], writes=[("sm", c % 2, 11)])
        fw.op("dve", lambda e: e.tensor_scalar(out=hh[:], in0=psG[:, 0:256], scalar1=col(11), scalar2=None, op0=ALU.mult),
              reads=[("pG",), ("sm", c % 2, 11)], writes=[("hh",)], excl=GX)
        yield
        fw.op("act", lambda e: e.activation(out=col(8), in_=col(3), func=AF.Exp, bias=col(6), scale=1.0),
              reads=[("sm", c % 2, 3), ("sm", c % 2, 6)], writes=[("sm", c % 2, 8)])
        h = c % 2
        tx = [("PS", "T%d" % h)]
        fw.op("pe", lambda e: e.transpose(out=psT[h][:, 0:128], in_=kTb[:, ts], identity=idb[:]),
              reads=[("kTb", bs), ("idb",)], writes=[("psT", h)], excl=tx)
        fw.op("dve", lambda e: e.tensor_scalar(out=kw[:], in0=psT[h][:, 0:128], scalar1=col(8), scalar2=None, op0=ALU.mult),
              reads=[("psT", h), ("sm", c % 2, 8)], writes=[("kw",)], excl=tx)
        fw.op("pe", lambda e: e.matmul(psG[:, 0:257], kw[:], vtm[:, cc, :], start=True, stop=True),
              reads=[("kw",), ("vtm", bs, cc)], writes=[("pG",)], excl=GX)
        if c == 0:
            fw.op("dve", lambda e: e.tensor_copy(out=Ca[:], in_=psG[:, 0:257]), reads=[("pG",)], writes=[("Ca",)], excl=GX)
        else:
            fw.op("act", lambda e: e.activation(out=col(9), in_=bB[:, 127:128], func=AF.Exp, bias=col(5), scale=1.0),
                  reads=[("bB",), ("sm", c % 2, 5)], writes=[("sm", c % 2, 9)])
            fw.op("dve", lambda e: e.scalar_tensor_tensor(out=Ca[:], in0=Ca[:], scalar=col(9), in1=psG[:, 0:257], op0=ALU.mult, op1=ALU.add),
                  reads=[("Ca",), ("sm", c % 2, 9), ("pG",)], writes=[("Ca",)], excl=GX)
        fw.op("act", lambda e: e.copy(out=Cab[:], in_=Ca[:]), reads=[("Ca",)], writes=[("Cab",)])
        yield
        fw.op("dve", lambda e: e.bn_stats(out=st6[:], in_=hh[:]), reads=[("hh",)], writes=[("st6",)])
        fw.op("dve", lambda e: e.bn_aggr(out=s[:, 12:14], in_=st6[:]), reads=[("st6",)], writes=[("sm", c % 2, 12)])
        fw.op("act", lambda e: e.activation(out=col(14), in_=col(13), func=AF.Sqrt, bias=1e-5, scale=1.0),
              reads=[("sm", c % 2, 12)], writes=[("sm", c % 2, 14)])
        fw.op("dve", lambda e: e.reciprocal(out=col(14), in_=col(14)), reads=[("sm", c % 2, 14)], writes=[("sm", c % 2, 14)])
        fw.op("dve", lambda e: e.scalar_tensor_tensor(out=col(15), in0=col(12), scalar=-1.0, in1=col(14), op0=ALU.mult, op1=ALU.mult),
              reads=[("sm", c % 2, 12), ("sm", c % 2, 14)], writes=[("sm", c % 2, 15)])
        fw.op("act", lambda e: e.activation(out=hh[:], in_=hh[:], func=AF.Identity, bias=col(15), scale=col(14)),
              reads=[("hh",), ("sm", c % 2, 14), ("sm", c % 2, 15)], writes=[("hh",)])
        fw.op("dve", lambda e: e.tensor_tensor(out=hh[:], in0=hh[:], in1=gsig[:, cc, :], op=ALU.mult),
              reads=[("hh",), ("gsig", bs, cc)], writes=[("hh",)])
        for e2 in range(2):
            fw.op("pe", lambda e, e2=e2: e.transpose(out=psG[:, e2 * 128:(e2 + 1) * 128], in_=hh[:, e2 * 128:(e2 + 1) * 128], identity=identF),
                  reads=[("hh",), ("cst",)], writes=[("pG",)], excl=GX)
        fw.op("act", lambda e: e.copy(out=oTa[:, 0:2, ts], in_=psG[:, 0:256].rearrange("p (a n) -> p a n", a=2)),
              reads=[("pG",)], writes=[("oTa", 0, cc)], excl=GX)
        yield

    sci = [0]
    pti = [0]
    VX = [("PS", "V")]

    def mla_tile(qt, h):
        qc = qt % NTB
        bs = (qt // NTB) % 2
        qn, qr = qn2[bs], qr2[bs]
        qs = slice(qc * 128, (qc + 1) * 128)
        n = (qt + 1) * 128
        done = 0
        while done < n:
            m = min(512, n - done)
            p = sci[0] % 2
            sci[0] += 1
            ex = [("PS", "S%d" % p)]
            blks = sorted(set(range(done // TB, (done + m - 1) // TB + 1)))
            fw.op("pe", lambda e, p=p, m=m, done=done: e.matmul(psS[p][:, 0:m], qn[h][:, qs], knT[h][:, done:done + m], start=True, stop=False),
                  reads=[("qn", bs, h)] + [("knT", h, b) for b in blks], writes=[("psS", p)], excl=ex)
            fw.op("pe", lambda e, p=p, m=m, done=done: e.matmul(psS[p][:, 0:m], qr[h][:, qs], krT[:, done:done + m], start=False, stop=True),
                  reads=[("qr", bs, h)] + [("krT", b) for b in blks], writes=[("psS", p)], excl=ex)
            last = (done + m == n)
            mm = m - 128 if last else m
            if mm > 0:
                fw.op("act", lambda e, p=p, mm=mm, done=done: e.copy(out=sc[:, done:done + mm], in_=psS[p][:, 0:mm]),
                      reads=[("psS", p)], writes=[("sc", done // 512)], excl=ex)
            if last:
                fw.op("dve", lambda e, p=p, mm=mm, done=done: e.tensor_tensor(out=sc[:, done + mm:done + mm + 128], in0=psS[p][:, mm:mm + 128],
                                                                               in1=causAdd, op=ALU.add),
                      reads=[("psS", p), ("cst",)], writes=[("sc", "d")], excl=ex)
            done += m
            yield
        SCK = [("sc", k) for k in range((n + 511) // 512)] + [("sc", "d")]
        fw.op("dve", lambda e: e.reduce_max(out=am[:, 0:1], in_=sc[:, 0:n], axis=AX.X), reads=SCK, writes=[("am", 0)])
        fw.op("dve", lambda e: e.tensor_scalar(out=am[:, 1:2], in0=am[:, 0:1], scalar1=-1.0, scalar2=None, op0=ALU.mult),
              reads=[("am", 0)], writes=[("am", 1)])
        fw.op("act", lambda e: e.activation(out=P[:, 0:n], in_=sc[:, 0:n], func=AF.Exp, bias=am[:, 1:2], scale=1.0, accum_out=am[:, 2:3]),
              reads=SCK + [("am", 1)], writes=[("P",), ("am", 2)])
        fw.op("dve", lambda e: e.reciprocal(out=am[:, 3:4], in_=am[:, 2:3]), reads=[("am", 2)], writes=[("am", 3)])
        yield
        ntile = qt + 1
        for g0 in range(0, ntile, 4):
            grp = list(range(g0, min(g0 + 4, ntile)))
            hb = pti[0] % 2
            pti[0] += 1
            tx = [("PS", "T%d" % hb)]
            for k, kt in enumerate(grp):
                fw.op("pe", lambda e, k=k, kt=kt, hb=hb: e.transpose(out=psT[hb][:, k * 128:(k + 1) * 128], in_=P[:, kt * 128:(kt + 1) * 128],
                                                                   identity=idb[:]),
                      reads=[("P",), ("idb",)], writes=[("psT", hb)], excl=tx)
            nn = len(grp) * 128
            if hb == 0:
                fw.op("act", lambda e, hb=hb, nn=nn: e.copy(out=PT[hb][:, 0:nn], in_=psT[hb][:, 0:nn]), reads=[("psT", hb)], writes=[("PT", hb)], excl=tx)
            else:
                fw.op("dve", lambda e, hb=hb, nn=nn: e.tensor_copy(out=PT[hb][:, 0:nn], in_=psT[hb][:, 0:nn]), reads=[("psT", hb)], writes=[("PT", hb)], excl=tx)
            for k, kt in enumerate(grp):
                fw.op("pe", lambda e, k=k, kt=kt, hb=hb: e.matmul(psV[:, 0:128], PT[hb][:, k * 128:(k + 1) * 128], vh[:, kt, h * 128:(h + 1) * 128],
                                                                start=(kt == 0), stop=(kt == ntile - 1)),
                      reads=[("PT", hb), ("vh", kt)], writes=[("pV", 0)], excl=VX)
            yield
        fw.op("dve", lambda e: e.tensor_scalar(out=od[:], in0=psV[:, 0:128], scalar1=am[:, 3:4], scalar2=None, op0=ALU.mult),
              reads=[("pV", 0), ("am", 3)], writes=[("od",)], excl=VX)
        fw.op("pe", lambda e: e.transpose(out=psV[:, 128:256], in_=od[:], identity=identF), reads=[("od",), ("cst",)], writes=[("pV", 1)], excl=VX)
        fw.op("act", lambda e: e.copy(out=oTa[:, 2 + h, qs], in_=psV[:, 128:256]), reads=[("pV", 1)], writes=[("oTa", 1 + h, qc)], excl=VX)

    def pre(tb):
        bs = tb % 2
        rope_tables(tb)
        yield
        yield from inproj(tb)
        yield from conv_silu(ubq, "ubq", 0, 8, qTb2[bs], ("qTb", bs), 1.0)
        yield from conv_silu(ubk, "ubk", 4, 9, kTb2[bs], ("kTb", bs), 128.0 ** -0.5)
        yield from mla_proj(tb)

    def mixers(tb):
        for cc in range(NTB):
            c = tb * NTB + cc
            yield from mlstm_chunk(c)
            for h in range(2):
                yield from mla_tile(c, h)

    for _ in pre(0):
        pass
    load_x(1)
    load_pos(1)
    for tb in range(NBO):
        bg = pre(tb + 1) if tb + 1 < NBO else iter(())
        bg_live = [tb + 1 < NBO]

        def bg_step(tb=tb):
            if not bg_live[0]:
                return
            try:
                next(bg)
            except StopIteration:
                bg_live[0] = False
                if tb + 2 < NBO:
                    load_x(tb + 2)
                    load_pos(tb + 2)
                elif tail_prefetch is not None:
                    tail_prefetch(WINK + [("wuq",), ("wukv",)])

        n = 0
        nfg = sum(5 + 2 * (2 * ((tb * NTB + cc) // 4 + 1) + 1) for cc in range(NTB))
        step = max(1, int(0.5 * nfg / 26))
        for _ in mixers(tb):
            n += 1
            if n % step == 0:
                bg_step()
        while bg_live[0]:
            bg_step()
        fw.dma("sp", o_dst(tb * TB, TB), oTa[:],
               reads=[("oTa", a, q) for a in range(3) for q in range(NTB)], writes=[("odram", tb * TB)], is_output=True)
        if post is not None:
            post(tb * TB, TB)


def mo_consts():
    import ml_dtypes
    j = np.arange(128)[:, None]
    i = np.arange(128)[None, :]
    cst = np.zeros((128, 4, 128), np.float32)
    cst[:, 0, :] = np.eye(128)
    cst[:, 1, :] = np.where(j <= i, 1.0, 0.0)
    cst[:, 2, :] = np.where(i <= j, 0.0, NEG)
    cst[:, 3, :] = 1.0
    return cst, np.eye(128, dtype=np.float32).astype(ml_dtypes.bfloat16)


def mo_cols(conv_w, conv_b, bi, bf, gq, gkv, h):
    cols = np.zeros((128, 32), np.float32)
    cols[:, 0:4] = conv_w[:, h * 128:(h + 1) * 128].T
    cols[:, 4:8] = conv_w[:, 512 + h * 128:512 + (h + 1) * 128].T
    cols[:, 8] = conv_b[h * 128:(h + 1) * 128]
    cols[:, 9] = conv_b[512 + h * 128:512 + (h + 1) * 128]
    cols[:, 10] = bi[h]
    cols[:, 11] = -bf[h]
    invf = (10000.0 ** (-np.arange(0, 64, 2, dtype=np.float32) / 64)).astype(np.float32)
    cols[0:64, 12] = np.concatenate([invf, invf])
    cols[0:64, 13] = math.pi / 2
    cols[0:32, 14] = math.pi
    cols[32:64, 14] = 0.0
    cols[:, 16:20] = gq.reshape(4, 128).T
    cols[:, 20:24] = gkv.reshape(4, 128).T
    return cols


def mo_win_layout(w_in, h):
    def cols(a, n):
        return w_in[:, a:a + n]
    cq = cols(h * 128, 128)
    ck = cols(512 + h * 128, 128)
    cv = cols(1024 + h * 256, 256)
    ci = np.repeat(cols(2048 + h, 1), 128, axis=1)
    cf = np.repeat(cols(2052 + h, 1), 128, axis=1)
    co = cols(2056 + h * 256, 256)
    mq = cols(3080, 512)
    mkv = cols(3592, 512)
    kr = cols(4104, 64)
    krB = np.concatenate([kr[:, 32:], kr[:, :32]], axis=1)
    w = np.concatenate([cq, ck, ci, cf, mq, mkv, kr, krB, cv, co], axis=1)
    assert w.shape[1] == MO_NC
    return np.ascontiguousarray(w.reshape(16, 128, MO_NC).transpose(1, 0, 2).reshape(128, 16 * MO_NC))


def mo_wu_layout(wuq, wukv, h):
    qs = []
    for hm in (2 * h, 2 * h + 1):
        blk = wuq[:, hm * 192:(hm + 1) * 192]
        nope, rope = blk[:, :128], blk[:, 128:]
        qs += [nope, rope, np.concatenate([rope[:, 32:], rope[:, :32]], axis=1)]
    q = np.concatenate(qs, axis=1)
    kn = [wukv[:, hm * 256:hm * 256 + 128] for hm in (2 * h, 2 * h + 1)]
    vv = [wukv[:, hm * 256 + 128:hm * 256 + 256] for hm in (2 * h, 2 * h + 1)]
    kv = np.concatenate(kn + vv, axis=1)
    lay = lambda w: np.ascontiguousarray(w.reshape(4, 128, 512).transpose(1, 0, 2).reshape(128, 4 * 512))
    return lay(q), lay(kv)


def build_MO(x_dt=F32):
    nc = bass.Bass("TRN2", target_bir_lowering=False)
    xT_d = nc.dram_tensor("xT", [D, S], x_dt, kind="ExternalInput").ap()
    win_d = nc.dram_tensor("win", [128, 16 * MO_NC], F32, kind="ExternalInput").ap()
    wuq_d = nc.dram_tensor("wuq", [128, 4 * 512], F32, kind="ExternalInput").ap()
    wukv_d = nc.dram_tensor("wukv", [128, 4 * 512], F32, kind="ExternalInput").ap()
    cols_d = nc.dram_tensor("cols", [128, 32], F32, kind="ExternalInput").ap()
    ng_d = nc.dram_tensor("ng", [256], F32, kind="ExternalInput").ap()
    pos_d = nc.dram_tensor("pos", [S], I32, kind="ExternalInput").ap()
    cst_d = nc.dram_tensor("cst", [128, 4, 128], F32, kind="ExternalInput").ap()
    idb_d = nc.dram_tensor("idb", [128, 128], BF16, kind="ExternalInput").ap()
    oT_d = nc.dram_tensor("oT", [512, S], BF16, kind="ExternalOutput").ap()
    fw = FW(nc)
    emit_MO(fw, lambda t0, n: [(0, 16, tl, tl + 256, xT_d[:, t0 + tl:t0 + tl + 256].rearrange("(c p) t -> p c t", p=128)) for tl in range(0, n, 256)], win_d, wuq_d, wukv_d, cols_d, ng_d, pos_d, cst_d, idb_d,
            lambda t0, n: oT_d[:, t0:t0 + n].rearrange("(a p) t -> p a t", p=128))
    fw.emit()
    fw.close()
    return nc


GROUPS = [[0, 1, 2, 3], [4, 5, 6, 7]]


def build_fused(nl=4, stop_after_mixer=False):
    nc = bass.Bass("TRN2", target_bir_lowering=False)

    def EI(name, shape, dt):
        return nc.dram_tensor(name, list(shape), dt, kind="ExternalInput").ap()

    x0T_d = EI("x0T", [D, S], F32)
    xres0_d = EI("xres0", [TOK, D], F32)
    pos_d = EI("pos", [S], I32)
    sel_d = EI("sel", [128, 4], F32)
    cstE_d = EI("cstE", [128, 4, 128], F32)
    maskE_d = EI("maskE", [128, DIL_TOT * 128], BF16)
    idb_d = EI("idb", [128, 128], BF16)
    cstO_d = EI("cstO", [128, 4, 128], F32)
    ident_d = EI("ident", [128, 128], F32)
    W = {}
    for j in range((nl + 1) // 2):
        W["winE", j] = EI("winE%d" % j, [128, 16 * ME_NC], F32)
        W["wg2E", j] = EI("wg2E%d" % j, [17, 128], F32)
        W["ngE", j] = EI("ngE%d" % j, [256], F32)
    for j in range(nl // 2):
        W["winO", j] = EI("winO%d" % j, [128, 16 * MO_NC], F32)
        W["wuqO", j] = EI("wuqO%d" % j, [128, 4 * 512], F32)
        W["wukvO", j] = EI("wukvO%d" % j, [128, 4 * 512], F32)
        W["colsO", j] = EI("colsO%d" % j, [128, 32], F32)
        W["ngO", j] = EI("ngO%d" % j, [256], F32)
    for l in range(nl):
        F = 1536 if l % 2 == 0 else 2048
        W["wo", l] = EI("wo%d" % l, [F, D], F32)
        W["wgu", l] = EI("wgu%d" % l, [NJ, 128, 16 * 256], F32)
        W["wd", l] = EI("wd%d" % l, [HID, D], F32)
        W["ln", l] = EI("ln%d" % l, [4, D], F32)
    xo_d = nc.dram_tensor("xo", [TOK, D], F32, kind="ExternalOutput").ap()
    FH = {0: 384, 1: 512}
    oTx = {o: [nc.dram_tensor("oTx%d_%d" % (o, c), [FH[o], TOK], BF16) for c in range(4)] for o in (0, 1)}
    oTg = {o: [nc.dram_tensor("oTg%d_%d" % (o, c), [4 * FH[o], TOK], BF16) for c in range(4)] for o in (0, 1)}
    xoT = [nc.dram_tensor("xoT_%d" % c, [128, 16 * 256], BF16) for c in range(4)]
    xTg = [[nc.dram_tensor("xTg%d_%d" % (i, c), [4 * 128, 16 * 256], BF16) for c in range(4)] for i in range(2)]
    xbuf = nc.dram_tensor("xbuf", [TOK, D], F32)

    fw = FW(nc)
    WB = fw.sb("WB", [128, 16 * MO_NC], BF16)
    for l in range(nl):
        j = l // 2
        odd = l % 2
        F = 2048 if odd else 1536
        fw.begin_phase()
        if l == 0:
            x_src = lambda t0, n: [(0, 16, tl, tl + 256, x0T_d[:, t0 + tl:t0 + tl + 256].rearrange("(c p) t -> p c t", p=128))
                                   for tl in range(0, n, 256)]
        else:
            gbs = xTg[(l - 1) % 2]
            x_src = lambda t0, n, gbs=gbs, l=l: [
                (0, 16, tl, tl + 256, gbs[((t0 + tl) % TOK) // 256].ap()[((t0 + tl) // TOK) * 128:((t0 + tl) // TOK + 1) * 128, :].rearrange(
                    "p (c t) -> p c t", c=16), ("xTg", (l - 1) % 2, ((t0 + tl) % TOK) // 256)) for tl in range(0, n, 256)]
        def post_mix(t0, n, odd=odd):
            if (t0 + n) % TOK == 0:
                c = t0 // TOK
                fw.collective("AllGather", GROUPS, oTx[odd][c], oTg[odd][c], reads=[("odram", tt) for tt in range(c * TOK, (c + 1) * TOK, n)],
                              writes=[("oTg", odd, c)])

        o_dst = lambda t0, n, odd=odd: oTx[odd][t0 // TOK].ap()[:, (t0 % TOK):(t0 % TOK) + n].rearrange("(a p) t -> p a t", p=128)
        wo_l = W["wo", l]
        KCl = F // 128

        def tail_prefetch(keys, wo_l=wo_l, KCl=KCl):
            for cb in range(2):
                dst = WB[:, 16384 + cb * 8192:16384 + (cb + 1) * 8192].rearrange("p (c n) -> p c n", c=16)
                fw.dma("pool", dst[:, 0:KCl, :], wo_l[:, cb * 512:(cb + 1) * 512].rearrange("(c p) n -> p c n", p=128), writes=list(keys))

        NCl = MO_NC if odd else ME_NC
        win_view = WB[:, 0:16 * NCl].rearrange("p (c n) -> p c n", c=16)
        pre_l = (l > 0)
        if not odd:
            emit_ME(fw, x_src, W["winE", j], W["wg2E", j], W["ngE", j], cstE_d, maskE_d, idb_d, o_dst, pfx="E%d" % l, post=post_mix,
                    win_ext=win_view, win_preloaded=pre_l, tail_prefetch=tail_prefetch)
        else:
            emit_MO(fw, x_src, W["winO", j], W["wuqO", j], W["wukvO", j], W["colsO", j], W["ngO", j], pos_d, cstO_d, idb_d,
                    o_dst, pfx="O%d" % l, post=post_mix, win_ext=win_view, win_preloaded=pre_l, tail_prefetch=tail_prefetch)
        fw.end_phase(wait_cc=False)
        fw.begin_phase()
        ogs = [t.ap() for t in oTg[odd]]

        def loader(fw_, oT, KC, tmp, ogs=ogs, l=l):
            selt = fw_.sb("selt%d" % l, [128, 4], F32)
            fw_.dma("sp", selt[:], sel_d, writes=[("selt",)])
            i = 0
            for r in range(4):
                for k in range(KC):
                    tb = tmp[i % len(tmp)]
                    key = ("tmp", i % len(tmp))
                    i += 1
                    fw_.dma("sp", tb, ogs[r][k * 128:(k + 1) * 128, :], reads=[("oTg", l % 2, r)], writes=[key])
                    if r == 0:
                        fw_.op("dve", lambda e, tb=tb, k=k: e.tensor_scalar(out=oT[:, k, :], in0=tb, scalar1=selt[:, 0:1], scalar2=None,
                                                                            op0=ALU.mult),
                               reads=[key, ("selt",)], writes=[("oT", k)])
                    else:
                        fw_.op("dve", lambda e, tb=tb, k=k, r=r: e.scalar_tensor_tensor(out=oT[:, k, :], in0=tb, scalar=selt[:, r:r + 1],
                                                                                       in1=oT[:, k, :], op0=ALU.mult, op1=ALU.add),
                               reads=[key, ("selt",), ("oT", k)], writes=[("oT", k)])

        last = (l == nl - 1)

        def prefetch_next(keys, l=l):
            nodd = (l + 1) % 2
            nj = (l + 1) // 2
            NCn = MO_NC if nodd else ME_NC
            wsrc = W["winO", nj] if nodd else W["winE", nj]
            wv = WB[:, 0:16 * NCn].rearrange("p (c n) -> p c n", c=16)
            for c4 in range(4):
                fw.dma("pool", wv[:, c4 * 4:(c4 + 1) * 4, :].rearrange("p c n -> p (c n)"),
                       wsrc[:, c4 * 4 * NCn:(c4 + 1) * 4 * NCn], writes=list(keys))

        def post_T(c, l=l):
            fw.collective("AllGather", GROUPS, xoT[c], xTg[l % 2][c], reads=[("xodram", c)], writes=[("xTg", l % 2, c)])
        emit_T(fw, F, loader, xres0_d if l == 0 else xbuf.ap(), W["wo", l], W["wgu", l], W["wd", l], W["ln", l], ident_d,
               xo_d if last else xbuf.ap(),
               None if last else (lambda c: xoT[c].ap()),
               BF16, pfx="T%d" % l, post=None if last else post_T, U_ext=WB, wo_preloaded=True,
               prefetch=None if last else prefetch_next)
        fw.end_phase(wait_cc=last)
    if nl % 2 == 1 and nl < 4:
        pass
    fw.close()
    return nc


def lay_wgu(wgu):
    g = wgu[:, :HID].reshape(16, 128, NJ, 128)
    u = wgu[:, HID:].reshape(16, 128, NJ, 128)
    gu = np.concatenate([g, u], axis=-1)
    return np.ascontiguousarray(gu.transpose(2, 1, 0, 3).reshape(NJ, 128, 16 * 256))


def wo_gathered(w_o, odd):
    rows = []
    for q in range(4):
        rows.append(w_o[q * 256:(q + 1) * 256])
        if odd:
            rows.append(w_o[1024 + q * 256:1024 + (q + 1) * 256])
        else:
            rows.append(w_o[1024 + q * 128:1024 + (q + 1) * 128])
    return np.ascontiguousarray(np.concatenate(rows, 0)).astype(np.float32)


_NC = {}
NL = 4


def kernel(**inputs):
    inp = {k: np.asarray(v) for k, v in inputs.items()}
    x = np.ascontiguousarray(inp["x"]).astype(np.float32)
    pos = inp["positions"].astype(np.int32)
    cstE, maskE, idb = me_consts()
    cstO, _ = mo_consts()
    shared = dict(cstE=cstE, maskE=maskE, idb=idb, cstO=cstO, ident=np.eye(128, dtype=np.float32))
    for l in range(NL):
        j = l // 2
        odd = l % 2
        shared["wo%d" % l] = wo_gathered(inp["odd_w_o"][j] if odd else inp["even_w_o"][j], odd)
        shared["wgu%d" % l] = lay_wgu(inp["ffn_wgu"][l])
        shared["wd%d" % l] = np.ascontiguousarray(inp["ffn_wd"][l]).astype(np.float32)
        shared["ln%d" % l] = np.stack([inp["ln1_g"][l], inp["ln1_b"][l], inp["ln2_g"][l], inp["ln2_b"][l]]).astype(np.float32)
    xT = [np.ascontiguousarray(x[b].T) for b in range(2)]
    in_maps = []
    for c in range(8):
        b, h = c // 4, c % 4
        m = dict(shared)
        m["x0T"] = xT[b]
        m["xres0"] = np.ascontiguousarray(x[b, h * TOK:(h + 1) * TOK])
        m["pos"] = np.ascontiguousarray(pos[b])
        sel = np.zeros((128, 4), np.float32)
        sel[:, h] = 1.0
        m["sel"] = sel
        for j in range((NL + 1) // 2):
            wg2 = np.concatenate([inp["even_gla_wg2"][j][:, h * 128:(h + 1) * 128],
                                  inp["even_gla_bg"][j][None, h * 128:(h + 1) * 128]], 0).astype(np.float32)
            m["winE%d" % j] = me_win_layout(inp["even_w_in"][j], h)
            m["wg2E%d" % j] = np.ascontiguousarray(wg2)
            m["ngE%d" % j] = np.ascontiguousarray(inp["even_gla_norm_g"][j]).astype(np.float32)
            if j >= NL // 2:
                continue
            wq, wkv = mo_wu_layout(inp["odd_mla_wuq"][j], inp["odd_mla_wukv"][j], h)
            m["winO%d" % j] = mo_win_layout(inp["odd_w_in"][j], h)
            m["wuqO%d" % j] = wq
            m["wukvO%d" % j] = wkv
            m["colsO%d" % j] = mo_cols(inp["odd_conv_w"][j], inp["odd_conv_b"][j], inp["odd_mlstm_bi"][j], inp["odd_mlstm_bf"][j],
                                       inp["odd_mla_qnorm_g"][j], inp["odd_mla_kvnorm_g"][j], h)
            m["ngO%d" % j] = np.ascontiguousarray(inp["odd_mlstm_norm_g"][j]).astype(np.float32)
        in_maps.append(m)
    if "nc" not in _NC:
        _NC["nc"] = build_fused(NL)
    res = run_bass_kernel_spmd(_NC["nc"], in_maps, core_ids=list(range(8)))
    out = np.empty_like(x)
    for c in range(8):
        b, h = c // 4, c % 4
        out[b, h * TOK:(h + 1) * TOK] = res.results[c]["xo"]
    return out
```
